# Optimizing a Trainium2 kernel written in Bass

```python
import jax, jax.numpy as jnp
from jax import lax
import numpy as np

D_MODEL = 1024
BATCH = 4
SEQ = 4096
DEPTH = 4
DEC_BATCH = 8
DEC_SEQ = 8192
PAST_LEN = 128

N_EVEN = (DEPTH + 1) // 2
N_ODD = DEPTH // 2
NORM_EPS = 1e-6
CHUNK = 64
Q_BLOCK = 128

GLA_HEADS = 4
GLA_DK = D_MODEL // 8
GLA_DV = D_MODEL // 4
GLA_LR = 16
GLA_GATE_NORM = 16.0
GLA_K_TOT = GLA_HEADS * GLA_DK
GLA_V_TOT = GLA_HEADS * GLA_DV

POOL_GROUPS = 4
POOL_WINDOWS = (2, 4, 8, 16)
POOL_DG = D_MODEL // 8
POOL_W = POOL_GROUPS * POOL_DG

MLA_HEADS = 8
MLA_NOPE = D_MODEL // 16
MLA_ROPE = D_MODEL // 32
MLA_DV = D_MODEL // 16
MLA_Q_LORA = 3 * D_MODEL // 8
MLA_KV_LORA = D_MODEL // 4
MLA_W = MLA_HEADS * MLA_DV
ROPE_THETA = 10000.0

ML_HEADS = 4
ML_DH = D_MODEL // 8
ML_W = ML_HEADS * ML_DH

E_SPLITS = (GLA_K_TOT, GLA_K_TOT, GLA_V_TOT, GLA_V_TOT, 2 * GLA_LR, POOL_W, POOL_W)
E_COLS = sum(E_SPLITS)
E_OUT = GLA_V_TOT + POOL_W
O_SPLITS = (MLA_Q_LORA, MLA_KV_LORA, MLA_ROPE, MLA_W, ML_W, ML_W, ML_W, ML_W, 4 * ML_HEADS, ML_W)
O_COLS = sum(O_SPLITS)
O_OUT = MLA_W + ML_W

kernel_name = 'hybrid_bidir_gla_pool_mla_mlstm'

F32 = jnp.float32


def rmsnorm(x, g):
    xf = x.astype(F32)
    y = xf * lax.rsqrt(jnp.mean(xf * xf, axis=-1, keepdims=True) + NORM_EPS) * g.astype(F32)
    return y.astype(x.dtype)


def head_rmsnorm(o, g):
    y = o * lax.rsqrt(jnp.mean(o * o, axis=-1, keepdims=True) + NORM_EPS) * g.astype(F32)
    B, H, S, d = y.shape
    return y.transpose(0, 2, 1, 3).reshape(B, S, H * d)


def split_cols(z, sizes):
    idx = [int(i) for i in np.cumsum(sizes)[:-1]]
    return jnp.split(z, idx, axis=-1)


def to_heads(t, n_heads):
    B, S, _ = t.shape
    return t.reshape(B, S, n_heads, -1).transpose(0, 2, 1, 3)


def flip_seq(t):
    return jnp.flip(t, axis=2)


def chunk_first(t):
    B, H, S = t.shape[:3]
    t = t.reshape(B, H, S // CHUNK, CHUNK, *t.shape[3:])
    return jnp.moveaxis(t, 2, 0)


def chunk_last(t):
    t = jnp.moveaxis(t, 0, 2)
    B, H, N, L = t.shape[:4]
    return t.reshape(B, H, N * L, *t.shape[4:])


def gla_scan(q, k, v, log_a, inclusive):
    B, H, S, dk = q.shape
    dv = v.shape[-1]
    mask = jnp.tril(jnp.ones((CHUNK, CHUNK), bool), 0 if inclusive else -1)

    def step(state, inp):
        qc, kc, vc, ac = inp
        b = jnp.cumsum(ac, axis=-2)
        b_last = b[..., -1:, :]
        qe = qc * jnp.exp(b)
        a = jnp.einsum('bhid,bhjd->bhij', qe, kc * jnp.exp(-b))
        a = jnp.where(mask, a, 0.0)
        o = jnp.einsum('bhij,bhjv->bhiv', a, vc) + jnp.einsum('bhid,bhdv->bhiv', qe, state)
        state = (jnp.exp(b_last)[..., 0, :, None] * state
                 + jnp.einsum('bhjd,bhjv->bhdv', kc * jnp.exp(b_last - b), vc))
        return state, o

    init = jnp.zeros((B, H, dk, dv), F32)
    _, o = lax.scan(step, init, (chunk_first(q), chunk_first(k), chunk_first(v), chunk_first(log_a)))
    return chunk_last(o)


def mlstm_scan(q, k, v, log_i, log_f, inclusive):
    B, H, S, dk = q.shape
    dv = v.shape[-1]
    mask = jnp.tril(jnp.ones((CHUNK, CHUNK), bool), 0 if inclusive else -1)

    def step(carry, inp):
        C, n, m = carry
        qc, kc, vc, li, lf = inp
        b = jnp.cumsum(lf, axis=-1)
        g = b[..., -1]
        d = jnp.where(mask, b[..., :, None] - b[..., None, :] + li[..., None, :], -jnp.inf)
        inter = b + m[..., None]
        m_t = jnp.maximum(inter, jnp.max(d, axis=-1))
        s = jnp.einsum('bhid,bhjd->bhij', qc, kc) * jnp.exp(d - m_t[..., None])
        e = jnp.exp(inter - m_t)
        num = jnp.einsum('bhij,bhjv->bhiv', s, vc) + e[..., None] * jnp.einsum('bhid,bhdv->bhiv', qc, C)
        den = jnp.sum(s, axis=-1) + e * jnp.einsum('bhid,bhd->bhi', qc, n)
        h = num / jnp.maximum(jnp.abs(den), jnp.exp(-m_t))[..., None]
        lw = g[..., None] - b + li
        m_new = jnp.maximum(g + m, jnp.max(lw, axis=-1))
        wk = jnp.exp(lw - m_new[..., None])
        decay = jnp.exp(g + m - m_new)
        C = decay[..., None, None] * C + jnp.einsum('bhj,bhjd,bhjv->bhdv', wk, kc, vc)
        n = decay[..., None] * n + jnp.einsum('bhj,bhjd->bhd', wk, kc)
        return (C, n, m_new), h

    init = (jnp.zeros((B, H, dk, dv), F32), jnp.zeros((B, H, dk), F32), jnp.zeros((B, H), F32))
    _, h = lax.scan(step, init, (chunk_first(q), chunk_first(k), chunk_first(v),
                                 chunk_first(log_i), chunk_first(log_f)))
    return chunk_last(h)


def multiscale_pool(u, pool_w, pool_scale):
    B, S, _ = u.shape
    ug = u.astype(F32).reshape(B, S, POOL_GROUPS, POOL_DG)
    cs = jnp.concatenate([jnp.zeros_like(ug[:, :1]), jnp.cumsum(ug, axis=1)], axis=1)
    pos = jnp.arange(S)
    outs = []
    for gi, w in enumerate(POOL_WINDOWS):
        lo = jnp.clip(pos - w // 2, 0, S)
        hi = jnp.clip(pos + w // 2, 0, S)
        csg = cs[:, :, gi]
        win_sum = jnp.take(csg, hi, axis=1) - jnp.take(csg, lo, axis=1)
        cnt = (hi - lo).astype(F32)[None, :, None]
        outs.append(win_sum / cnt - ug[:, :, gi])
    pooled = jnp.stack(outs, axis=2)
    mixed = jnp.einsum('bsgc,gcd->bsgd', pooled, pool_w.astype(F32)).reshape(B, S, POOL_W)
    return mixed * pool_scale.astype(F32)


def rope_tables(S):
    inv = ROPE_THETA ** (-jnp.arange(0, MLA_ROPE, 2, dtype=F32) / MLA_ROPE)
    ang = jnp.arange(S, dtype=F32)[:, None] * inv[None, :]
    return jnp.cos(ang), jnp.sin(ang)


def apply_rope(x, cos, sin):
    xf = x.astype(F32)
    x1, x2 = jnp.split(xf, 2, axis=-1)
    return jnp.concatenate([x1 * cos - x2 * sin, x2 * cos + x1 * sin], axis=-1).astype(x.dtype)


def mla_attention(q_nope, q_rope, k_nope, k_rope, v):
    B, S, H, _ = q_nope.shape
    scale = (MLA_NOPE + MLA_ROPE) ** -0.5
    nb = S // Q_BLOCK

    def blocks(t):
        return jnp.moveaxis(t.reshape(B, nb, Q_BLOCK, *t.shape[2:]), 1, 0)

    def attend(qs):
        qn, qr = qs
        s = (jnp.einsum('bqhd,bkhd->bhqk', qn, k_nope, preferred_element_type=F32)
             + jnp.einsum('bqhr,bkr->bhqk', qr, k_rope, preferred_element_type=F32)) * scale
        p = jax.nn.softmax(s, axis=-1)
        return jnp.einsum('bhqk,bkhd->bqhd', p.astype(v.dtype), v)

    o = lax.map(attend, (blocks(q_nope), blocks(q_rope)))
    return jnp.moveaxis(o, 0, 1).reshape(B, S, H * MLA_DV)


def even_layer(h, w_in, a_up, a_bias, gla_norm_g, pool_w, pool_scale, w_out):
    B, S, _ = h.shape
    z = h @ w_in
    q, k, v, gla_gate, a_lr, pool_u, pool_gate = split_cols(z, E_SPLITS)
    q = to_heads(q, GLA_HEADS).astype(F32) * GLA_DK ** -0.5
    k = to_heads(k, GLA_HEADS).astype(F32)
    v = to_heads(v, GLA_HEADS).astype(F32)
    lr_f, lr_b = jnp.split(a_lr, 2, axis=-1)
    log_a_f = jax.nn.log_sigmoid((lr_f @ a_up[0] + a_bias[0]).astype(F32)) / GLA_GATE_NORM
    log_a_b = jax.nn.log_sigmoid((lr_b @ a_up[1] + a_bias[1]).astype(F32)) / GLA_GATE_NORM
    log_a_f = to_heads(log_a_f, GLA_HEADS)
    log_a_b = to_heads(log_a_b, GLA_HEADS)
    o = (gla_scan(q, k, v, log_a_f, True)
         + flip_seq(gla_scan(flip_seq(q), flip_seq(k), flip_seq(v), flip_seq(log_a_b), False)))
    gla_out = head_rmsnorm(o, gla_norm_g).astype(h.dtype) * jax.nn.silu(gla_gate)
    pool_out = multiscale_pool(pool_u, pool_w, pool_scale).astype(h.dtype) * jax.nn.silu(pool_gate)
    return jnp.concatenate([gla_out, pool_out], axis=-1) @ w_out


def odd_layer(h, cos, sin, w_in, q_norm_g, q_up, kv_norm_g, kv_up, if_bias, ml_norm_g, w_out):
    B, S, _ = h.shape
    z = h @ w_in
    cq, ckv, k_rope, mla_gate, mq, mk, mv, mo, mif, ml_gate = split_cols(z, O_SPLITS)
    qh = (rmsnorm(cq, q_norm_g) @ q_up).reshape(B, S, MLA_HEADS, MLA_NOPE + MLA_ROPE)
    q_nope, q_rope = qh[..., :MLA_NOPE], qh[..., MLA_NOPE:]
    kvh = (rmsnorm(ckv, kv_norm_g) @ kv_up).reshape(B, S, MLA_HEADS, MLA_NOPE + MLA_DV)
    k_nope, v_mla = kvh[..., :MLA_NOPE], kvh[..., MLA_NOPE:]
    q_rope = apply_rope(q_rope, cos[:, None, :], sin[:, None, :])
    k_rope = apply_rope(k_rope, cos, sin)
    mla_out = mla_attention(q_nope, q_rope, k_nope, k_rope, v_mla) * jax.nn.silu(mla_gate)
    q = to_heads(mq, ML_HEADS).astype(F32)
    k = to_heads(mk, ML_HEADS).astype(F32) * ML_DH ** -0.5
    v = to_heads(mv, ML_HEADS).astype(F32)
    gates = (mif.astype(F32) + if_bias.astype(F32)).reshape(B, S, 4, ML_HEADS).transpose(2, 0, 3, 1)
    li_f, li_b = gates[0], gates[1]
    lf_f, lf_b = jax.nn.log_sigmoid(gates[2]), jax.nn.log_sigmoid(gates[3])
    hm = (mlstm_scan(q, k, v, li_f, lf_f, True)
          + flip_seq(mlstm_scan(flip_seq(q), flip_seq(k), flip_seq(v), flip_seq(li_b), flip_seq(lf_b), False)))
    ml_out = head_rmsnorm(hm, ml_norm_g) * jax.nn.sigmoid(mo.astype(F32))
    ml_out = ml_out.astype(h.dtype) * jax.nn.silu(ml_gate)
    return jnp.concatenate([mla_out.astype(h.dtype), ml_out], axis=-1) @ w_out


def trunk(x, norm_g, final_norm_g, e_w_in, e_gla_a_up, e_gla_a_bias, e_gla_norm_g, e_pool_w,
          e_pool_scale, e_w_out, o_w_in, o_q_norm_g, o_q_up, o_kv_norm_g, o_kv_up, o_if_bias,
          o_mlstm_norm_g, o_w_out):
    S = x.shape[1]
    cos, sin = rope_tables(S)
    for layer in range(DEPTH):
        h = rmsnorm(x, norm_g[layer])
        i = layer // 2
        if layer % 2 == 0:
            y = even_layer(h, e_w_in[i], e_gla_a_up[i], e_gla_a_bias[i], e_gla_norm_g[i],
                           e_pool_w[i], e_pool_scale[i], e_w_out[i])
        else:
            y = odd_layer(h, cos, sin, o_w_in[i], o_q_norm_g[i], o_q_up[i], o_kv_norm_g[i], o_kv_up[i],
                          o_if_bias[i], o_mlstm_norm_g[i], o_w_out[i])
        x = x + y.astype(x.dtype)
    return rmsnorm(x, final_norm_g)


def setup_inputs(seed: int = 0) -> dict:
    key = jax.random.key(seed)
    ks = jax.random.split(key, 24)

    def nrm(k, shape, scale):
        return jax.random.normal(k, shape, F32) * scale

    f_bias = jnp.tile(jnp.linspace(3.0, 6.0, ML_HEADS), 2)
    if_bias = jnp.concatenate([
        jnp.broadcast_to(0.1 * jax.random.normal(ks[20], (1, 2 * ML_HEADS), F32), (N_ODD, 2 * ML_HEADS)),
        f_bias[None, :] + 0.1 * jax.random.normal(ks[21], (N_ODD, 2 * ML_HEADS), F32)], axis=-1)
    if_bias = if_bias + 0.01 * jax.random.normal(ks[22], if_bias.shape, F32)
    return {
        'x_prompt': nrm(ks[0], (BATCH, SEQ, D_MODEL), 1.0),
        'x_sample': nrm(ks[1], (DEC_BATCH, DEC_SEQ, D_MODEL), 1.0),
        'norm_g': 1.0 + nrm(ks[2], (DEPTH, D_MODEL), 0.02),
        'final_norm_g': 1.0 + nrm(ks[3], (D_MODEL,), 0.02),
        'e_w_in': nrm(ks[4], (N_EVEN, D_MODEL, E_COLS), D_MODEL ** -0.5),
        'e_gla_a_up': nrm(ks[5], (N_EVEN, 2, GLA_LR, GLA_K_TOT), GLA_LR ** -0.5),
        'e_gla_a_bias': nrm(ks[6], (N_EVEN, 2, GLA_K_TOT), 0.1),
        'e_gla_norm_g': 1.0 + nrm(ks[7], (N_EVEN, GLA_DV), 0.02),
        'e_pool_w': nrm(ks[8], (N_EVEN, POOL_GROUPS, POOL_DG, POOL_DG), POOL_DG ** -0.5),
        'e_pool_scale': 1.0 + nrm(ks[9], (N_EVEN, POOL_W), 0.1),
        'e_w_out': nrm(ks[10], (N_EVEN, E_OUT, D_MODEL), E_OUT ** -0.5),
        'o_w_in': nrm(ks[11], (N_ODD, D_MODEL, O_COLS), D_MODEL ** -0.5),
        'o_q_norm_g': 1.0 + nrm(ks[12], (N_ODD, MLA_Q_LORA), 0.02),
        'o_q_up': nrm(ks[13], (N_ODD, MLA_Q_LORA, MLA_HEADS * (MLA_NOPE + MLA_ROPE)), MLA_Q_LORA ** -0.5),
        'o_kv_norm_g': 1.0 + nrm(ks[14], (N_ODD, MLA_KV_LORA), 0.02),
        'o_kv_up': nrm(ks[15], (N_ODD, MLA_KV_LORA, MLA_HEADS * (MLA_NOPE + MLA_DV)), MLA_KV_LORA ** -0.5),
        'o_if_bias': if_bias,
        'o_mlstm_norm_g': 1.0 + nrm(ks[16], (N_ODD, ML_DH), 0.02),
        'o_w_out': nrm(ks[17], (N_ODD, O_OUT, D_MODEL), O_OUT ** -0.5),
    }


def reference(x_prompt, x_sample, norm_g, final_norm_g, e_w_in, e_gla_a_up, e_gla_a_bias, e_gla_norm_g,
              e_pool_w, e_pool_scale, e_w_out, o_w_in, o_q_norm_g, o_q_up, o_kv_norm_g, o_kv_up,
              o_if_bias, o_mlstm_norm_g, o_w_out):
    y_prompt = trunk(x_prompt, norm_g, final_norm_g, e_w_in, e_gla_a_up, e_gla_a_bias, e_gla_norm_g,
                     e_pool_w, e_pool_scale, e_w_out, o_w_in, o_q_norm_g, o_q_up, o_kv_norm_g, o_kv_up,
                     o_if_bias, o_mlstm_norm_g, o_w_out)
    y_sample = trunk(x_sample, norm_g, final_norm_g, e_w_in, e_gla_a_up, e_gla_a_bias, e_gla_norm_g,
                     e_pool_w, e_pool_scale, e_w_out, o_w_in, o_q_norm_g, o_q_up, o_kv_norm_g, o_kv_up,
                     o_if_bias, o_mlstm_norm_g, o_w_out)
    return (y_prompt, y_sample)
```

```python
import numpy as np
import ml_dtypes
from contextlib import ExitStack
import concourse.bass as bass
import concourse.mybir as mybir
from concourse.bass_utils import run_bass_kernel_spmd

F32 = mybir.dt.float32
BF16 = mybir.dt.bfloat16
ALU = mybir.AluOpType
AF = mybir.ActivationFunctionType

ENGS = ("pe", "act", "dve", "pool", "sp")
SEM_WRAP = 30000
P = 128
DM = 1024
EPS = 1e-6


class Buf:
    __slots__ = ("name", "writers", "readers")

    def __init__(self, name):
        self.name = name
        self.writers = []
        self.readers = []


class Instr:
    __slots__ = ("eng", "fn", "deps", "signal", "count", "semidx", "is_dma", "dma_slot", "dma_target", "dma_prev")

    def __init__(self, eng, fn, is_dma):
        self.eng = eng
        self.fn = fn
        self.deps = []
        self.signal = False
        self.count = 0
        self.semidx = 0
        self.is_dma = is_dma
        self.dma_slot = None
        self.dma_target = 0
        self.dma_prev = None


class Sched:
    def __init__(self, same_engine_sync=True, dma_slots=8):
        self.lists = {e: [] for e in ENGS}
        self.same_engine_sync = same_engine_sync
        self.dma_slots = dma_slots
        self.dma_count = {e: 0 for e in ENGS}
        self.dma_hist = {e: [] for e in ENGS}
        self.last_compute = {e: None for e in ENGS}
        self.cur_fence = None

    def add(self, eng, fn, reads=(), writes=(), dma=False):
        I = Instr(eng, fn, dma)
        deps = {}
        for b in reads:
            for w in b.writers:
                deps[id(w)] = w
        for b in writes:
            for w in b.writers:
                deps[id(w)] = w
            for r in b.readers:
                deps[id(r)] = r
        if self.cur_fence is not None:
            deps[id(self.cur_fence)] = self.cur_fence
        for d in deps.values():
            if (not d.is_dma) and d.eng == eng and not dma:
                if eng == "pe" or eng == "sp" or not self.same_engine_sync:
                    continue
            if not d.is_dma:
                d.signal = True
            I.deps.append(d)
        if dma:
            n = self.dma_count[eng]
            self.dma_count[eng] = n + 1
            I.dma_slot = n % self.dma_slots
            I.dma_target = 16 * (n // self.dma_slots + 1)
            if n >= self.dma_slots:
                I.dma_prev = self.dma_hist[eng][n - self.dma_slots]
            self.dma_hist[eng].append(I)
        else:
            self.last_compute[eng] = I
        for b in reads:
            if dma:
                b.readers.append(I)
            else:
                b.readers = [r for r in b.readers if r.is_dma or r.eng != eng]
                b.readers.append(I)
        for b in writes:
            b.writers = [I]
            b.readers = []
        self.lists[eng].append(I)
        return I

    def fence(self):
        I = Instr("sp", lambda e: e.nop(), False)
        for e in ENGS:
            lc = self.last_compute[e]
            if lc is not None and e != "sp":
                lc.signal = True
                I.deps.append(lc)
            h = self.dma_hist[e]
            for d in h[-self.dma_slots:]:
                I.deps.append(d)
        I.signal = True
        self.lists["sp"].append(I)
        self.last_compute["sp"] = I
        self.cur_fence = I

    def emit(self, nc, E):
        nsem = {}
        for e in ENGS:
            c = 0
            si = 0
            for I in self.lists[e]:
                if I.is_dma:
                    continue
                if I.signal:
                    c += 1
                    if c > SEM_WRAP:
                        si += 1
                        c = 1
                    I.count = c
                    I.semidx = si
            nsem[e] = si + 1
        prog = {e: [E(nc.semaphore(f"pg_{e}{i}")) for i in range(nsem[e])] for e in ENGS}
        dsem = {e: [E(nc.semaphore(f"dm_{e}{i}")) for i in range(self.dma_slots)]
                for e in ENGS if self.dma_count[e] > 0}
        block = E(nc.Block())
        engobj = {"pe": block.tensor, "act": block.scalar, "dve": block.vector, "pool": block.gpsimd,
                  "sp": block.sync}
        stats = {e: [0, 0] for e in ENGS}

        def run(e, eng):
            waited = {}
            for I in self.lists[e]:
                need = {}
                for d in I.deps:
                    if d.is_dma:
                        key = ("d", d.eng, d.dma_slot)
                        val = d.dma_target
                    else:
                        key = ("p", d.eng, d.semidx)
                        val = d.count
                    if need.get(key, 0) < val:
                        need[key] = val
                if I.is_dma and I.dma_prev is not None:
                    key = ("d", e, I.dma_slot)
                    val = I.dma_prev.dma_target
                    if need.get(key, 0) < val:
                        need[key] = val
                for key, val in need.items():
                    if waited.get(key, 0) >= val:
                        continue
                    waited[key] = val
                    sem = dsem[key[1]][key[2]] if key[0] == "d" else prog[key[1]][key[2]]
                    eng.wait_ge(sem, val)
                    stats[e][1] += 1
                ins = I.fn(eng)
                stats[e][0] += 1
                if I.is_dma:
                    ins.then_inc(dsem[e][I.dma_slot], 16)
                elif I.signal:
                    ins.then_inc(prog[e][I.semidx], 1)
            if e in dsem:
                n = self.dma_count[e]
                for s in range(min(n, self.dma_slots)):
                    last_n = ((n - 1 - s) // self.dma_slots) * self.dma_slots + s
                    eng.wait_ge(dsem[e][s], 16 * (last_n // self.dma_slots + 1))

        for e in ENGS:
            if not self.lists[e]:
                continue

            def mk(e):
                def f(eng):
                    run(e, eng)
                return f
            engobj[e](mk(e))
        return stats


class Tile:
    def __init__(self, ap, name, nsub=1):
        self.ap = ap
        self.bs = [Buf(f"{name}.{i}") for i in range(nsub)]

    def __getitem__(self, idx):
        return self.ap[idx]

    @property
    def b(self):
        return self.bs[0]


class Arena:
    def __init__(self, tensor, size):
        self.t = tensor
        self.size = size
        self.off = 0
        self.peak = 0

    def alloc(self, name, free, nsub=1):
        n = 1
        for f in free:
            n *= f
        n2 = (n + 3) // 4 * 4
        assert self.off + n2 <= self.size, f"arena overflow {name}: {self.off}+{n2}>{self.size}"
        a = self.t[:, self.off:self.off + n]
        self.off += n2
        self.peak = max(self.peak, self.off)
        if len(free) == 2:
            a = a.rearrange("p (a b) -> p a b", a=free[0])
        elif len(free) == 3:
            a = a.rearrange("p (a b c) -> p a b c", a=free[0], b=free[1])
        return Tile(a, name, nsub)


def interleave(gens):
    gens = [g for g in gens if g is not None]
    while gens:
        for g in list(gens):
            try:
                next(g)
            except StopIteration:
                gens.remove(g)


def _bl(xs):
    out = []
    for x in xs:
        if isinstance(x, Buf):
            out.append(x)
        elif isinstance(x, Tile):
            out.extend(x.bs)
        else:
            out.extend(_bl(x))
    return out


E_Q, E_K, E_V, E_G, E_LR, E_PU, E_PG, E_COLS = 0, 512, 1024, 2048, 3072, 3104, 3616, 4128
O_CQ, O_CKV, O_KR, O_MG, O_MQ, O_MK, O_MV, O_MO, O_IF, O_MLG, O_COLS = (0, 384, 640, 672, 1184, 1696, 2208, 2720,
                                                                      3232, 3248, 3760)
WMAX = 4128
A16_SIZE = 68000
A32_SIZE = 14848


class Builder:
    def __init__(self, seqs, layers=(0, 1, 2, 3), TB=256, final_norm=True, smax=None):
        self.seqs = list(seqs)
        self.layers = list(layers)
        self.TB = TB
        self.NT = TB // P
        self.final_norm = final_norm
        self.smax = smax or max(s for _, s in seqs)
        self.nc = bass.Bass("TRN2", target_bir_lowering=False)
        self.S = Sched()
        self.dr = {}
        self._decl_dram()

    def _in(self, name, shape, dt=F32):
        self.dr[name] = self.nc.dram_tensor(name, list(shape), dt, kind="ExternalInput").ap()

    def _scr(self, name, shape, dt):
        self.dr[name] = self.nc.dram_tensor(name, list(shape), dt, kind="Internal").ap()

    def _decl_dram(self):
        nc = self.nc
        for tag, S in self.seqs:
            self._in(f"x_{tag}", [S, DM])
            self.dr[f"y_{tag}"] = nc.dram_tensor(f"y_{tag}", [S, DM], F32, kind="ExternalOutput").ap()
            self._in(f"invcnt_{tag}", [4, S])
            self._scr(f"qT_{tag}", [4, P, S], BF16)
            self._scr(f"kT_{tag}", [4, P, S], BF16)
            self._scr(f"puT_{tag}", [4, P, S], BF16)
            self._scr(f"pgT_{tag}", [4, P, S], BF16)
            self._scr(f"k_{tag}", [S, 512], BF16)
            self._scr(f"spb_{tag}", [S, 512], BF16)
            self._scr(f"v_{tag}", [S, 1024], BF16)
            self._scr(f"gg_{tag}", [S, 1024], BF16)
            self._scr(f"of_{tag}", [S, 1024], F32)
            self._scr(f"Q_{tag}", [8, 96, S], BF16)
            self._scr(f"Kn_{tag}", [4, P, S], BF16)
            self._scr(f"KR_{tag}", [32, S], BF16)
            self._scr(f"V_{tag}", [S, 512], BF16)
            self._scr(f"AT_{tag}", [4, P, S], BF16)
            self._scr(f"mgT_{tag}", [4, P, S], BF16)
            self._scr(f"mqT_{tag}", [4, P, S], BF16)
            self._scr(f"mkT_{tag}", [4, P, S], BF16)
            self._scr(f"mk_{tag}", [S, 512], BF16)
            self._scr(f"mv_{tag}", [S, 512], BF16)
            self._scr(f"mog_{tag}", [S, 512], BF16)
            self._scr(f"gb_{tag}", [S, 16], F32)
            self._scr(f"hf_{tag}", [S, 512], F32)
        self._in("norm_gT", [4, P, 8])
        self._in("final_g", [DM])
        self._in("e_w_in", [2, DM, E_COLS])
        self._in("e_aup", [2, 2, 17, 512])
        self._in("e_gn", [2, 256])
        self._in("e_pool_w", [2, 4, P, P])
        self._in("e_pscT", [2, P, 4])
        self._in("e_w_out", [2, 1536, DM])
        self._in("o_w_in", [2, DM, O_COLS])
        self._in("o_qgT", [2, P, 3])
        self._in("o_q_up", [2, 384, 768])
        self._in("o_kvgT", [2, P, 2])
        self._in("o_kv_up", [2, 256, 1024])
        self._in("o_if_bias", [2, 16])
        self._in("o_mlg", [2, P])
        self._in("o_w_out", [2, 1024, DM])
        self._in("c_bf", [P, 5, P], BF16)
        self._in("c_f32", [P, 3, P], F32)
        self._in("ropeT", [2, 32, self.smax], F32)

    def A(self, eng, fn, r=(), w=()):
        return self.S.add(eng, fn, reads=_bl(r), writes=_bl(w))

    def DMA(self, out, in_, r=(), w=(), q="sp"):
        return self.S.add(q, lambda e: e.dma_start(out=out, in_=in_), reads=_bl(r), writes=_bl(w), dma=True)

    def mm(self, out, lhsT, rhs, start=True, stop=True, r=(), w=()):
        return self.A("pe", lambda e: e.matmul(out, lhsT=lhsT, rhs=rhs, start=start, stop=stop), r, w)

    def tr(self, out, in_, r=(), w=()):
        idt = self.c_bf[:, 0, :]
        return self.A("pe", lambda e: e.transpose(out=out, in_=in_, identity=idt), list(r) + [self.c_bf], w)

    def act(self, out, in_, func, r=(), w=(), scale=1.0, bias=0.0, accum=None):
        if accum is None:
            return self.A("act", lambda e: e.activation(out=out, in_=in_, func=func, bias=bias, scale=scale), r, w)
        return self.A("act", lambda e: e.activation(out=out, in_=in_, func=func, bias=bias, scale=scale,
                                                    accum_out=accum), r, w)

    def ts(self, eng, out, in0, s1, s2=None, op0=ALU.mult, op1=None, r=(), w=()):
        if op1 is None:
            return self.A(eng, lambda e: e.tensor_scalar(out=out, in0=in0, scalar1=s1, scalar2=None, op0=op0), r, w)
        return self.A(eng, lambda e: e.tensor_scalar(out=out, in0=in0, scalar1=s1, scalar2=s2, op0=op0, op1=op1), r, w)

    def tt(self, eng, out, in0, in1, op, r=(), w=()):
        return self.A(eng, lambda e: e.tensor_tensor(out=out, in0=in0, in1=in1, op=op), r, w)

    def stt(self, out, in0, scalar, in1, op0, op1, r=(), w=()):
        return self.A("dve", lambda e: e.scalar_tensor_tensor(out=out, in0=in0, scalar=scalar, in1=in1, op0=op0,
                                                               op1=op1), r, w)

    def cp(self, eng, out, in_, r=(), w=()):
        if eng == "act":
            return self.A("act", lambda e: e.copy(out=out, in_=in_), r, w)
        return self.A(eng, lambda e: e.tensor_copy(out=out, in_=in_), r, w)

    def ms(self, eng, ap, val, w=()):
        return self.A(eng, lambda e: e.memset(ap, val), (), w)

    def silu_ps(self, dst_ap, dst_bufs, pst, et, W):
        e = et[:, 0:W]
        self.act(e, pst.ap, AF.Exp, r=[pst], w=[et], scale=-1.0)
        self.ts("dve", e, e, 1.0, op0=ALU.add, r=[et], w=[et])
        self.A("dve", lambda en, o=e, i_=e: en.reciprocal(out=o, in_=i_), r=[et], w=[et])
        self.tt("dve", dst_ap, pst.ap, e, ALU.mult, r=[pst, et], w=dst_bufs)

    def rsqrt_(self, t, n, p0=0, p1=P):
        self.act(t[p0:p1], t[p0:p1], AF.Ln, r=[t], w=[t], scale=1.0 / n, bias=self.eps_t[p0:p1, 0:1])
        self.act(t[p0:p1], t[p0:p1], AF.Exp, r=[t], w=[t], scale=-0.5)

    def build(self):
        nc = self.nc
        with ExitStack() as es:
            E = es.enter_context
            self.a16 = Arena(E(nc.sbuf_tensor("arena16", [P, A16_SIZE], BF16)).ap(), A16_SIZE)
            self.a32 = Arena(E(nc.sbuf_tensor("arena32", [P, A32_SIZE], F32)).ap(), A32_SIZE)
            self.c_bf = Tile(E(nc.sbuf_tensor("sb_c_bf", [P, 5, P], BF16)).ap(), "c_bf")
            self.c_f32 = Tile(E(nc.sbuf_tensor("sb_c_f32", [P, 3, P], F32)).ap(), "c_f32")
            self.eps_t = Tile(E(nc.sbuf_tensor("eps_t", [P, 1], F32)).ap(), "eps")
            self.gcol = Tile(E(nc.sbuf_tensor("gcol", [P, 8], F32)).ap(), "gcol")
            self.gfin = Tile(E(nc.sbuf_tensor("gfin", [P, DM], F32)).ap(), "gfin")
            self.psall = E(nc.psum_tensor("psall", [P, 7 * 512], F32)).ap()
            self.psb = [self.psall[:, i * 512:(i + 1) * 512] for i in range(7)]
            self.psT = Tile(E(nc.psum_tensor("psT", [P, 1024], BF16)).ap(), "psT")
            self.bankbuf = [Buf(f"bank{i}") for i in range(7)]
            self.DMA(self.c_bf.ap, self.dr["c_bf"], w=[self.c_bf])
            self.DMA(self.c_f32.ap, self.dr["c_f32"], w=[self.c_f32])
            self.DMA(self.gfin.ap, self.dr["final_g"].partition_broadcast(P), w=[self.gfin])
            self.ms("dve", self.eps_t.ap, EPS, w=[self.eps_t])
            self.S.fence()
            nl = len(self.layers)
            for li, layer in enumerate(self.layers):
                last = (li == nl - 1)
                first = (li == 0)
                if layer % 2 == 0:
                    self.even_layer(layer, first, last)
                else:
                    self.odd_layer(layer, first, last)
            stats = self.S.emit(nc, E)
            self.stats = stats
        return nc

    def ps(self, bank, off, width, name, parts=P):
        t = Tile(self.psb[bank][0:parts, off:off + width], name)
        t.bs = [self.bankbuf[bank]]
        return t

    def load_w_in(self, dram_w, ncols, layer):
        a16, a32 = self.a16, self.a32
        win = a16.alloc("win", (8, WMAX))
        self.DMA(self.gcol.ap, self.dr["norm_gT"][layer], w=[self.gcol])
        m32 = a32.off
        CH = 1032
        nch = (ncols + CH - 1) // CH
        stg = [a32.alloc(f"wstg{i}", (CH,)) for i in range(3)]
        k = 0
        for kc in range(8):
            for c in range(nch):
                c0 = c * CH
                cw = min(CH, ncols - c0)
                s = stg[k % 3]
                self.DMA(s[:, 0:cw], dram_w[kc * P:(kc + 1) * P, c0:c0 + cw], w=[s])
                eng = ("dve", "pool", "act")[k % 3]
                if eng == "act":
                    self.A("act", lambda e, s=s, kc=kc, c0=c0, cw=cw: e.mul(out=win[:, kc, c0:c0 + cw], in_=s[:, 0:cw],
                                                                        mul=self.gcol[:, kc:kc + 1]),
                           r=[s, self.gcol], w=[win])
                else:
                    self.ts(eng, win[:, kc, c0:c0 + cw], s[:, 0:cw], self.gcol[:, kc:kc + 1], r=[s, self.gcol], w=[win])
                k += 1
        a32.off = m32
        return win

    def load_w_out(self, dram_w, nchunks):
        a16, a32 = self.a16, self.a32
        wout = a16.alloc("wout", (nchunks, DM))
        m32 = a32.off
        stg = [a32.alloc(f"wostg{i}", (DM,)) for i in range(3)]
        for c in range(nchunks):
            s = stg[c % 3]
            self.DMA(s.ap, dram_w[c * P:(c + 1) * P, :], w=[s])
            self.cp(("dve", "pool", "act")[c % 3], wout[:, c, :], s.ap, r=[s], w=[wout])
        a32.off = m32
        return wout

    def alloc_xh(self):
        a16, a32 = self.a16, self.a32
        self.xt = [a32.alloc(f"xt{i}", (DM,)) for i in range(2)]
        self.hn = [a16.alloc(f"hn{i}", (DM,)) for i in range(2)]
        self.sqj = a16.alloc("sqj", (DM,))
        self.ssq = [a32.alloc(f"ssq{i}", (1,)) for i in range(2)]
        self.hT = [a16.alloc(f"hT{i}", (8, self.TB)) for i in range(2)]
        self.xcnt = 0

    def x_to_hT(self, xsrc, t0, bi):
        hT = self.hT[bi % 2]
        for j in range(self.NT):
            k = self.xcnt
            self.xcnt += 1
            xt, hn, ssq = self.xt[k % 2], self.hn[k % 2], self.ssq[k % 2]
            self.DMA(xt.ap, xsrc[t0 + j * P:t0 + (j + 1) * P, :], w=[xt])
            self.act(self.sqj.ap, xt.ap, AF.Square, r=[xt], w=[self.sqj, ssq], accum=ssq.ap)
            self.rsqrt_(ssq, DM)
            self.ts("dve", hn.ap, xt.ap, ssq[:, 0:1], r=[xt, ssq], w=[hn])
            yield
            for c in range(8):
                self.tr(self.psT[:, c * P:(c + 1) * P], hn[:, c * P:(c + 1) * P], r=[hn], w=[self.psT])
            self.cp("act" if j % 2 == 0 else "dve", hT[:, :, j * P:(j + 1) * P],
                    self.psT.ap.rearrange("p (c t) -> p c t", c=8), r=[self.psT], w=[hT])
            yield

    def proj_fm(self, win, hT, col0, m, pst):
        for kc in range(8):
            self.mm(pst.ap, win[:, kc, col0:col0 + m], hT[:, kc, :], start=(kc == 0), stop=(kc == 7),
                    r=[win, hT], w=[pst])

    def proj_tm(self, win, hT, j, col0, n, pst):
        for kc in range(8):
            self.mm(pst.ap, hT[:, kc, j * P:(j + 1) * P], win[:, kc, col0:col0 + n], start=(kc == 0), stop=(kc == 7),
                    r=[win, hT], w=[pst])

    def alloc_resid(self):
        a32 = self.a32
        self.xr = [a32.alloc(f"xr{i}", (DM,)) for i in range(2)]
        self.xo = [a32.alloc(f"xo{i}", (DM,)) for i in range(2)]
        self.fssq = [a32.alloc(f"fssq{i}", (1,)) for i in range(2)]
        self.fsq = self.a16.alloc("fsq", (DM,))
        self.rcnt = 0

    def resid_out(self, featT, nchunks, wout, j, xsrc, ydst, t, last, psA, psB):
        k = self.rcnt
        self.rcnt += 1
        xr, xo, fssq = self.xr[k % 2], self.xo[k % 2], self.fssq[k % 2]
        self.DMA(xr.ap, xsrc[t:t + P, :], w=[xr])
        for n, pst in enumerate((psA, psB)):
            for c in range(nchunks):
                self.mm(pst.ap, featT[:, c, j * P:(j + 1) * P], wout[:, c, n * 512:(n + 1) * 512],
                        start=(c == 0), stop=(c == nchunks - 1), r=[featT, wout], w=[pst])
            self.tt("dve", xo[:, n * 512:(n + 1) * 512], pst.ap, xr[:, n * 512:(n + 1) * 512], ALU.add,
                    r=[pst, xr], w=[xo])
            yield
        if last and self.final_norm:
            self.act(self.fsq.ap, xo.ap, AF.Square, r=[xo], w=[self.fsq, fssq], accum=fssq.ap)
            self.rsqrt_(fssq, DM)
            self.stt(xo.ap, xo.ap, fssq[:, 0:1], self.gfin.ap, ALU.mult, ALU.mult, r=[xo, fssq, self.gfin], w=[xo])
        self.DMA(ydst[t:t + P, :], xo.ap, r=[xo])
        yield

    def even_layer(self, layer, first, last):
        i = layer // 2
        a16, a32 = self.a16, self.a32
        dr = self.dr
        S = self.S
        TB, NT = self.TB, self.NT
        a16.off = 0
        a32.off = 0
        aup32 = a32.alloc("aup32", (2, 512))
        aup = a16.alloc("aup", (2, 512))
        self.DMA(aup32[0:17], dr["e_aup"][i].rearrange("d r c -> r d c"), w=[aup32])
        self.cp("dve", aup[0:17], aup32[0:17], r=[aup32], w=[aup])
        win = self.load_w_in(dr["e_w_in"][i], E_COLS, layer)
        S.fence()
        mW16, mW32 = a16.off, 0
        for tag, Sq in self.seqs:
            a16.off, a32.off = mW16, mW32
            self.even_fwd(tag, Sq, win, aup, first)
            S.fence()
        a16.off, a32.off = 0, 0
        gn = a32.alloc("gn_bc", (256,))
        self.DMA(gn.ap, dr["e_gn"][i].partition_broadcast(P), w=[gn])
        psc = a32.alloc("psc", (4,))
        self.DMA(psc.ap, dr["e_pscT"][i], w=[psc])
        mB32 = a32.off
        pw32 = a32.alloc("pw32", (4, P))
        pw = a16.alloc("pw", (4, P))
        self.DMA(pw32.ap, dr["e_pool_w"][i].rearrange("g c d -> c g d"), w=[pw32])
        self.cp("dve", pw.ap, pw32.ap, r=[pw32], w=[pw])
        wout = self.load_w_out(dr["e_w_out"][i], 12)
        S.fence()
        mB16 = a16.off
        for tag, Sq in self.seqs:
            a16.off, a32.off = mB16, mB32
            self.even_bwd(tag, Sq, wout, gn, pw, psc, first, last)
            S.fence()

    def gla_alloc(self):
        a16, a32 = self.a16, self.a32
        self.gE = [[a32.alloc(f"gE{b}{k}", (P,)) for k in range(3)] for b in range(2)]
        self.gq = [[a16.alloc(f"gq{b}{k}", (P,)) for k in range(4)] for b in range(2)]
        self.gS = a32.alloc("gS", (4, 256), nsub=4)
        self.gSb = a16.alloc("gSb", (4, 256), nsub=4)
        self.gps = []
        for b in range(2):
            small = 3 + 2 * b
            big = 4 + 2 * b
            self.gps.append(dict(bT=self.ps(small, 0, P, f"bT{b}"), cT=self.ps(small, P, P, f"cT{b}"),
                                 aT=self.ps(small, 2 * P, P, f"aT{b}"), o=self.ps(big, 0, 256, f"o{b}"),
                                 Pm=self.ps(big, 256, 256, f"Pm{b}")))
        self.gcnt = 0

    def gla_init_state(self):
        self.ms("dve", self.gS.ap, 0.0, w=[self.gS])
        self.ms("pool", self.gSb.ap, 0.0, w=[self.gSb])

    def gla_step(self, kk, h, sp, qT, kT, ktok, v, bwd, post):
        E1, E2, E3 = self.gE[kk]
        qeT, kdT, kl, aTm = self.gq[kk]
        g = self.gps[kk]
        cbf = self.c_bf
        TRI = cbf[:, 2, :] if bwd else cbf[:, 1, :]
        CL = cbf[:, 4, :] if bwd else cbf[:, 3, :]
        MSK = cbf[:, 3, :] if bwd else cbf[:, 1, :]
        Sb = self.gSb.bs[h]
        Sf = self.gS.bs[h]
        self.mm(g["bT"].ap, sp[0], TRI, r=[sp[1], cbf], w=[g["bT"]])
        self.mm(g["cT"].ap, CL, sp[0], r=[sp[1], cbf], w=[g["cT"]])
        yield
        self.act(E1.ap, g["bT"].ap, AF.Exp, r=[g["bT"]], w=[E1], scale=-1.0 / 16)
        self.act(E2.ap, g["bT"].ap, AF.Exp, r=[g["bT"]], w=[E2], scale=1.0 / 16)
        self.act(E3.ap, g["cT"].ap, AF.Exp, r=[g["cT"]], w=[E3], scale=-1.0 / 16)
        yield
        self.tt("pool", qeT.ap, qT[0], E1.ap, ALU.mult, r=[qT[1], E1], w=[qeT])
        self.tt("pool", kdT.ap, kT[0], E2.ap, ALU.mult, r=[kT[1], E2], w=[kdT])
        self.tt("pool", kl.ap, ktok[0], E3.ap, ALU.mult, r=[ktok[1], E3], w=[kl])
        yield
        self.mm(g["aT"].ap, kdT.ap, qeT.ap, r=[kdT, qeT], w=[g["aT"]])
        yield
        self.tt("dve", aTm.ap, g["aT"].ap, MSK, ALU.mult, r=[g["aT"], cbf], w=[aTm])
        yield
        self.mm(g["o"].ap, aTm.ap, v[0], start=True, stop=False, r=[aTm, v[1]], w=[g["o"]])
        self.mm(g["o"].ap, qeT.ap, self.gSb[:, h, :], start=False, stop=True, r=[qeT, Sb], w=[g["o"]])
        self.mm(g["Pm"].ap, kl.ap, v[0], r=[kl, v[1]], w=[g["Pm"]])
        yield
        lam = E1[:, 0:1] if bwd else E1[:, P - 1:P]
        self.stt(self.gS[:, h, :], self.gS[:, h, :], lam, g["Pm"].ap, ALU.mult, ALU.add, r=[Sf, E1, g["Pm"]], w=[Sf])
        self.cp("act", self.gSb[:, h, :], self.gS[:, h, :], r=[Sf], w=[Sb])
        yield from post(g["o"])

    def even_fwd(self, tag, Sq, win, aup, first):
        a16, a32, dr, TB, NT = self.a16, self.a32, self.dr, self.TB, self.NT
        xsrc = dr[f"x_{tag}"] if first else dr[f"y_{tag}"]
        self.alloc_xh()
        self.gla_alloc()
        fmq = [a16.alloc(f"fm_qT{b}", (4, TB), nsub=4) for b in range(2)]
        fmk = [a16.alloc(f"fm_kT{b}", (4, TB), nsub=4) for b in range(2)]
        fpu = a16.alloc("fm_puT", (4, TB), nsub=4)
        fpg = a16.alloc("fm_pgT", (4, TB), nsub=4)
        lrx = [a16.alloc(f"lrx{d}", (TB,)) for d in range(2)]
        NB4 = 2 * NT
        tk = [a16.alloc(f"tk{b}", (512,)) for b in range(NB4)]
        tspf = [a16.alloc(f"tspf{b}", (512,)) for b in range(NB4)]
        tv = [a16.alloc(f"tv{b}", (1024,), nsub=2) for b in range(NB4)]
        tspb = [a16.alloc(f"tspb{b}", (512,)) for b in range(2)]
        tgg = [a16.alloc(f"tgg{b}", (1024,), nsub=2) for b in range(2)]
        tof = [a32.alloc(f"tof{b}", (1024,), nsub=4) for b in range(2)]
        etmp = [a32.alloc(f"etmp{b}", (512,)) for b in range(2)]
        esil = [a32.alloc(f"esil{b}", (512,)) for b in range(2)]
        ps3 = [self.ps(k, 0, 512, f"ps3_{k}") for k in range(3)]
        for d in range(2):
            self.ms("pool", lrx[d][0:32], 1.0, w=[lrx[d]])
        self.gla_init_state()
        nb = Sq // TB
        st = dict(pc=0, tc=0)

        def nps():
            p = ps3[st["pc"] % 3]
            st["pc"] += 1
            return p

        def Pgen(bi):
            t0 = bi * TB
            bp = bi % 2
            hT = self.hT[bi % 2]
            yield from self.x_to_hT(xsrc, t0, bi)
            specs = [("qT", h, E_Q + h * P, fmq[bp]) for h in range(4)] + \
                    [("kT", h, E_K + h * P, fmk[bp]) for h in range(4)] + \
                    [("puT", h, E_PU + h * P, fpu) for h in range(4)] + \
                    [("pgT", h, E_PG + h * P, fpg) for h in range(4)]
            for (nm, h, col, dst) in specs:
                pf = nps()
                pst = Tile(pf[:, 0:TB], "pf")
                pst.bs = pf.bs
                self.proj_fm(win, hT, col, P, pst)
                if nm == "qT":
                    self.A("act", lambda e, o=dst[:, h, :], p=pst.ap: e.mul(out=o, in_=p, mul=float(P) ** -0.5),
                           r=[pst], w=[dst.bs[h]])
                elif nm == "pgT":
                    self.silu_ps(dst[:, h, :], [dst.bs[h]], pst, esil[h % 2], TB)
                else:
                    self.cp("dve", dst[:, h, :], pst.ap, r=[pst], w=[dst.bs[h]])
                yield
            for d in range(2):
                pf = nps()
                p16 = Tile(pf[0:16, 0:TB], "p16")
                p16.bs = pf.bs
                self.proj_fm(win, hT, E_LR + 16 * d, 16, p16)
                self.cp("dve", lrx[d][0:16, :], pf[0:16, 0:TB], r=[pf], w=[lrx[d]])
            yield
            for nm, src in (("qT", fmq[bp]), ("kT", fmk[bp]), ("puT", fpu), ("pgT", fpg)):
                self.DMA(dr[f"{nm}_{tag}"][:, :, t0:t0 + TB].rearrange("h p t -> p h t"), src.ap, r=[src])
            for j in range(NT):
                t = t0 + j * P
                b4 = (bi * NT + j) % NB4
                b2 = (bi * NT + j) % 2
                k_, spf_, v_, spb_, gg_ = tk[b4], tspf[b4], tv[b4], tspb[b2], tgg[b2]
                for d, dst in ((0, spf_), (1, spb_)):
                    pst = nps()
                    self.mm(pst.ap, lrx[d][0:17, j * P:(j + 1) * P], aup[0:17, d, :], r=[lrx[d], aup], w=[pst])
                    et = etmp[d]
                    self.act(et.ap, pst.ap, AF.Exp, r=[pst], w=[et], scale=-1.0)
                    self.act(dst.ap, et.ap, AF.Ln, r=[et], w=[dst], bias=1.0)
                    yield
                pst = nps()
                self.proj_tm(win, hT, j, E_K, 512, pst)
                self.cp("dve", k_.ap, pst.ap, r=[pst], w=[k_])
                yield
                for n in range(2):
                    pst = nps()
                    self.proj_tm(win, hT, j, E_V + n * 512, 512, pst)
                    self.cp("dve", v_[:, n * 512:(n + 1) * 512], pst.ap, r=[pst], w=[v_.bs[n]])
                    yield
                for n in range(2):
                    pst = nps()
                    self.proj_tm(win, hT, j, E_G + n * 512, 512, pst)
                    self.silu_ps(gg_[:, n * 512:(n + 1) * 512], [gg_.bs[n]], pst, esil[n], 512)
                    yield
                self.DMA(dr[f"k_{tag}"][t:t + P, :], k_.ap, r=[k_])
                self.DMA(dr[f"spb_{tag}"][t:t + P, :], spb_.ap, r=[spb_])
                self.DMA(dr[f"v_{tag}"][t:t + P, :], v_.ap, r=[v_])
                self.DMA(dr[f"gg_{tag}"][t:t + P, :], gg_.ap, r=[gg_])

        def Sgen(bi, par):
            t0 = bi * TB
            bp = bi % 2
            for j in range(NT):
                t = t0 + j * P
                b4 = (bi * NT + j) % NB4
                b2 = (bi * NT + j) % 2
                k_, spf_, v_, of_ = tk[b4], tspf[b4], tv[b4], tof[b2]
                for h in (par, par + 2):
                    def post(po, h=h, of_=of_, t=t):
                        self.cp("dve", of_[:, h * 256:(h + 1) * 256], po.ap, r=[po], w=[of_.bs[h]])
                        self.DMA(dr[f"of_{tag}"][t:t + P, h * 256:(h + 1) * 256], of_[:, h * 256:(h + 1) * 256],
                                 r=[of_.bs[h]])
                        yield
                    yield from self.gla_step(par, h, (spf_[:, h * P:(h + 1) * P], spf_),
                                             (fmq[bp][:, h, j * P:(j + 1) * P], fmq[bp].bs[h]),
                                             (fmk[bp][:, h, j * P:(j + 1) * P], fmk[bp].bs[h]),
                                             (k_[:, h * P:(h + 1) * P], k_),
                                             (v_[:, h * 256:(h + 1) * 256], v_.bs[h // 2]), False, post)

        for bi in range(nb + 1):
            streams = []
            if bi >= 1:
                streams += [Sgen(bi - 1, 0), Sgen(bi - 1, 1)]
            if bi < nb:
                streams.append(Pgen(bi))
            interleave(streams)

    def even_bwd(self, tag, Sq, wout, gn, pw, psc, first, last):
        a16, a32, dr, TB, NT = self.a16, self.a32, self.dr, self.TB, self.NT
        xsrc = dr[f"x_{tag}"] if first else dr[f"y_{tag}"]
        ydst = dr[f"y_{tag}"]
        self.gla_alloc()
        self.alloc_resid()
        H = 8
        fmq = [a16.alloc(f"bq{b}", (4, TB), nsub=4) for b in range(2)]
        fmk = [a16.alloc(f"bk{b}", (4, TB), nsub=4) for b in range(2)]
        fpg = [a16.alloc(f"bpg{b}", (4, TB)) for b in range(2)]
        fpu = [a16.alloc(f"bpu{b}", (4, TB + 2 * H)) for b in range(2)]
        icn = [a32.alloc(f"icn{b}", (4, TB)) for b in range(2)]
        tk = [a16.alloc(f"tk{b}", (512,)) for b in range(2)]
        tspb = [a16.alloc(f"tspb{b}", (512,)) for b in range(2)]
        tv = [a16.alloc(f"tv{b}", (1024,)) for b in range(2)]
        tgg = [a16.alloc(f"tgg{b}", (1024,)) for b in range(2)]
        tof = [a32.alloc(f"tof{b}", (1024,)) for b in range(2)]
        osum = [a32.alloc(f"osum{b}", (256,)) for b in range(2)]
        osq = [a16.alloc(f"osq{b}", (256,)) for b in range(2)]
        hss = [a32.alloc(f"hss{b}", (1,)) for b in range(2)]
        otmp = [a32.alloc(f"otmp{b}", (256,)) for b in range(2)]
        glo = [a16.alloc(f"glo{b}", (1024,), nsub=4) for b in range(2)]
        featT = [a16.alloc(f"featT{b}", (12, TB), nsub=12) for b in range(2)]
        pt32 = [a32.alloc(f"pt32_{k}", (TB + 2 * H,)) for k in range(3)]
        pwin = a32.alloc("pwin", (TB,))
        pld = a16.alloc("pld", (TB,))
        psA, psB = self.ps(1, 0, 512, "psA"), self.ps(2, 0, 512, "psB")
        psX = self.ps(0, 0, TB, "psX")
        self.gla_init_state()
        nb = Sq // TB
        tiles = [(bi, j) for bi in range(nb - 1, -1, -1) for j in range(NT - 1, -1, -1)]

        def Lgen(bi):
            t0 = bi * TB
            bb = bi % 2
            q_, k_f, pg_, pu_, ic_, ft = fmq[bb], fmk[bb], fpg[bb], fpu[bb], icn[bb], featT[bb]
            self.DMA(q_.ap, dr[f"qT_{tag}"][:, :, t0:t0 + TB].rearrange("h p t -> p h t"), w=[q_])
            self.DMA(k_f.ap, dr[f"kT_{tag}"][:, :, t0:t0 + TB].rearrange("h p t -> p h t"), w=[k_f])
            self.DMA(pg_.ap, dr[f"pgT_{tag}"][:, :, t0:t0 + TB].rearrange("h p t -> p h t"), w=[pg_])
            lo = max(t0 - H, 0)
            hi = min(t0 + TB + H, Sq)
            if lo != t0 - H or hi != t0 + TB + H:
                self.ms("pool", pu_.ap, 0.0, w=[pu_])
            self.DMA(pu_[:, :, lo - (t0 - H):hi - (t0 - H)],
                     dr[f"puT_{tag}"][:, :, lo:hi].rearrange("h p t -> p h t"), w=[pu_])
            for g in range(4):
                self.DMA(ic_[:, g, :], dr[f"invcnt_{tag}"][g, t0:t0 + TB].partition_broadcast(P), w=[ic_])
            yield
            for g in range(4):
                W = TB + 2 * H
                hw = (1, 2, 4, 8)[g]
                if g == 0:
                    self.tt("pool", pwin.ap, pu_[:, g, H - 1:H - 1 + TB], pu_[:, g, H:H + TB], ALU.add, r=[pu_], w=[pwin])
                else:
                    step = 1
                    cur_ap = pu_[:, g, :]
                    cur_t = pu_
                    width = W
                    for lv in range(g):
                        dst = pt32[lv]
                        width = width - step
                        self.tt("pool", dst[:, 0:width], cur_ap[:, 0:width], cur_ap[:, step:step + width], ALU.add,
                                r=[cur_t], w=[dst])
                        cur_ap = dst.ap
                        cur_t = dst
                        step *= 2
                    self.tt("pool", pwin.ap, cur_ap[:, H - hw:H - hw + TB], cur_ap[:, H:H + TB], ALU.add, r=[cur_t], w=[pwin])
                yield
                self.tt("pool", pwin.ap, pwin.ap, ic_[:, g, :], ALU.mult, r=[pwin, ic_], w=[pwin])
                self.tt("pool", pld.ap, pwin.ap, pu_[:, g, H:H + TB], ALU.subtract, r=[pwin, pu_], w=[pld])
                yield
                self.mm(psX.ap, pw[:, g, :], pld.ap, r=[pw, pld], w=[psX])
                self.stt(ft[:, 8 + g, :], psX.ap, psc[:, g:g + 1], pg_[:, g, :], ALU.mult, ALU.mult,
                         r=[psX, psc, pg_], w=[ft.bs[8 + g]])
                yield

        def TLgen(n):
            bi, j = tiles[n]
            t = bi * TB + j * P
            tb = n % 2
            self.DMA(tk[tb].ap, dr[f"k_{tag}"][t:t + P, :], w=[tk[tb]])
            self.DMA(tspb[tb].ap, dr[f"spb_{tag}"][t:t + P, :], w=[tspb[tb]])
            self.DMA(tv[tb].ap, dr[f"v_{tag}"][t:t + P, :], w=[tv[tb]])
            self.DMA(tgg[tb].ap, dr[f"gg_{tag}"][t:t + P, :], w=[tgg[tb]])
            self.DMA(tof[tb].ap, dr[f"of_{tag}"][t:t + P, :], w=[tof[tb]])
            yield

        def Sgen(n, par):
            bi, j = tiles[n]
            bb = bi % 2
            tb = n % 2
            q_, k_f = fmq[bb], fmk[bb]
            k_, spb_, v_, gg_, of_, gl = tk[tb], tspb[tb], tv[tb], tgg[tb], tof[tb], glo[tb]
            for h in (par, par + 2):
                def post(po, h=h):
                    os_, hs_, ot_, oq_ = osum[par], hss[par], otmp[par], osq[par]
                    self.tt("dve", os_.ap, po.ap, of_[:, h * 256:(h + 1) * 256], ALU.add, r=[po, of_], w=[os_])
                    yield
                    self.act(oq_.ap, os_.ap, AF.Square, r=[os_], w=[oq_, hs_], accum=hs_.ap)
                    self.rsqrt_(hs_, 256)
                    yield
                    self.stt(ot_.ap, os_.ap, hs_[:, 0:1], gn.ap, ALU.mult, ALU.mult, r=[os_, hs_, gn], w=[ot_])
                    yield
                    self.tt("pool", gl[:, h * 256:(h + 1) * 256], ot_.ap, gg_[:, h * 256:(h + 1) * 256], ALU.mult,
                            r=[ot_, gg_], w=[gl.bs[h]])
                    yield
                yield from self.gla_step(par, h, (spb_[:, h * P:(h + 1) * P], spb_),
                                         (q_[:, h, j * P:(j + 1) * P], q_.bs[h]),
                                         (k_f[:, h, j * P:(j + 1) * P], k_f.bs[h]),
                                         (k_[:, h * P:(h + 1) * P], k_),
                                         (v_[:, h * 256:(h + 1) * 256], v_), True, post)

        def Rgen(n):
            bi, j = tiles[n]
            bb = bi % 2
            tb = n % 2
            ft, gl = featT[bb], glo[tb]
            t = bi * TB + j * P
            for c in range(8):
                self.tr(self.psT[:, c * P:(c + 1) * P], gl[:, c * P:(c + 1) * P], r=[gl], w=[self.psT])
            self.cp("act", ft[:, 0:8, j * P:(j + 1) * P], self.psT.ap.rearrange("p (c t) -> p c t", c=8),
                    r=[self.psT], w=ft.bs[0:8])
            yield
            yield from self.resid_out(ft, 12, wout, j, xsrc, ydst, t, last, psA, psB)

        interleave([Lgen(tiles[0][0]), TLgen(0)])
        for n in range(len(tiles) + 1):
            streams = []
            if n < len(tiles):
                streams += [Sgen(n, 0), Sgen(n, 1)]
                if n + 1 < len(tiles):
                    streams.append(TLgen(n + 1))
                    if tiles[n + 1][0] != tiles[n][0]:
                        streams.append(Lgen(tiles[n + 1][0]))
            if n >= 1:
                streams.append(Rgen(n - 1))
            interleave(streams)

    def odd_layer(self, layer, first, last):
        i = layer // 2
        a16, a32, dr, S = self.a16, self.a32, self.dr, self.S
        a16.off = 0
        a32.off = 0
        ifb = a32.alloc("ifb", (16,))
        self.DMA(ifb.ap, dr["o_if_bias"][i].partition_broadcast(P), w=[ifb])
        mF32 = a32.off
        qg = a32.alloc("qg", (4,))
        kvg = a32.alloc("kvg", (4,))
        self.DMA(qg[:, 0:3], dr["o_qgT"][i], w=[qg])
        self.DMA(kvg[:, 0:2], dr["o_kvgT"][i], w=[kvg])
        qup32 = a32.alloc("qup32", (3, 768))
        kvup32 = a32.alloc("kvup32", (2, 1024))
        self.DMA(qup32.ap, dr["o_q_up"][i].rearrange("(c p) n -> p c n", p=P), w=[qup32])
        self.DMA(kvup32.ap, dr["o_kv_up"][i].rearrange("(c p) n -> p c n", p=P), w=[kvup32])
        qup = a16.alloc("qup", (3, 768))
        quprot = a16.alloc("quprot", (3, 768))
        kvk = a16.alloc("kvk", (2, 512))
        kvv = a16.alloc("kvv", (2, 512))
        wkrot = a16.alloc("wkrot", (8, 32))
        self.ms("pool", quprot.ap, 0.0, w=[quprot])
        for c in range(3):
            self.ts("dve", qup[:, c, :], qup32[:, c, :], qg[:, c:c + 1], r=[qup32, qg], w=[qup])
            qv = qup[:, c, :].rearrange("p (h r) -> p h r", h=8)
            rv = quprot[:, c, :].rearrange("p (h r) -> p h r", h=8)
            self.ts("pool", rv[:, :, 64:80], qv[:, :, 80:96], -1.0, r=[qup], w=[quprot])
            self.cp("pool", rv[:, :, 80:96], qv[:, :, 64:80], r=[qup], w=[quprot])
        for c in range(2):
            kv3 = kvup32[:, c, :].rearrange("p (h r) -> p h r", h=8)
            self.ts("dve", kvk[:, c, :].rearrange("p (h r) -> p h r", h=8), kv3[:, :, 0:64], kvg[:, c:c + 1],
                    r=[kvup32, kvg], w=[kvk])
            self.ts("dve", kvv[:, c, :].rearrange("p (h r) -> p h r", h=8), kv3[:, :, 64:128], kvg[:, c:c + 1],
                    r=[kvup32, kvg], w=[kvv])
        win = self.load_w_in(dr["o_w_in"][i], O_COLS, layer)
        self.ts("pool", wkrot[:, :, 0:16], win[:, :, O_KR + 16:O_KR + 32], -1.0, r=[win], w=[wkrot])
        self.cp("pool", wkrot[:, :, 16:32], win[:, :, O_KR:O_KR + 16], r=[win], w=[wkrot])
        S.fence()
        mW16 = a16.off
        import os as _os
        dbg = _os.environ.get("KDBG", "FMB")
        for tag, Sq in self.seqs:
            a16.off, a32.off = mW16, mF32
            if "F" in dbg:
                self.odd_fwd(tag, Sq, win, wkrot, qup, quprot, kvk, kvv, ifb, first)
            S.fence()
        for tag, Sq in self.seqs:
            a16.off, a32.off = 0, 0
            if "M" in dbg:
                self.mla_pass(tag, Sq)
            S.fence()
        if "B" not in dbg:
            return
        a16.off, a32.off = 0, 0
        gml = a32.alloc("gml_bc", (P,))
        self.DMA(gml.ap, dr["o_mlg"][i].partition_broadcast(P), w=[gml])
        mB32 = a32.off
        wout = self.load_w_out(dr["o_w_out"][i], 8)
        S.fence()
        mB16 = a16.off
        for tag, Sq in self.seqs:
            a16.off, a32.off = mB16, mB32
            self.odd_bwd(tag, Sq, wout, gml, first, last)
            S.fence()

    def mlstm_alloc(self):
        a16, a32 = self.a16, self.a32
        self.mq = [[a16.alloc(f"mq{b}{k}", (P,)) for k in range(3)] for b in range(2)]
        self.mC = a32.alloc("mC", (4, 130), nsub=4)
        self.mCb = a16.alloc("mCb", (4, 130), nsub=4)
        self.malpha = a32.alloc("malpha", (4,), nsub=4)
        self.mden = [a32.alloc(f"mden{b}", (2,)) for b in range(2)]
        self.mps = []
        for b in range(2):
            self.mps.append(dict(sT=self.ps(3 + 2 * b, 0, P, f"sT{b}"), nd=self.ps(4 + 2 * b, 0, 129, f"nd{b}"),
                                 Pm=self.ps(4 + 2 * b, 256, 129, f"mPm{b}")))
        self.shl = a16.alloc("shl", (2, 8))
        self.ones_bf = a16.alloc("ones_bf", (P,))
        self.ms("pool", self.ones_bf.ap, 1.0, w=[self.ones_bf])
        self.ms("dve", self.mC.ap, 0.0, w=[self.mC])
        self.ms("pool", self.mCb.ap, 0.0, w=[self.mCb])
        self.ms("dve", self.malpha.ap, 1.0, w=[self.malpha])

    def mlstm_step(self, kk, h, w, db, anext, qT, kT, ktok, vx, bwd, post):
        sTm, kw, qa = self.mq[kk]
        g = self.mps[kk]
        den = self.mden[kk]
        cbf = self.c_bf
        MSK = cbf[:, 3, :] if bwd else cbf[:, 1, :]
        al = self.malpha[:, h:h + 1]
        alb = self.malpha.bs[h]
        Cf, Cb = self.mC.bs[h], self.mCb.bs[h]
        self.mm(g["sT"].ap, kT[0], qT[0], r=[kT[1], qT[1]], w=[g["sT"]])
        self.ts("dve", kw.ap, ktok[0], w[0], r=[ktok[1], w[1]], w=[kw])
        self.ts("dve", qa.ap, qT[0], al, r=[qT[1], alb], w=[qa])
        yield
        self.stt(sTm.ap, g["sT"].ap, w[0], MSK, ALU.mult, ALU.mult, r=[g["sT"], w[1], cbf], w=[sTm])
        yield
        self.mm(g["nd"].ap, sTm.ap, vx[0], start=True, stop=False, r=[sTm, vx[1]], w=[g["nd"]])
        self.mm(g["nd"].ap, qa.ap, self.mCb[:, h, 0:129], start=False, stop=True, r=[qa, Cb], w=[g["nd"]])
        self.mm(g["Pm"].ap, kw.ap, vx[0], r=[kw, vx[1]], w=[g["Pm"]])
        yield
        self.cp("dve", den[:, 1:2], g["nd"][:, 128:129], r=[g["nd"]], w=[den])
        self.stt(self.mC[:, h, 0:129], self.mC[:, h, 0:129], al, g["Pm"].ap, ALU.mult, ALU.add,
                 r=[Cf, alb, g["Pm"]], w=[Cf])
        yield
        self.cp("act", self.mCb[:, h, 0:129], self.mC[:, h, 0:129], r=[Cf], w=[Cb])
        self.stt(den[:, 0:1], den[:, 1:2], -1.0, den[:, 1:2], ALU.mult, ALU.max, r=[den], w=[den])
        self.tt("dve", den[:, 0:1], den[:, 0:1], db[0], ALU.max, r=[den, db[1]], w=[den])
        self.A("dve", lambda e, o=den[:, 1:2], i_=den[:, 0:1]: e.reciprocal(out=o, in_=i_), r=[den], w=[den])
        self.cp("pool", al, anext[0], r=[anext[1]], w=[alb])
        yield
        yield from post(g["nd"], den)

    def odd_fwd(self, tag, Sq, win, wkrot, qup, quprot, kvk, kvv, ifb, first):
        a16, a32, dr, TB, NT = self.a16, self.a32, self.dr, self.TB, self.NT
        xsrc = dr[f"x_{tag}"] if first else dr[f"y_{tag}"]
        cbf = self.c_bf
        self.alloc_xh()
        self.mlstm_alloc()
        cq = {n: a16.alloc(f"cq{n}", (3, TB)) for n in ("T", "sq", "n")}
        ckv = {n: a16.alloc(f"ckv{n}", (2, TB)) for n in ("T", "sq", "n")}
        fmg = a16.alloc("fm_mgT", (4, TB), nsub=4)
        fKn = a16.alloc("fm_Kn", (4, TB), nsub=4)
        fmq = [a16.alloc(f"fm_mqT{b}", (4, TB), nsub=4) for b in range(2)]
        fmk = [a16.alloc(f"fm_mkT{b}", (4, TB), nsub=4) for b in range(2)]
        Qst = a16.alloc("Qst", (8, TB), nsub=8)
        KRst = a16.alloc("KRst", (TB,))
        rhl = a16.alloc("rhl", (2, TB))
        NB4 = 2 * NT
        tmk = [a16.alloc(f"tmk{b}", (512,)) for b in range(NB4)]
        tmv = [a16.alloc(f"tmv{b}", (4, 130)) for b in range(NB4)]
        wt = [a32.alloc(f"wt{b}", (24,)) for b in range(NB4)]
        tmog = [a16.alloc(f"tmog{b}", (512,)) for b in range(2)]
        tV = [a16.alloc(f"tV{b}", (512,)) for b in range(2)]
        thf = [a32.alloc(f"thf{b}", (512,), nsub=4) for b in range(2)]
        rrow = {n: a32.alloc(f"rrow{n}", (TB,)) for n in ("q", "kv")}
        rbc = {n: a32.alloc(f"rbc{n}", (TB,)) for n in ("q", "kv")}
        cosT = a32.alloc("cosT", (TB,))
        sinT = a32.alloc("sinT", (TB,))
        rt1 = a32.alloc("rt1", (TB,))
        rt2 = a32.alloc("rt2", (TB,))
        sig = a32.alloc("sig", (512,))
        sil = a32.alloc("sil", (512,))
        G = [a32.alloc(f"G{b}", (16,)) for b in range(2)]
        spm = [a32.alloc(f"spm{b}", (8,)) for b in range(2)]
        gbs = [a32.alloc(f"gbs{b}", (16,)) for b in range(2)]
        ps3 = [self.ps(k, 0, 512, f"ps3_{k}") for k in range(3)]
        for b in range(NB4):
            self.ms("pool", tmv[b][:, :, 128:130], 1.0, w=[tmv[b]])
        for b in range(2):
            self.ms("pool", gbs[b].ap, 0.0, w=[gbs[b]])
        ones_col = cbf[:, 1, P - 1:P]
        nb = Sq // TB
        st = dict(pc=0)

        def nps():
            p = ps3[st["pc"] % 3]
            st["pc"] += 1
            return p

        def sub(pf, p1, width, name="pfs"):
            t = Tile(pf[0:p1, 0:width], name)
            t.bs = pf.bs
            return t

        def Pgen(bi):
            t0 = bi * TB
            bp = bi % 2
            hT = self.hT[bi % 2]
            yield from self.x_to_hT(xsrc, t0, bi)
            for rows in ((0, 32), (64, 96)):
                self.DMA(cosT[rows[0]:rows[1], :], dr["ropeT"][0, :, t0:t0 + TB], w=[cosT])
                self.DMA(sinT[rows[0]:rows[1], :], dr["ropeT"][1, :, t0:t0 + TB], w=[sinT])
            for (nm, col, nch, dim, T3) in (("q", O_CQ, 3, 384, cq), ("kv", O_CKV, 2, 256, ckv)):
                for c in range(nch):
                    pst = sub(nps(), P, TB)
                    self.proj_fm(win, hT, col + c * P, P, pst)
                    self.cp("dve", T3["T"][:, c, :], pst.ap, r=[pst], w=[T3["T"]])
                    self.tt("pool", T3["sq"][:, c, :], T3["T"][:, c, :], T3["T"][:, c, :], ALU.mult, r=[T3["T"]],
                            w=[T3["sq"]])
                    yield
                pst = nps()
                for c in range(nch):
                    self.mm(pst[0:1, 0:TB], ones_col, T3["sq"][:, c, :], start=(c == 0), stop=(c == nch - 1),
                            r=[T3["sq"], cbf], w=[pst])
                rr = rrow[nm]
                self.cp("dve", rr[0:1, :], pst[0:1, 0:TB], r=[pst], w=[rr])
                self.rsqrt_(rr, dim, 0, 1)
                self.cp("dve", rhl[0:1, 0, :], rr[0:1, :], r=[rr], w=[rhl])
                self.tt("dve", rhl[0:1, 1, :], rr[0:1, :], rhl[0:1, 0, :], ALU.subtract, r=[rr, rhl], w=[rhl])
                yield
                pst = nps()
                for k2 in range(2):
                    self.mm(pst[:, 0:TB], cbf[0:1, 1, :], rhl[0:1, k2, :], start=(k2 == 0), stop=(k2 == 1),
                            r=[cbf, rhl], w=[pst])
                self.cp("act", rbc[nm].ap, pst[:, 0:TB], r=[pst], w=[rbc[nm]])
                yield
                for c in range(nch):
                    self.tt("pool", T3["n"][:, c, :], T3["T"][:, c, :], rbc[nm].ap, ALU.mult, r=[T3["T"], rbc[nm]],
                            w=[T3["n"]])
                yield
            pa = nps()
            pb = nps()
            self.proj_fm(win, hT, O_KR, 32, sub(pa, 32, TB))
            for kc in range(8):
                self.mm(pb[0:32, 0:TB], wkrot[:, kc, :], hT[:, kc, :], start=(kc == 0), stop=(kc == 7),
                        r=[wkrot, hT], w=[pb])
            self.tt("dve", rt1[0:32, :], pa[0:32, 0:TB], cosT[0:32, :], ALU.mult, r=[pa, cosT], w=[rt1])
            self.tt("dve", rt2[0:32, :], pb[0:32, 0:TB], sinT[0:32, :], ALU.mult, r=[pb, sinT], w=[rt2])
            self.tt("pool", KRst[0:32, :], rt1[0:32, :], rt2[0:32, :], ALU.add, r=[rt1, rt2], w=[KRst])
            self.DMA(dr[f"KR_{tag}"][:, t0:t0 + TB], KRst[0:32, :], r=[KRst])
            yield
            for (dst, col, mode) in ((fmg, O_MG, "silu"), (fmq[bp], O_MQ, "copy"), (fmk[bp], O_MK, "scale")):
                for h in range(4):
                    pst = sub(nps(), P, TB)
                    self.proj_fm(win, hT, col + h * P, P, pst)
                    if mode == "silu":
                        self.silu_ps(dst[:, h, :], [dst.bs[h]], pst, sig if h % 2 == 0 else sil, TB)
                    elif mode == "copy":
                        self.cp("dve", dst[:, h, :], pst.ap, r=[pst], w=[dst.bs[h]])
                    else:
                        self.A("act", lambda e, o=dst[:, h, :], p=pst.ap: e.mul(out=o, in_=p, mul=float(P) ** -0.5),
                               r=[pst], w=[dst.bs[h]])
                    yield
            for h in range(8):
                pa = nps()
                pb = nps()
                for c in range(3):
                    self.mm(pa[0:96, 0:TB], qup[:, c, h * 96:(h + 1) * 96], cq["n"][:, c, :], start=(c == 0),
                            stop=(c == 2), r=[qup, cq["n"]], w=[pa])
                for c in range(3):
                    self.mm(pb[0:96, 0:TB], quprot[:, c, h * 96:(h + 1) * 96], cq["n"][:, c, :], start=(c == 0),
                            stop=(c == 2), r=[quprot, cq["n"]], w=[pb])
                self.cp("dve", Qst[0:64, h, :], pa[0:64, 0:TB], r=[pa], w=[Qst.bs[h]])
                self.tt("dve", rt1[64:96, :], pa[64:96, 0:TB], cosT[64:96, :], ALU.mult, r=[pa, cosT], w=[rt1])
                self.tt("dve", rt2[64:96, :], pb[64:96, 0:TB], sinT[64:96, :], ALU.mult, r=[pb, sinT], w=[rt2])
                self.tt("pool", Qst[64:96, h, :], rt1[64:96, :], rt2[64:96, :], ALU.add, r=[rt1, rt2], w=[Qst.bs[h]])
                yield
            for pr in range(4):
                pst = nps()
                for c in range(2):
                    self.mm(pst[:, 0:TB], kvk[:, c, pr * P:(pr + 1) * P], ckv["n"][:, c, :], start=(c == 0), stop=(c == 1),
                            r=[kvk, ckv["n"]], w=[pst])
                self.cp("dve", fKn[:, pr, :], pst[:, 0:TB], r=[pst], w=[fKn.bs[pr]])
                yield
            self.DMA(dr[f"Q_{tag}"][:, :, t0:t0 + TB].rearrange("h r t -> r h t"), Qst[0:96], r=[Qst])
            for nm, src in (("Kn", fKn), ("mgT", fmg), ("mqT", fmq[bp]), ("mkT", fmk[bp])):
                self.DMA(dr[f"{nm}_{tag}"][:, :, t0:t0 + TB].rearrange("h p t -> p h t"), src.ap, r=[src])
            for j in range(NT):
                t = t0 + j * P
                b4 = (bi * NT + j) % NB4
                b2 = (bi * NT + j) % 2
                mk_, mv_, wt_ = tmk[b4], tmv[b4], wt[b4]
                mog_, V_, G_, spm_, gbs_ = tmog[b2], tV[b2], G[b2], spm[b2], gbs[b2]
                pst = nps()
                self.proj_tm(win, hT, j, O_MK, 512, pst)
                self.A("act", lambda e, o=mk_.ap, p=pst.ap: e.mul(out=o, in_=p, mul=float(P) ** -0.5), r=[pst], w=[mk_])
                yield
                pst = nps()
                self.proj_tm(win, hT, j, O_MV, 512, pst)
                self.cp("dve", mv_[:, :, 0:128], pst.ap.rearrange("p (h d) -> p h d", h=4), r=[pst], w=[mv_])
                yield
                pst = nps()
                self.proj_tm(win, hT, j, O_MO, 512, pst)
                self.act(sig.ap, pst.ap, AF.Exp, r=[pst], w=[sig], scale=-1.0)
                yield
                pst = nps()
                self.proj_tm(win, hT, j, O_MLG, 512, pst)
                self.act(sil.ap, pst.ap, AF.Exp, r=[pst], w=[sil], scale=-1.0)
                self.ts("dve", sil.ap, sil.ap, 1.0, op0=ALU.add, r=[sil], w=[sil])
                self.stt(sig.ap, sig.ap, 1.0, sil.ap, ALU.add, ALU.mult, r=[sig, sil], w=[sig])
                self.A("dve", lambda en, o=sig.ap, i_=sig.ap: en.reciprocal(out=o, in_=i_), r=[sig], w=[sig])
                self.tt("dve", mog_.ap, pst.ap, sig.ap, ALU.mult, r=[pst, sig], w=[mog_])
                yield
                pst = nps()
                for c in range(2):
                    self.mm(pst.ap, ckv["n"][:, c, j * P:(j + 1) * P], kvv[:, c, :], start=(c == 0), stop=(c == 1),
                            r=[ckv["n"], kvv], w=[pst])
                self.cp("dve", V_.ap, pst.ap, r=[pst], w=[V_])
                yield
                pst = nps()
                p16 = Tile(pst[:, 0:16], "p16")
                p16.bs = pst.bs
                self.proj_tm(win, hT, j, O_IF, 16, p16)
                self.tt("dve", G_.ap, pst[:, 0:16], ifb.ap, ALU.add, r=[pst, ifb], w=[G_])
                yield
                self.gates(G_, spm_, wt_, nps())
                self.cp("pool", gbs_[:, 0:4], wt_[:, 4:8], r=[wt_], w=[gbs_])
                self.cp("pool", gbs_[:, 4:8], wt_[:, 12:16], r=[wt_], w=[gbs_])
                self.cp("pool", gbs_[:, 8:12], wt_[:, 20:24], r=[wt_], w=[gbs_])
                yield
                self.DMA(dr[f"mk_{tag}"][t:t + P, :], mk_.ap, r=[mk_])
                self.DMA(dr[f"mv_{tag}"][t:t + P, :].rearrange("p (h d) -> p h d", h=4), mv_[:, :, 0:128], r=[mv_])
                self.DMA(dr[f"mog_{tag}"][t:t + P, :], mog_.ap, r=[mog_])
                self.DMA(dr[f"V_{tag}"][t:t + P, :], V_.ap, r=[V_])
                self.DMA(dr[f"gb_{tag}"][t:t + P, :], gbs_.ap, r=[gbs_])

        def Sgen(bi, par):
            t0 = bi * TB
            bp = bi % 2
            for j in range(NT):
                t = t0 + j * P
                b4 = (bi * NT + j) % NB4
                b2 = (bi * NT + j) % 2
                mk_, mv_, wt_, hf_ = tmk[b4], tmv[b4], wt[b4], thf[b2]
                for h in (par, par + 2):
                    def post(nd, den, h=h, hf_=hf_, t=t):
                        self.ts("dve", hf_[:, h * P:(h + 1) * P], nd[:, 0:P], den[:, 1:2], r=[nd, den], w=[hf_.bs[h]])
                        self.DMA(dr[f"hf_{tag}"][t:t + P, h * P:(h + 1) * P], hf_[:, h * P:(h + 1) * P], r=[hf_.bs[h]])
                        yield
                    yield from self.mlstm_step(par, h, (wt_[:, h:h + 1], wt_), (wt_[:, 8 + h:9 + h], wt_),
                                               (wt_[:, 16 + h:17 + h], wt_),
                                               (fmq[bp][:, h, j * P:(j + 1) * P], fmq[bp].bs[h]),
                                               (fmk[bp][:, h, j * P:(j + 1) * P], fmk[bp].bs[h]),
                                               (mk_[:, h * P:(h + 1) * P], mk_),
                                               (mv_[:, h, 0:129], mv_), False, post)

        for bi in range(nb + 1):
            streams = []
            if bi >= 1:
                streams += [Sgen(bi - 1, 0), Sgen(bi - 1, 1)]
            if bi < nb:
                streams.append(Pgen(bi))
            interleave(streams)

    def gates(self, G_, spm_, wt_, gp):
        cf32 = self.c_f32
        self.act(spm_.ap, G_[:, 8:16], AF.Exp, r=[G_], w=[spm_], scale=-1.0)
        self.act(spm_.ap, spm_.ap, AF.Ln, r=[spm_], w=[spm_], bias=1.0)
        cbf = self.c_bf
        shl = self.shl
        self.cp("dve", shl[:, 0, :], spm_.ap, r=[spm_], w=[shl])
        self.tt("dve", shl[:, 1, :], spm_.ap, shl[:, 0, :], ALU.subtract, r=[spm_, shl], w=[shl])
        for k2 in range(2):
            self.mm(gp[:, 0:4], cbf[:, 1, :], shl[:, k2, 0:4], start=(k2 == 0), stop=(k2 == 1), r=[cbf, shl], w=[gp])
        for k2 in range(2):
            self.mm(gp[:, 4:8], cbf[:, 2, :], shl[:, k2, 4:8], start=(k2 == 0), stop=(k2 == 1), r=[cbf, shl], w=[gp])
        for k2 in range(2):
            self.mm(gp[:, 8:16], self.ones_bf.ap, shl[:, k2, 0:8], start=(k2 == 0), stop=(k2 == 1),
                    r=[self.ones_bf, shl], w=[gp])
        self.cp("dve", wt_[:, 8:24], gp[:, 0:16], r=[gp], w=[wt_])
        self.tt("dve", wt_[:, 0:8], wt_[:, 8:16], G_[:, 0:8], ALU.add, r=[wt_, G_], w=[wt_])
        self.act(wt_[:, 0:16], wt_[:, 0:16], AF.Exp, r=[wt_], w=[wt_])
        self.act(wt_[:, 16:24], wt_[:, 16:24], AF.Exp, r=[wt_], w=[wt_], scale=-1.0)

    def mla_pass(self, tag, Sq):
        a16, a32, dr = self.a16, self.a32, self.dr
        cf32 = self.c_f32
        NK = Sq // P
        QB = 512
        KT = [a16.alloc(f"KT{b}", (Sq,)) for b in range(2)]
        QT = [a16.alloc(f"QT{b}", (Sq,)) for b in range(2)]
        Vh = [a16.alloc(f"Vh{b}", (NK, 66)) for b in range(2)]
        PT = [a16.alloc(f"PT{b}", (1024,)) for b in range(3)]
        ATs = [a16.alloc(f"ATs{b}", (QB,)) for b in range(2)]
        rhl = a16.alloc("mrhl", (2, QB))
        rrow = a32.alloc("mrrow", (QB,))
        bcs = a32.alloc("mbcs", (QB,))
        sT = []
        for g in range(2):
            t = Tile(self.psall[:, g * 1024:(g + 1) * 1024], f"msT{g}")
            t.bs = [self.bankbuf[2 * g], self.bankbuf[2 * g + 1]]
            sT.append(t)
        oT = [self.ps(4, 0, QB, "oT0"), self.ps(5, 0, QB, "oT1")]
        bcp = self.ps(6, 0, QB, "bcp")
        for b in range(2):
            self.ms("pool", Vh[b][:, :, 64:66], 1.0, w=[Vh[b]])
        scale = 96.0 ** -0.5
        qcnt = 0
        gcnt = 0
        pcnt = 0
        ngr = NK // 2
        for h in range(8):
            kb = h % 2
            K_, Q_, V_ = KT[kb], QT[kb], Vh[kb]
            r0 = (h % 2) * 64
            self.DMA(K_[0:64, :], dr[f"Kn_{tag}"][h // 2, r0:r0 + 64, :], w=[K_])
            self.DMA(K_[64:96, :], dr[f"KR_{tag}"][:, :], w=[K_])
            self.DMA(Q_[0:96, :], dr[f"Q_{tag}"][h], w=[Q_])
            self.DMA(V_[:, :, 0:64], dr[f"V_{tag}"][:, h * 64:(h + 1) * 64].rearrange("(n p) c -> p n c", p=P), w=[V_])
            for qb in range(Sq // QB):
                o_ = oT[qcnt % 2]
                at_ = ATs[qcnt % 2]
                qcnt += 1
                qs = Q_[0:96, qb * QB:(qb + 1) * QB]

                def qk(kg, sg):
                    for n in range(2):
                        kt = 2 * kg + n
                        self.mm(sg[:, n * 512:(n + 1) * 512], K_[0:96, kt * P:(kt + 1) * P], qs, r=[K_, Q_], w=[sg])
                def pv(kg, pt):
                    for n in range(2):
                        kt = 2 * kg + n
                        self.mm(o_[0:65, :], V_[:, kt, 0:65], pt[:, n * 512:(n + 1) * 512], start=(kt == 0),
                                stop=(kt == NK - 1), r=[V_, pt], w=[o_])
                qk(0, sT[gcnt % 2])
                prev = None
                for kg in range(ngr):
                    sg = sT[gcnt % 2]
                    gcnt += 1
                    pt = PT[pcnt % 3]
                    pcnt += 1
                    if kg + 1 < ngr:
                        qk(kg + 1, sT[gcnt % 2])
                    self.act(pt.ap, sg.ap, AF.Exp, r=[sg], w=[pt], scale=scale)
                    if prev is not None:
                        pv(*prev)
                    prev = (kg, pt)
                pv(*prev)
                self.A("dve", lambda e, o=rrow[64:65, :], i_=o_[64:65, :]: e.reciprocal(out=o, in_=i_), r=[o_], w=[rrow])
                self.cp("dve", rhl[64:65, 0, :], rrow[64:65, :], r=[rrow], w=[rhl])
                self.tt("dve", rhl[64:65, 1, :], rrow[64:65, :], rhl[64:65, 0, :], ALU.subtract, r=[rrow, rhl], w=[rhl])
                for k2 in range(2):
                    self.mm(bcp[0:64, :], self.c_bf[64:65, 2, 0:64], rhl[64:65, k2, :], start=(k2 == 0), stop=(k2 == 1),
                            r=[self.c_bf, rhl], w=[bcp])
                self.cp("act", bcs[0:64, :], bcp[0:64, :], r=[bcp], w=[bcs])
                self.tt("dve", at_[0:64, :], o_[0:64, :], bcs[0:64, :], ALU.mult, r=[o_, bcs], w=[at_])
                self.DMA(dr[f"AT_{tag}"][h // 2, r0:r0 + 64, qb * QB:(qb + 1) * QB], at_[0:64, :], r=[at_])

    def odd_bwd(self, tag, Sq, wout, gml, first, last):
        a16, a32, dr, TB, NT = self.a16, self.a32, self.dr, self.TB, self.NT
        xsrc = dr[f"x_{tag}"] if first else dr[f"y_{tag}"]
        ydst = dr[f"y_{tag}"]
        self.mlstm_alloc()
        self.alloc_resid()
        fmq = [a16.alloc(f"bmq{b}", (4, TB), nsub=4) for b in range(2)]
        fmk = [a16.alloc(f"bmk{b}", (4, TB), nsub=4) for b in range(2)]
        fmg = [a16.alloc(f"bmg{b}", (4, TB)) for b in range(2)]
        fat = [a16.alloc(f"bat{b}", (4, TB)) for b in range(2)]
        tmk = [a16.alloc(f"tmk{b}", (512,)) for b in range(2)]
        tmv = [a16.alloc(f"tmv{b}", (4, 130)) for b in range(2)]
        tmog = [a16.alloc(f"tmog{b}", (512,)) for b in range(2)]
        tgb = [a32.alloc(f"tgb{b}", (16,)) for b in range(2)]
        thf = [a32.alloc(f"thf{b}", (512,)) for b in range(2)]
        hsum = [a32.alloc(f"hsum{b}", (P,)) for b in range(2)]
        hsq = [a16.alloc(f"hsq{b}", (P,)) for b in range(2)]
        hss = [a32.alloc(f"hss{b}", (1,)) for b in range(2)]
        otmp = [a32.alloc(f"otmp{b}", (P,)) for b in range(2)]
        mlo = [a16.alloc(f"mlo{b}", (512,), nsub=4) for b in range(2)]
        featT = [a16.alloc(f"featT{b}", (8, TB), nsub=8) for b in range(2)]
        psA, psB = self.ps(1, 0, 512, "psA"), self.ps(2, 0, 512, "psB")
        for b in range(2):
            self.ms("pool", tmv[b][:, :, 128:130], 1.0, w=[tmv[b]])
        nb = Sq // TB
        tiles = [(bi, j) for bi in range(nb - 1, -1, -1) for j in range(NT - 1, -1, -1)]

        def Lgen(bi):
            t0 = bi * TB
            bb = bi % 2
            q_, k_f, mg_, at_, ft = fmq[bb], fmk[bb], fmg[bb], fat[bb], featT[bb]
            for (dst, nm) in ((q_, "mqT"), (k_f, "mkT"), (mg_, "mgT"), (at_, "AT")):
                self.DMA(dst.ap, dr[f"{nm}_{tag}"][:, :, t0:t0 + TB].rearrange("h p t -> p h t"), w=[dst])
            yield
            self.tt("pool", ft[:, 0:4, :], at_.ap, mg_.ap, ALU.mult, r=[at_, mg_], w=ft.bs[0:4])
            yield

        def TLgen(n):
            bi, j = tiles[n]
            t = bi * TB + j * P
            tb = n % 2
            self.DMA(tmk[tb].ap, dr[f"mk_{tag}"][t:t + P, :], w=[tmk[tb]])
            self.DMA(tmv[tb][:, :, 0:128], dr[f"mv_{tag}"][t:t + P, :].rearrange("p (h d) -> p h d", h=4), w=[tmv[tb]])
            self.DMA(tmog[tb].ap, dr[f"mog_{tag}"][t:t + P, :], w=[tmog[tb]])
            self.DMA(tgb[tb].ap, dr[f"gb_{tag}"][t:t + P, :], w=[tgb[tb]])
            self.DMA(thf[tb].ap, dr[f"hf_{tag}"][t:t + P, :], w=[thf[tb]])
            yield

        def Sgen(n, par):
            bi, j = tiles[n]
            bb = bi % 2
            tb = n % 2
            q_, k_f = fmq[bb], fmk[bb]
            mk_, mv_, mog_, gb_, hf_, ml = tmk[tb], tmv[tb], tmog[tb], tgb[tb], thf[tb], mlo[tb]
            for h in (par, par + 2):
                def post(nd, den, h=h):
                    hs_, ss_, ot_, hq_ = hsum[par], hss[par], otmp[par], hsq[par]
                    self.stt(hs_.ap, nd[:, 0:P], den[:, 1:2], hf_[:, h * P:(h + 1) * P], ALU.mult, ALU.add,
                             r=[nd, den, hf_], w=[hs_])
                    yield
                    self.act(hq_.ap, hs_.ap, AF.Square, r=[hs_], w=[hq_, ss_], accum=ss_.ap)
                    self.rsqrt_(ss_, P)
                    yield
                    self.stt(ot_.ap, hs_.ap, ss_[:, 0:1], gml.ap, ALU.mult, ALU.mult, r=[hs_, ss_, gml], w=[ot_])
                    yield
                    self.tt("pool", ml[:, h * P:(h + 1) * P], ot_.ap, mog_[:, h * P:(h + 1) * P], ALU.mult,
                            r=[ot_, mog_], w=[ml.bs[h]])
                    yield
                yield from self.mlstm_step(par, h, (gb_[:, h:h + 1], gb_), (gb_[:, 4 + h:5 + h], gb_),
                                           (gb_[:, 8 + h:9 + h], gb_),
                                           (q_[:, h, j * P:(j + 1) * P], q_.bs[h]),
                                           (k_f[:, h, j * P:(j + 1) * P], k_f.bs[h]),
                                           (mk_[:, h * P:(h + 1) * P], mk_),
                                           (mv_[:, h, 0:129], mv_), True, post)

        def Rgen(n):
            bi, j = tiles[n]
            bb = bi % 2
            tb = n % 2
            ft, ml = featT[bb], mlo[tb]
            t = bi * TB + j * P
            for c in range(4):
                self.tr(self.psT[:, c * P:(c + 1) * P], ml[:, c * P:(c + 1) * P], r=[ml], w=[self.psT])
            self.cp("act", ft[:, 4:8, j * P:(j + 1) * P],
                    self.psT[:, 0:512].rearrange("p (c t) -> p c t", c=4), r=[self.psT], w=ft.bs[4:8])
            yield
            yield from self.resid_out(ft, 8, wout, j, xsrc, ydst, t, last, psA, psB)

        interleave([Lgen(tiles[0][0]), TLgen(0)])
        for n in range(len(tiles) + 1):
            streams = []
            if n < len(tiles):
                streams += [Sgen(n, 0), Sgen(n, 1)]
                if n + 1 < len(tiles):
                    streams.append(TLgen(n + 1))
                    if tiles[n + 1][0] != tiles[n][0]:
                        streams.append(Lgen(tiles[n + 1][0]))
            if n >= 1:
                streams.append(Rgen(n - 1))
            interleave(streams)


def make_consts(smax):
    j = np.arange(P)[:, None]
    i = np.arange(P)[None, :]
    ident = (j == i)
    TL = (j <= i)
    TG = (j >= i)
    SG = (j > i)
    SL = (j < i)
    c_bf = np.stack([ident, TL, TG, SG, SL], axis=1).astype(np.float32).astype(ml_dtypes.bfloat16)
    c_f32 = np.stack([TL, TG, np.ones((P, P), bool)], axis=1).astype(np.float32)
    inv = (np.float32(10000.0) ** (-np.arange(0, 32, 2, dtype=np.float32) / np.float32(32))).astype(np.float32)
    ang = (np.arange(smax, dtype=np.float32)[:, None] * inv[None, :]).astype(np.float32)
    cos = np.cos(ang).astype(np.float32).T
    sin = np.sin(ang).astype(np.float32).T
    ropeT = np.stack([np.concatenate([cos, cos], 0), np.concatenate([sin, sin], 0)], 0)
    return c_bf, c_f32, np.ascontiguousarray(ropeT)


def make_invcnt(S):
    pos = np.arange(S)
    out = np.zeros((4, S), np.float32)
    for gi, w in enumerate((2, 4, 8, 16)):
        lo = np.clip(pos - w // 2, 0, S)
        hi = np.clip(pos + w // 2, 0, S)
        out[gi] = 1.0 / (hi - lo).astype(np.float32)
    return out


def prep_weights(w):
    f = lambda a: np.ascontiguousarray(np.asarray(a, dtype=np.float32))
    out = {}
    out["norm_gT"] = f(np.asarray(w["norm_g"]).reshape(-1, 8, P).transpose(0, 2, 1))
    out["final_g"] = f(w["final_norm_g"])
    out["e_w_in"] = f(w["e_w_in"])
    out["e_aup"] = f(np.concatenate([np.asarray(w["e_gla_a_up"]), np.asarray(w["e_gla_a_bias"])[:, :, None, :]], axis=2))
    out["e_gn"] = f(w["e_gla_norm_g"])
    out["e_pool_w"] = f(w["e_pool_w"])
    out["e_pscT"] = f(np.asarray(w["e_pool_scale"]).reshape(-1, 4, P).transpose(0, 2, 1))
    out["e_w_out"] = f(w["e_w_out"])
    out["o_w_in"] = f(w["o_w_in"])
    out["o_qgT"] = f(np.asarray(w["o_q_norm_g"]).reshape(-1, 3, P).transpose(0, 2, 1))
    out["o_q_up"] = f(w["o_q_up"])
    out["o_kvgT"] = f(np.asarray(w["o_kv_norm_g"]).reshape(-1, 2, P).transpose(0, 2, 1))
    out["o_kv_up"] = f(w["o_kv_up"])
    out["o_if_bias"] = f(w["o_if_bias"])
    out["o_mlg"] = f(w["o_mlstm_norm_g"])
    out["o_w_out"] = f(w["o_w_out"])
    return out


_CACHE = {}


def run_trunk(seq_inputs, weights, layers=(0, 1, 2, 3), final_norm=True, TB=256, n_cores=8):
    seqs = [(tag, a.shape[0]) for tag, a in seq_inputs[0].items()]
    key = (tuple(seqs), tuple(layers), final_norm, TB)
    if key not in _CACHE:
        b = Builder(seqs, layers, TB, final_norm)
        b.build()
        _CACHE[key] = b
    b = _CACHE[key]
    c_bf, c_f32, ropeT = make_consts(b.smax)
    wp = prep_weights(weights)
    in_maps = []
    for c in range(n_cores):
        m = dict(wp)
        m["c_bf"], m["c_f32"], m["ropeT"] = c_bf, c_f32, ropeT
        for tag, a in seq_inputs[c].items():
            m[f"x_{tag}"] = np.ascontiguousarray(a, dtype=np.float32)
            m[f"invcnt_{tag}"] = make_invcnt(a.shape[0])
        in_maps.append(m)
    res = run_bass_kernel_spmd(b.nc, in_maps, core_ids=list(range(n_cores)))
    return [{tag: r[f"y_{tag}"] for tag, _ in seqs} for r in res.results]


def kernel(x_prompt, x_sample, **weights):
    x_prompt = np.asarray(x_prompt, dtype=np.float32)
    x_sample = np.asarray(x_sample, dtype=np.float32)
    nb_p = x_prompt.shape[0]
    seq_inputs = []
    for c in range(8):
        seq_inputs.append({"s": x_sample[c], "p": x_prompt[c % nb_p]})
    outs = run_trunk(seq_inputs, weights)
    y_sample = np.stack([outs[c]["s"] for c in range(8)], axis=0)
    y_prompt = np.stack([outs[c]["p"] for c in range(nb_p)], axis=0)
    return (y_prompt, y_sample)
```

```python
import numpy as np
import ml_dtypes
from contextlib import ExitStack
import concourse.bass as bass
import concourse.mybir as mybir
from concourse.bass_utils import run_bass_kernel_spmd

F32 = mybir.dt.float32
BF16 = mybir.dt.bfloat16
ALU = mybir.AluOpType
AF = mybir.ActivationFunctionType

ENGS = ("pe", "act", "dve", "pool", "sp")
SEM_WRAP = 30000
P = 128
DM = 1024
EPS = 1e-6


class Buf:
    __slots__ = ("name", "writers", "readers")

    def __init__(self, name):
        self.name = name
        self.writers = []
        self.readers = []


class Instr:
    __slots__ = ("eng", "fn", "deps", "signal", "count", "semidx", "is_dma", "dma_slot", "dma_target", "dma_prev")

    def __init__(self, eng, fn, is_dma):
        self.eng = eng
        self.fn = fn
        self.deps = []
        self.signal = False
        self.count = 0
        self.semidx = 0
        self.is_dma = is_dma
        self.dma_slot = None
        self.dma_target = 0
        self.dma_prev = None


class Sched:
    def __init__(self, same_engine_sync=True, dma_slots=8):
        self.lists = {e: [] for e in ENGS}
        self.same_engine_sync = same_engine_sync
        self.dma_slots = dma_slots
        self.dma_count = {e: 0 for e in ENGS}
        self.dma_hist = {e: [] for e in ENGS}
        self.last_compute = {e: None for e in ENGS}
        self.cur_fence = None

    def add(self, eng, fn, reads=(), writes=(), dma=False):
        I = Instr(eng, fn, dma)
        deps = {}
        for b in reads:
            for w in b.writers:
                deps[id(w)] = w
        for b in writes:
            for w in b.writers:
                deps[id(w)] = w
            for r in b.readers:
                deps[id(r)] = r
        if self.cur_fence is not None:
            deps[id(self.cur_fence)] = self.cur_fence
        for d in deps.values():
            if (not d.is_dma) and d.eng == eng and not dma:
                if eng == "pe" or eng == "sp" or not self.same_engine_sync:
                    continue
            if not d.is_dma:
                d.signal = True
            I.deps.append(d)
        if dma:
            n = self.dma_count[eng]
            self.dma_count[eng] = n + 1
            I.dma_slot = n % self.dma_slots
            I.dma_target = 16 * (n // self.dma_slots + 1)
            if n >= self.dma_slots:
                I.dma_prev = self.dma_hist[eng][n - self.dma_slots]
            self.dma_hist[eng].append(I)
        else:
            self.last_compute[eng] = I
        for b in reads:
            if dma:
                b.readers.append(I)
            else:
                b.readers = [r for r in b.readers if r.is_dma or r.eng != eng]
                b.readers.append(I)
        for b in writes:
            b.writers = [I]
            b.readers = []
        self.lists[eng].append(I)
        return I

    def fence(self):
        I = Instr("sp", lambda e: e.nop(), False)
        for e in ENGS:
            lc = self.last_compute[e]
            if lc is not None and e != "sp":
                lc.signal = True
                I.deps.append(lc)
            h = self.dma_hist[e]
            for d in h[-self.dma_slots:]:
                I.deps.append(d)
        I.signal = True
        self.lists["sp"].append(I)
        self.last_compute["sp"] = I
        self.cur_fence = I

    def emit(self, nc, E):
        nsem = {}
        for e in ENGS:
            c = 0
            si = 0
            for I in self.lists[e]:
                if I.is_dma:
                    continue
                if I.signal:
                    c += 1
                    if c > SEM_WRAP:
                        si += 1
                        c = 1
                    I.count = c
                    I.semidx = si
            nsem[e] = si + 1
        prog = {e: [E(nc.semaphore(f"pg_{e}{i}")) for i in range(nsem[e])] for e in ENGS}
        dsem = {e: [E(nc.semaphore(f"dm_{e}{i}")) for i in range(self.dma_slots)]
                for e in ENGS if self.dma_count[e] > 0}
        block = E(nc.Block())
        engobj = {"pe": block.tensor, "act": block.scalar, "dve": block.vector, "pool": block.gpsimd,
                  "sp": block.sync}
        stats = {e: [0, 0] for e in ENGS}

        def run(e, eng):
            waited = {}
            for I in self.lists[e]:
                need = {}
                for d in I.deps:
                    if d.is_dma:
                        key = ("d", d.eng, d.dma_slot)
                        val = d.dma_target
                    else:
                        key = ("p", d.eng, d.semidx)
                        val = d.count
                    if need.get(key, 0) < val:
                        need[key] = val
                if I.is_dma and I.dma_prev is not None:
                    key = ("d", e, I.dma_slot)
                    val = I.dma_prev.dma_target
                    if need.get(key, 0) < val:
                        need[key] = val
                for key, val in need.items():
                    if waited.get(key, 0) >= val:
                        continue
                    waited[key] = val
                    sem = dsem[key[1]][key[2]] if key[0] == "d" else prog[key[1]][key[2]]
                    eng.wait_ge(sem, val)
                    stats[e][1] += 1
                ins = I.fn(eng)
                stats[e][0] += 1
                if I.is_dma:
                    ins.then_inc(dsem[e][I.dma_slot], 16)
                elif I.signal:
                    ins.then_inc(prog[e][I.semidx], 1)
            if e in dsem:
                n = self.dma_count[e]
                for s in range(min(n, self.dma_slots)):
                    last_n = ((n - 1 - s) // self.dma_slots) * self.dma_slots + s
                    eng.wait_ge(dsem[e][s], 16 * (last_n // self.dma_slots + 1))

        for e in ENGS:
            if not self.lists[e]:
                continue

            def mk(e):
                def f(eng):
                    run(e, eng)
                return f
            engobj[e](mk(e))
        return stats


class Tile:
    def __init__(self, ap, name, nsub=1):
        self.ap = ap
        self.bs = [Buf(f"{name}.{i}") for i in range(nsub)]

    def __getitem__(self, idx):
        return self.ap[idx]

    @property
    def b(self):
        return self.bs[0]


class Arena:
    def __init__(self, tensor, size):
        self.t = tensor
        self.size = size
        self.off = 0
        self.peak = 0

    def alloc(self, name, free, nsub=1):
        n = 1
        for f in free:
            n *= f
        n2 = (n + 3) // 4 * 4
        assert self.off + n2 <= self.size, f"arena overflow {name}: {self.off}+{n2}>{self.size}"
        a = self.t[:, self.off:self.off + n]
        self.off += n2
        self.peak = max(self.peak, self.off)
        if len(free) == 2:
            a = a.rearrange("p (a b) -> p a b", a=free[0])
        elif len(free) == 3:
            a = a.rearrange("p (a b c) -> p a b c", a=free[0], b=free[1])
        return Tile(a, name, nsub)


def interleave(gens):
    gens = [g for g in gens if g is not None]
    while gens:
        for g in list(gens):
            try:
                next(g)
            except StopIteration:
                gens.remove(g)


def _bl(xs):
    out = []
    for x in xs:
        if isinstance(x, Buf):
            out.append(x)
        elif isinstance(x, Tile):
            out.extend(x.bs)
        else:
            out.extend(_bl(x))
    return out


E_Q, E_K, E_V, E_G, E_LR, E_PU, E_PG, E_COLS = 0, 512, 1024, 2048, 3072, 3104, 3616, 4128
O_CQ, O_CKV, O_KR, O_MG, O_MQ, O_MK, O_MV, O_MO, O_IF, O_MLG, O_COLS = (0, 384, 640, 672, 1184, 1696, 2208, 2720,
                                                                      3232, 3248, 3760)
WMAX = 4128
A16_SIZE = 68000
A32_SIZE = 14848


class Builder:
    def __init__(self, seqs, layers=(0, 1, 2, 3), TB=256, final_norm=True, smax=None):
        self.seqs = list(seqs)
        self.layers = list(layers)
        self.TB = TB
        self.NT = TB // P
        self.final_norm = final_norm
        self.smax = smax or max(s for _, s in seqs)
        self.nc = bass.Bass("TRN2", target_bir_lowering=False)
        self.S = Sched()
        self.dr = {}
        self._decl_dram()

    def _in(self, name, shape, dt=F32):
        self.dr[name] = self.nc.dram_tensor(name, list(shape), dt, kind="ExternalInput").ap()

    def _scr(self, name, shape, dt):
        self.dr[name] = self.nc.dram_tensor(name, list(shape), dt, kind="Internal").ap()

    def _decl_dram(self):
        nc = self.nc
        for tag, S in self.seqs:
            self._in(f"x_{tag}", [S, DM])
            self.dr[f"y_{tag}"] = nc.dram_tensor(f"y_{tag}", [S, DM], F32, kind="ExternalOutput").ap()
            self._in(f"invcnt_{tag}", [4, S])
            self._scr(f"qT_{tag}", [4, P, S], BF16)
            self._scr(f"kT_{tag}", [4, P, S], BF16)
            self._scr(f"puT_{tag}", [4, P, S], BF16)
            self._scr(f"pgT_{tag}", [4, P, S], BF16)
            self._scr(f"k_{tag}", [S, 512], BF16)
            self._scr(f"spb_{tag}", [S, 512], BF16)
            self._scr(f"v_{tag}", [S, 1024], BF16)
            self._scr(f"gg_{tag}", [S, 1024], BF16)
            self._scr(f"of_{tag}", [S, 1024], F32)
            self._scr(f"Q_{tag}", [8, 96, S], BF16)
            self._scr(f"Kn_{tag}", [4, P, S], BF16)
            self._scr(f"KR_{tag}", [32, S], BF16)
            self._scr(f"V_{tag}", [S, 512], BF16)
            self._scr(f"AT_{tag}", [4, P, S], BF16)
            self._scr(f"mgT_{tag}", [4, P, S], BF16)
            self._scr(f"mqT_{tag}", [4, P, S], BF16)
            self._scr(f"mkT_{tag}", [4, P, S], BF16)
            self._scr(f"mk_{tag}", [S, 512], BF16)
            self._scr(f"mv_{tag}", [S, 512], BF16)
            self._scr(f"mog_{tag}", [S, 512], BF16)
            self._scr(f"gb_{tag}", [S, 16], F32)
            self._scr(f"hf_{tag}", [S, 512], F32)
        self._in("norm_gT", [4, P, 8])
        self._in("final_g", [DM])
        self._in("e_w_in", [2, DM, E_COLS])
        self._in("e_aup", [2, 2, 17, 512])
        self._in("e_gn", [2, 256])
        self._in("e_pool_w", [2, 4, P, P])
        self._in("e_pscT", [2, P, 4])
        self._in("e_w_out", [2, 1536, DM])
        self._in("o_w_in", [2, DM, O_COLS])
        self._in("o_qgT", [2, P, 3])
        self._in("o_q_up", [2, 384, 768])
        self._in("o_kvgT", [2, P, 2])
        self._in("o_kv_up", [2, 256, 1024])
        self._in("o_if_bias", [2, 16])
        self._in("o_mlg", [2, P])
        self._in("o_w_out", [2, 1024, DM])
        self._in("c_bf", [P, 5, P], BF16)
        self._in("c_f32", [P, 3, P], F32)
        self._in("ropeT", [2, 32, self.smax], F32)

    def A(self, eng, fn, r=(), w=()):
        return self.S.add(eng, fn, reads=_bl(r), writes=_bl(w))

    def DMA(self, out, in_, r=(), w=(), q="sp"):
        return self.S.add(q, lambda e: e.dma_start(out=out, in_=in_), reads=_bl(r), writes=_bl(w), dma=True)

    def mm(self, out, lhsT, rhs, start=True, stop=True, r=(), w=()):
        return self.A("pe", lambda e: e.matmul(out, lhsT=lhsT, rhs=rhs, start=start, stop=stop), r, w)

    def tr(self, out, in_, r=(), w=()):
        idt = self.c_bf[:, 0, :]
        return self.A("pe", lambda e: e.transpose(out=out, in_=in_, identity=idt), list(r) + [self.c_bf], w)

    def act(self, out, in_, func, r=(), w=(), scale=1.0, bias=0.0, accum=None):
        if accum is None:
            return self.A("act", lambda e: e.activation(out=out, in_=in_, func=func, bias=bias, scale=scale), r, w)
        return self.A("act", lambda e: e.activation(out=out, in_=in_, func=func, bias=bias, scale=scale,
                                                    accum_out=accum), r, w)

    def ts(self, eng, out, in0, s1, s2=None, op0=ALU.mult, op1=None, r=(), w=()):
        if op1 is None:
            return self.A(eng, lambda e: e.tensor_scalar(out=out, in0=in0, scalar1=s1, scalar2=None, op0=op0), r, w)
        return self.A(eng, lambda e: e.tensor_scalar(out=out, in0=in0, scalar1=s1, scalar2=s2, op0=op0, op1=op1), r, w)

    def tt(self, eng, out, in0, in1, op, r=(), w=()):
        return self.A(eng, lambda e: e.tensor_tensor(out=out, in0=in0, in1=in1, op=op), r, w)

    def stt(self, out, in0, scalar, in1, op0, op1, r=(), w=()):
        return self.A("dve", lambda e: e.scalar_tensor_tensor(out=out, in0=in0, scalar=scalar, in1=in1, op0=op0,
                                                               op1=op1), r, w)

    def cp(self, eng, out, in_, r=(), w=()):
        if eng == "act":
            return self.A("act", lambda e: e.copy(out=out, in_=in_), r, w)
        return self.A(eng, lambda e: e.tensor_copy(out=out, in_=in_), r, w)

    def ms(self, eng, ap, val, w=()):
        return self.A(eng, lambda e: e.memset(ap, val), (), w)

    def silu_ps(self, dst_ap, dst_bufs, pst, et, W):
        e = et[:, 0:W]
        self.act(e, pst.ap, AF.Exp, r=[pst], w=[et], scale=-1.0)
        self.act(e, e, AF.Ln, r=[et], w=[et], bias=1.0)
        self.act(e, e, AF.Exp, r=[et], w=[et], scale=-1.0)
        self.tt("dve", dst_ap, pst.ap, e, ALU.mult, r=[pst, et], w=dst_bufs)

    def rsqrt_(self, t, n, p0=0, p1=P):
        self.act(t[p0:p1], t[p0:p1], AF.Ln, r=[t], w=[t], scale=1.0 / n, bias=self.eps_t[p0:p1, 0:1])
        self.act(t[p0:p1], t[p0:p1], AF.Exp, r=[t], w=[t], scale=-0.5)

    def build(self):
        nc = self.nc
        with ExitStack() as es:
            E = es.enter_context
            self.a16 = Arena(E(nc.sbuf_tensor("arena16", [P, A16_SIZE], BF16)).ap(), A16_SIZE)
            self.a32 = Arena(E(nc.sbuf_tensor("arena32", [P, A32_SIZE], F32)).ap(), A32_SIZE)
            self.c_bf = Tile(E(nc.sbuf_tensor("sb_c_bf", [P, 5, P], BF16)).ap(), "c_bf")
            self.c_f32 = Tile(E(nc.sbuf_tensor("sb_c_f32", [P, 3, P], F32)).ap(), "c_f32")
            self.eps_t = Tile(E(nc.sbuf_tensor("eps_t", [P, 1], F32)).ap(), "eps")
            self.gcol = Tile(E(nc.sbuf_tensor("gcol", [P, 8], F32)).ap(), "gcol")
            self.gfin = Tile(E(nc.sbuf_tensor("gfin", [P, DM], F32)).ap(), "gfin")
            self.psall = E(nc.psum_tensor("psall", [P, 7 * 512], F32)).ap()
            self.psb = [self.psall[:, i * 512:(i + 1) * 512] for i in range(7)]
            self.psT = Tile(E(nc.psum_tensor("psT", [P, 1024], BF16)).ap(), "psT")
            self.bankbuf = [Buf(f"bank{i}") for i in range(7)]
            self.DMA(self.c_bf.ap, self.dr["c_bf"], w=[self.c_bf])
            self.DMA(self.c_f32.ap, self.dr["c_f32"], w=[self.c_f32])
            self.DMA(self.gfin.ap, self.dr["final_g"].partition_broadcast(P), w=[self.gfin])
            self.ms("dve", self.eps_t.ap, EPS, w=[self.eps_t])
            self.S.fence()
            nl = len(self.layers)
            for li, layer in enumerate(self.layers):
                last = (li == nl - 1)
                first = (li == 0)
                if layer % 2 == 0:
                    self.even_layer(layer, first, last)
                else:
                    self.odd_layer(layer, first, last)
            stats = self.S.emit(nc, E)
            self.stats = stats
        return nc

    def ps(self, bank, off, width, name, parts=P):
        t = Tile(self.psb[bank][0:parts, off:off + width], name)
        t.bs = [self.bankbuf[bank]]
        return t

    def load_w_in(self, dram_w, ncols, layer):
        a16, a32 = self.a16, self.a32
        win = a16.alloc("win", (8, WMAX))
        self.DMA(self.gcol.ap, self.dr["norm_gT"][layer], w=[self.gcol])
        m32 = a32.off
        CH = 1032
        nch = (ncols + CH - 1) // CH
        stg = [a32.alloc(f"wstg{i}", (CH,)) for i in range(3)]
        k = 0
        for kc in range(8):
            for c in range(nch):
                c0 = c * CH
                cw = min(CH, ncols - c0)
                s = stg[k % 3]
                self.DMA(s[:, 0:cw], dram_w[kc * P:(kc + 1) * P, c0:c0 + cw], w=[s])
                eng = ("dve", "pool", "act")[k % 3]
                if eng == "act":
                    self.A("act", lambda e, s=s, kc=kc, c0=c0, cw=cw: e.mul(out=win[:, kc, c0:c0 + cw], in_=s[:, 0:cw],
                                                                        mul=self.gcol[:, kc:kc + 1]),
                           r=[s, self.gcol], w=[win])
                else:
                    self.ts(eng, win[:, kc, c0:c0 + cw], s[:, 0:cw], self.gcol[:, kc:kc + 1], r=[s, self.gcol], w=[win])
                k += 1
        a32.off = m32
        return win

    def load_w_out(self, dram_w, nchunks):
        a16, a32 = self.a16, self.a32
        wout = a16.alloc("wout", (nchunks, DM))
        m32 = a32.off
        stg = [a32.alloc(f"wostg{i}", (DM,)) for i in range(3)]
        for c in range(nchunks):
            s = stg[c % 3]
            self.DMA(s.ap, dram_w[c * P:(c + 1) * P, :], w=[s])
            self.cp(("dve", "pool", "act")[c % 3], wout[:, c, :], s.ap, r=[s], w=[wout])
        a32.off = m32
        return wout

    def alloc_xh(self):
        a16, a32 = self.a16, self.a32
        self.xt = [a32.alloc(f"xt{i}", (DM,)) for i in range(2)]
        self.hn = [a16.alloc(f"hn{i}", (DM,)) for i in range(2)]
        self.sqj = a16.alloc("sqj", (DM,))
        self.ssq = [a32.alloc(f"ssq{i}", (1,)) for i in range(2)]
        self.hT = [a16.alloc(f"hT{i}", (8, self.TB)) for i in range(2)]
        self.xcnt = 0

    def x_to_hT(self, xsrc, t0, bi):
        hT = self.hT[bi % 2]
        for j in range(self.NT):
            k = self.xcnt
            self.xcnt += 1
            xt, hn, ssq = self.xt[k % 2], self.hn[k % 2], self.ssq[k % 2]
            self.DMA(xt.ap, xsrc[t0 + j * P:t0 + (j + 1) * P, :], w=[xt])
            self.act(self.sqj.ap, xt.ap, AF.Square, r=[xt], w=[self.sqj, ssq], accum=ssq.ap)
            self.rsqrt_(ssq, DM)
            self.ts("dve", hn.ap, xt.ap, ssq[:, 0:1], r=[xt, ssq], w=[hn])
            yield
            for c in range(8):
                self.tr(self.psT[:, c * P:(c + 1) * P], hn[:, c * P:(c + 1) * P], r=[hn], w=[self.psT])
            self.cp("act" if j % 2 == 0 else "dve", hT[:, :, j * P:(j + 1) * P],
                    self.psT.ap.rearrange("p (c t) -> p c t", c=8), r=[self.psT], w=[hT])
            yield

    def proj_fm(self, win, hT, col0, m, pst):
        for kc in range(8):
            self.mm(pst.ap, win[:, kc, col0:col0 + m], hT[:, kc, :], start=(kc == 0), stop=(kc == 7),
                    r=[win, hT], w=[pst])

    def proj_tm(self, win, hT, j, col0, n, pst):
        for kc in range(8):
            self.mm(pst.ap, hT[:, kc, j * P:(j + 1) * P], win[:, kc, col0:col0 + n], start=(kc == 0), stop=(kc == 7),
                    r=[win, hT], w=[pst])

    def alloc_resid(self):
        a32 = self.a32
        self.xr = [a32.alloc(f"xr{i}", (DM,)) for i in range(2)]
        self.xo = [a32.alloc(f"xo{i}", (DM,)) for i in range(2)]
        self.fssq = [a32.alloc(f"fssq{i}", (1,)) for i in range(2)]
        self.fsq = self.a16.alloc("fsq", (DM,))
        self.rcnt = 0

    def resid_out(self, featT, nchunks, wout, j, xsrc, ydst, t, last, psA, psB):
        k = self.rcnt
        self.rcnt += 1
        xr, xo, fssq = self.xr[k % 2], self.xo[k % 2], self.fssq[k % 2]
        self.DMA(xr.ap, xsrc[t:t + P, :], w=[xr])
        for n, pst in enumerate((psA, psB)):
            for c in range(nchunks):
                self.mm(pst.ap, featT[:, c, j * P:(j + 1) * P], wout[:, c, n * 512:(n + 1) * 512],
                        start=(c == 0), stop=(c == nchunks - 1), r=[featT, wout], w=[pst])
            self.tt("dve", xo[:, n * 512:(n + 1) * 512], pst.ap, xr[:, n * 512:(n + 1) * 512], ALU.add,
                    r=[pst, xr], w=[xo])
            yield
        if last and self.final_norm:
            self.act(self.fsq.ap, xo.ap, AF.Square, r=[xo], w=[self.fsq, fssq], accum=fssq.ap)
            self.rsqrt_(fssq, DM)
            self.stt(xo.ap, xo.ap, fssq[:, 0:1], self.gfin.ap, ALU.mult, ALU.mult, r=[xo, fssq, self.gfin], w=[xo])
        self.DMA(ydst[t:t + P, :], xo.ap, r=[xo])
        yield

    def even_layer(self, layer, first, last):
        i = layer // 2
        a16, a32 = self.a16, self.a32
        dr = self.dr
        S = self.S
        TB, NT = self.TB, self.NT
        a16.off = 0
        a32.off = 0
        aup32 = a32.alloc("aup32", (2, 512))
        aup = a16.alloc("aup", (2, 512))
        self.DMA(aup32[0:17], dr["e_aup"][i].rearrange("d r c -> r d c"), w=[aup32])
        self.cp("dve", aup[0:17], aup32[0:17], r=[aup32], w=[aup])
        win = self.load_w_in(dr["e_w_in"][i], E_COLS, layer)
        S.fence()
        mW16, mW32 = a16.off, 0
        for tag, Sq in self.seqs:
            a16.off, a32.off = mW16, mW32
            self.even_fwd(tag, Sq, win, aup, first)
            S.fence()
        a16.off, a32.off = 0, 0
        gn = a32.alloc("gn_bc", (256,))
        self.DMA(gn.ap, dr["e_gn"][i].partition_broadcast(P), w=[gn])
        psc = a32.alloc("psc", (4,))
        self.DMA(psc.ap, dr["e_pscT"][i], w=[psc])
        mB32 = a32.off
        pw32 = a32.alloc("pw32", (4, P))
        pw = a16.alloc("pw", (4, P))
        self.DMA(pw32.ap, dr["e_pool_w"][i].rearrange("g c d -> c g d"), w=[pw32])
        self.cp("dve", pw.ap, pw32.ap, r=[pw32], w=[pw])
        wout = self.load_w_out(dr["e_w_out"][i], 12)
        S.fence()
        mB16 = a16.off
        for tag, Sq in self.seqs:
            a16.off, a32.off = mB16, mB32
            self.even_bwd(tag, Sq, wout, gn, pw, psc, first, last)
            S.fence()

    def gla_alloc(self):
        a16, a32 = self.a16, self.a32
        self.gE = [[a32.alloc(f"gE{b}{k}", (P,)) for k in range(3)] for b in range(2)]
        self.gq = [[a16.alloc(f"gq{b}{k}", (P,)) for k in range(4)] for b in range(2)]
        self.gS = a32.alloc("gS", (4, 256), nsub=4)
        self.gSb = a16.alloc("gSb", (4, 256), nsub=4)
        self.gps = []
        for b in range(2):
            small = 3 + 2 * b
            big = 4 + 2 * b
            self.gps.append(dict(bT=self.ps(small, 0, P, f"bT{b}"), cT=self.ps(small, P, P, f"cT{b}"),
                                 aT=self.ps(small, 2 * P, P, f"aT{b}"), o=self.ps(big, 0, 256, f"o{b}"),
                                 Pm=self.ps(big, 256, 256, f"Pm{b}")))
        self.gcnt = 0

    def gla_init_state(self):
        self.ms("dve", self.gS.ap, 0.0, w=[self.gS])
        self.ms("pool", self.gSb.ap, 0.0, w=[self.gSb])

    def gla_step(self, kk, h, sp, qT, kT, ktok, v, bwd, post):
        E1, E2, E3 = self.gE[kk]
        qeT, kdT, kl, aTm = self.gq[kk]
        g = self.gps[kk]
        cbf = self.c_bf
        TRI = cbf[:, 2, :] if bwd else cbf[:, 1, :]
        CL = cbf[:, 4, :] if bwd else cbf[:, 3, :]
        MSK = cbf[:, 3, :] if bwd else cbf[:, 1, :]
        Sb = self.gSb.bs[h]
        Sf = self.gS.bs[h]
        self.mm(g["bT"].ap, sp[0], TRI, r=[sp[1], cbf], w=[g["bT"]])
        self.mm(g["cT"].ap, CL, sp[0], r=[sp[1], cbf], w=[g["cT"]])
        yield
        self.act(E1.ap, g["bT"].ap, AF.Exp, r=[g["bT"]], w=[E1], scale=-1.0 / 16)
        self.act(E2.ap, g["bT"].ap, AF.Exp, r=[g["bT"]], w=[E2], scale=1.0 / 16)
        self.act(E3.ap, g["cT"].ap, AF.Exp, r=[g["cT"]], w=[E3], scale=-1.0 / 16)
        yield
        self.tt("pool", qeT.ap, qT[0], E1.ap, ALU.mult, r=[qT[1], E1], w=[qeT])
        self.tt("pool", kdT.ap, kT[0], E2.ap, ALU.mult, r=[kT[1], E2], w=[kdT])
        self.tt("pool", kl.ap, ktok[0], E3.ap, ALU.mult, r=[ktok[1], E3], w=[kl])
        yield
        self.mm(g["aT"].ap, kdT.ap, qeT.ap, r=[kdT, qeT], w=[g["aT"]])
        yield
        self.tt("dve", aTm.ap, g["aT"].ap, MSK, ALU.mult, r=[g["aT"], cbf], w=[aTm])
        yield
        self.mm(g["o"].ap, aTm.ap, v[0], start=True, stop=False, r=[aTm, v[1]], w=[g["o"]])
        self.mm(g["o"].ap, qeT.ap, self.gSb[:, h, :], start=False, stop=True, r=[qeT, Sb], w=[g["o"]])
        self.mm(g["Pm"].ap, kl.ap, v[0], r=[kl, v[1]], w=[g["Pm"]])
        yield
        lam = E1[:, 0:1] if bwd else E1[:, P - 1:P]
        self.stt(self.gS[:, h, :], self.gS[:, h, :], lam, g["Pm"].ap, ALU.mult, ALU.add, r=[Sf, E1, g["Pm"]], w=[Sf])
        self.cp("act", self.gSb[:, h, :], self.gS[:, h, :], r=[Sf], w=[Sb])
        yield from post(g["o"])

    def even_fwd(self, tag, Sq, win, aup, first):
        a16, a32, dr, TB, NT = self.a16, self.a32, self.dr, self.TB, self.NT
        xsrc = dr[f"x_{tag}"] if first else dr[f"y_{tag}"]
        self.alloc_xh()
        self.gla_alloc()
        fmq = [a16.alloc(f"fm_qT{b}", (4, TB), nsub=4) for b in range(2)]
        fmk = [a16.alloc(f"fm_kT{b}", (4, TB), nsub=4) for b in range(2)]
        fpu = a16.alloc("fm_puT", (4, TB), nsub=4)
        fpg = a16.alloc("fm_pgT", (4, TB), nsub=4)
        lrx = [a16.alloc(f"lrx{d}", (TB,)) for d in range(2)]
        NB4 = 2 * NT
        tk = [a16.alloc(f"tk{b}", (512,)) for b in range(NB4)]
        tspf = [a16.alloc(f"tspf{b}", (512,)) for b in range(NB4)]
        tv = [a16.alloc(f"tv{b}", (1024,), nsub=2) for b in range(NB4)]
        tspb = [a16.alloc(f"tspb{b}", (512,)) for b in range(2)]
        tgg = [a16.alloc(f"tgg{b}", (1024,), nsub=2) for b in range(2)]
        tof = [a32.alloc(f"tof{b}", (1024,), nsub=4) for b in range(2)]
        etmp = [a32.alloc(f"etmp{b}", (512,)) for b in range(2)]
        esil = [a32.alloc(f"esil{b}", (512,)) for b in range(2)]
        ps3 = [self.ps(k, 0, 512, f"ps3_{k}") for k in range(3)]
        for d in range(2):
            self.ms("pool", lrx[d][0:32], 1.0, w=[lrx[d]])
        self.gla_init_state()
        nb = Sq // TB
        st = dict(pc=0, tc=0)

        def nps():
            p = ps3[st["pc"] % 3]
            st["pc"] += 1
            return p

        def Pgen(bi):
            t0 = bi * TB
            bp = bi % 2
            hT = self.hT[bi % 2]
            yield from self.x_to_hT(xsrc, t0, bi)
            specs = [("qT", h, E_Q + h * P, fmq[bp]) for h in range(4)] + \
                    [("kT", h, E_K + h * P, fmk[bp]) for h in range(4)] + \
                    [("puT", h, E_PU + h * P, fpu) for h in range(4)] + \
                    [("pgT", h, E_PG + h * P, fpg) for h in range(4)]
            for (nm, h, col, dst) in specs:
                pf = nps()
                pst = Tile(pf[:, 0:TB], "pf")
                pst.bs = pf.bs
                self.proj_fm(win, hT, col, P, pst)
                if nm == "qT":
                    self.A("act", lambda e, o=dst[:, h, :], p=pst.ap: e.mul(out=o, in_=p, mul=float(P) ** -0.5),
                           r=[pst], w=[dst.bs[h]])
                elif nm == "pgT":
                    self.silu_ps(dst[:, h, :], [dst.bs[h]], pst, esil[h % 2], TB)
                else:
                    self.cp("dve", dst[:, h, :], pst.ap, r=[pst], w=[dst.bs[h]])
                yield
            for d in range(2):
                pf = nps()
                p16 = Tile(pf[0:16, 0:TB], "p16")
                p16.bs = pf.bs
                self.proj_fm(win, hT, E_LR + 16 * d, 16, p16)
                self.cp("dve", lrx[d][0:16, :], pf[0:16, 0:TB], r=[pf], w=[lrx[d]])
            yield
            for nm, src in (("qT", fmq[bp]), ("kT", fmk[bp]), ("puT", fpu), ("pgT", fpg)):
                self.DMA(dr[f"{nm}_{tag}"][:, :, t0:t0 + TB].rearrange("h p t -> p h t"), src.ap, r=[src])
            for j in range(NT):
                t = t0 + j * P
                b4 = (bi * NT + j) % NB4
                b2 = (bi * NT + j) % 2
                k_, spf_, v_, spb_, gg_ = tk[b4], tspf[b4], tv[b4], tspb[b2], tgg[b2]
                for d, dst in ((0, spf_), (1, spb_)):
                    pst = nps()
                    self.mm(pst.ap, lrx[d][0:17, j * P:(j + 1) * P], aup[0:17, d, :], r=[lrx[d], aup], w=[pst])
                    et = etmp[d]
                    self.act(et.ap, pst.ap, AF.Exp, r=[pst], w=[et], scale=-1.0)
                    self.act(dst.ap, et.ap, AF.Ln, r=[et], w=[dst], bias=1.0)
                    yield
                pst = nps()
                self.proj_tm(win, hT, j, E_K, 512, pst)
                self.cp("dve", k_.ap, pst.ap, r=[pst], w=[k_])
                yield
                for n in range(2):
                    pst = nps()
                    self.proj_tm(win, hT, j, E_V + n * 512, 512, pst)
                    self.cp("act", v_[:, n * 512:(n + 1) * 512], pst.ap, r=[pst], w=[v_.bs[n]])
                    yield
                for n in range(2):
                    pst = nps()
                    self.proj_tm(win, hT, j, E_G + n * 512, 512, pst)
                    self.silu_ps(gg_[:, n * 512:(n + 1) * 512], [gg_.bs[n]], pst, esil[n], 512)
                    yield
                self.DMA(dr[f"k_{tag}"][t:t + P, :], k_.ap, r=[k_])
                self.DMA(dr[f"spb_{tag}"][t:t + P, :], spb_.ap, r=[spb_])
                self.DMA(dr[f"v_{tag}"][t:t + P, :], v_.ap, r=[v_])
                self.DMA(dr[f"gg_{tag}"][t:t + P, :], gg_.ap, r=[gg_])

        def Sgen(bi, par):
            t0 = bi * TB
            bp = bi % 2
            for j in range(NT):
                t = t0 + j * P
                b4 = (bi * NT + j) % NB4
                b2 = (bi * NT + j) % 2
                k_, spf_, v_, of_ = tk[b4], tspf[b4], tv[b4], tof[b2]
                for h in (par, par + 2):
                    def post(po, h=h, of_=of_, t=t):
                        self.cp("dve", of_[:, h * 256:(h + 1) * 256], po.ap, r=[po], w=[of_.bs[h]])
                        self.DMA(dr[f"of_{tag}"][t:t + P, h * 256:(h + 1) * 256], of_[:, h * 256:(h + 1) * 256],
                                 r=[of_.bs[h]])
                        yield
                    yield from self.gla_step(par, h, (spf_[:, h * P:(h + 1) * P], spf_),
                                             (fmq[bp][:, h, j * P:(j + 1) * P], fmq[bp].bs[h]),
                                             (fmk[bp][:, h, j * P:(j + 1) * P], fmk[bp].bs[h]),
                                             (k_[:, h * P:(h + 1) * P], k_),
                                             (v_[:, h * 256:(h + 1) * 256], v_.bs[h // 2]), False, post)

        for bi in range(nb + 1):
            streams = []
            if bi >= 1:
                streams += [Sgen(bi - 1, 0), Sgen(bi - 1, 1)]
            if bi < nb:
                streams.append(Pgen(bi))
            interleave(streams)

    def even_bwd(self, tag, Sq, wout, gn, pw, psc, first, last):
        a16, a32, dr, TB, NT = self.a16, self.a32, self.dr, self.TB, self.NT
        xsrc = dr[f"x_{tag}"] if first else dr[f"y_{tag}"]
        ydst = dr[f"y_{tag}"]
        self.gla_alloc()
        self.alloc_resid()
        H = 8
        fmq = [a16.alloc(f"bq{b}", (4, TB), nsub=4) for b in range(2)]
        fmk = [a16.alloc(f"bk{b}", (4, TB), nsub=4) for b in range(2)]
        fpg = [a16.alloc(f"bpg{b}", (4, TB)) for b in range(2)]
        fpu = [a16.alloc(f"bpu{b}", (4, TB + 2 * H)) for b in range(2)]
        icn = [a32.alloc(f"icn{b}", (4, TB)) for b in range(2)]
        tk = [a16.alloc(f"tk{b}", (512,)) for b in range(2)]
        tspb = [a16.alloc(f"tspb{b}", (512,)) for b in range(2)]
        tv = [a16.alloc(f"tv{b}", (1024,)) for b in range(2)]
        tgg = [a16.alloc(f"tgg{b}", (1024,)) for b in range(2)]
        tof = [a32.alloc(f"tof{b}", (1024,)) for b in range(2)]
        osum = [a32.alloc(f"osum{b}", (256,)) for b in range(2)]
        osq = [a16.alloc(f"osq{b}", (256,)) for b in range(2)]
        hss = [a32.alloc(f"hss{b}", (1,)) for b in range(2)]
        otmp = [a32.alloc(f"otmp{b}", (256,)) for b in range(2)]
        glo = [a16.alloc(f"glo{b}", (1024,), nsub=4) for b in range(2)]
        featT = [a16.alloc(f"featT{b}", (12, TB), nsub=12) for b in range(2)]
        pt32 = [a32.alloc(f"pt32_{k}", (TB + 2 * H,)) for k in range(3)]
        pwin = a32.alloc("pwin", (TB,))
        pld = a16.alloc("pld", (TB,))
        psA, psB = self.ps(1, 0, 512, "psA"), self.ps(2, 0, 512, "psB")
        psX = self.ps(0, 0, TB, "psX")
        self.gla_init_state()
        nb = Sq // TB
        tiles = [(bi, j) for bi in range(nb - 1, -1, -1) for j in range(NT - 1, -1, -1)]

        def Lgen(bi):
            t0 = bi * TB
            bb = bi % 2
            q_, k_f, pg_, pu_, ic_, ft = fmq[bb], fmk[bb], fpg[bb], fpu[bb], icn[bb], featT[bb]
            self.DMA(q_.ap, dr[f"qT_{tag}"][:, :, t0:t0 + TB].rearrange("h p t -> p h t"), w=[q_])
            self.DMA(k_f.ap, dr[f"kT_{tag}"][:, :, t0:t0 + TB].rearrange("h p t -> p h t"), w=[k_f])
            self.DMA(pg_.ap, dr[f"pgT_{tag}"][:, :, t0:t0 + TB].rearrange("h p t -> p h t"), w=[pg_])
            lo = max(t0 - H, 0)
            hi = min(t0 + TB + H, Sq)
            if lo != t0 - H or hi != t0 + TB + H:
                self.ms("pool", pu_.ap, 0.0, w=[pu_])
            self.DMA(pu_[:, :, lo - (t0 - H):hi - (t0 - H)],
                     dr[f"puT_{tag}"][:, :, lo:hi].rearrange("h p t -> p h t"), w=[pu_])
            for g in range(4):
                self.DMA(ic_[:, g, :], dr[f"invcnt_{tag}"][g, t0:t0 + TB].partition_broadcast(P), w=[ic_])
            yield
            for g in range(4):
                W = TB + 2 * H
                hw = (1, 2, 4, 8)[g]
                if g == 0:
                    self.tt("pool", pwin.ap, pu_[:, g, H - 1:H - 1 + TB], pu_[:, g, H:H + TB], ALU.add, r=[pu_], w=[pwin])
                else:
                    step = 1
                    cur_ap = pu_[:, g, :]
                    cur_t = pu_
                    width = W
                    for lv in range(g):
                        dst = pt32[lv]
                        width = width - step
                        self.tt("pool", dst[:, 0:width], cur_ap[:, 0:width], cur_ap[:, step:step + width], ALU.add,
                                r=[cur_t], w=[dst])
                        cur_ap = dst.ap
                        cur_t = dst
                        step *= 2
                    self.tt("pool", pwin.ap, cur_ap[:, H - hw:H - hw + TB], cur_ap[:, H:H + TB], ALU.add, r=[cur_t], w=[pwin])
                yield
                self.tt("pool", pwin.ap, pwin.ap, ic_[:, g, :], ALU.mult, r=[pwin, ic_], w=[pwin])
                self.tt("pool", pld.ap, pwin.ap, pu_[:, g, H:H + TB], ALU.subtract, r=[pwin, pu_], w=[pld])
                yield
                self.mm(psX.ap, pw[:, g, :], pld.ap, r=[pw, pld], w=[psX])
                self.stt(ft[:, 8 + g, :], psX.ap, psc[:, g:g + 1], pg_[:, g, :], ALU.mult, ALU.mult,
                         r=[psX, psc, pg_], w=[ft.bs[8 + g]])
                yield

        def TLgen(n):
            bi, j = tiles[n]
            t = bi * TB + j * P
            tb = n % 2
            self.DMA(tk[tb].ap, dr[f"k_{tag}"][t:t + P, :], w=[tk[tb]])
            self.DMA(tspb[tb].ap, dr[f"spb_{tag}"][t:t + P, :], w=[tspb[tb]])
            self.DMA(tv[tb].ap, dr[f"v_{tag}"][t:t + P, :], w=[tv[tb]])
            self.DMA(tgg[tb].ap, dr[f"gg_{tag}"][t:t + P, :], w=[tgg[tb]])
            self.DMA(tof[tb].ap, dr[f"of_{tag}"][t:t + P, :], w=[tof[tb]])
            yield

        def Sgen(n, par):
            bi, j = tiles[n]
            bb = bi % 2
            tb = n % 2
            q_, k_f = fmq[bb], fmk[bb]
            k_, spb_, v_, gg_, of_, gl = tk[tb], tspb[tb], tv[tb], tgg[tb], tof[tb], glo[tb]
            for h in (par, par + 2):
                def post(po, h=h):
                    os_, hs_, ot_, oq_ = osum[par], hss[par], otmp[par], osq[par]
                    self.tt("dve", os_.ap, po.ap, of_[:, h * 256:(h + 1) * 256], ALU.add, r=[po, of_], w=[os_])
                    yield
                    self.act(oq_.ap, os_.ap, AF.Square, r=[os_], w=[oq_, hs_], accum=hs_.ap)
                    self.rsqrt_(hs_, 256)
                    yield
                    self.stt(ot_.ap, os_.ap, hs_[:, 0:1], gn.ap, ALU.mult, ALU.mult, r=[os_, hs_, gn], w=[ot_])
                    yield
                    self.tt("pool", gl[:, h * 256:(h + 1) * 256], ot_.ap, gg_[:, h * 256:(h + 1) * 256], ALU.mult,
                            r=[ot_, gg_], w=[gl.bs[h]])
                    yield
                yield from self.gla_step(par, h, (spb_[:, h * P:(h + 1) * P], spb_),
                                         (q_[:, h, j * P:(j + 1) * P], q_.bs[h]),
                                         (k_f[:, h, j * P:(j + 1) * P], k_f.bs[h]),
                                         (k_[:, h * P:(h + 1) * P], k_),
                                         (v_[:, h * 256:(h + 1) * 256], v_), True, post)

        def Rgen(n):
            bi, j = tiles[n]
            bb = bi % 2
            tb = n % 2
            ft, gl = featT[bb], glo[tb]
            t = bi * TB + j * P
            for c in range(8):
                self.tr(self.psT[:, c * P:(c + 1) * P], gl[:, c * P:(c + 1) * P], r=[gl], w=[self.psT])
            self.cp("act", ft[:, 0:8, j * P:(j + 1) * P], self.psT.ap.rearrange("p (c t) -> p c t", c=8),
                    r=[self.psT], w=ft.bs[0:8])
            yield
            yield from self.resid_out(ft, 12, wout, j, xsrc, ydst, t, last, psA, psB)

        interleave([Lgen(tiles[0][0]), TLgen(0)])
        for n in range(len(tiles) + 1):
            streams = []
            if n < len(tiles):
                streams += [Sgen(n, 0), Sgen(n, 1)]
                if n + 1 < len(tiles):
                    streams.append(TLgen(n + 1))
                    if tiles[n + 1][0] != tiles[n][0]:
                        streams.append(Lgen(tiles[n + 1][0]))
            if n >= 1:
                streams.append(Rgen(n - 1))
            interleave(streams)

    def odd_layer(self, layer, first, last):
        i = layer // 2
        a16, a32, dr, S = self.a16, self.a32, self.dr, self.S
        a16.off = 0
        a32.off = 0
        ifb = a32.alloc("ifb", (16,))
        self.DMA(ifb.ap, dr["o_if_bias"][i].partition_broadcast(P), w=[ifb])
        mF32 = a32.off
        qg = a32.alloc("qg", (4,))
        kvg = a32.alloc("kvg", (4,))
        self.DMA(qg[:, 0:3], dr["o_qgT"][i], w=[qg])
        self.DMA(kvg[:, 0:2], dr["o_kvgT"][i], w=[kvg])
        qup32 = a32.alloc("qup32", (3, 768))
        kvup32 = a32.alloc("kvup32", (2, 1024))
        self.DMA(qup32.ap, dr["o_q_up"][i].rearrange("(c p) n -> p c n", p=P), w=[qup32])
        self.DMA(kvup32.ap, dr["o_kv_up"][i].rearrange("(c p) n -> p c n", p=P), w=[kvup32])
        qup = a16.alloc("qup", (3, 768))
        quprot = a16.alloc("quprot", (3, 768))
        kvk = a16.alloc("kvk", (2, 512))
        kvv = a16.alloc("kvv", (2, 512))
        wkrot = a16.alloc("wkrot", (8, 32))
        self.ms("pool", quprot.ap, 0.0, w=[quprot])
        for c in range(3):
            self.ts("dve", qup[:, c, :], qup32[:, c, :], qg[:, c:c + 1], r=[qup32, qg], w=[qup])
            qv = qup[:, c, :].rearrange("p (h r) -> p h r", h=8)
            rv = quprot[:, c, :].rearrange("p (h r) -> p h r", h=8)
            self.ts("pool", rv[:, :, 64:80], qv[:, :, 80:96], -1.0, r=[qup], w=[quprot])
            self.cp("pool", rv[:, :, 80:96], qv[:, :, 64:80], r=[qup], w=[quprot])
        for c in range(2):
            kv3 = kvup32[:, c, :].rearrange("p (h r) -> p h r", h=8)
            self.ts("dve", kvk[:, c, :].rearrange("p (h r) -> p h r", h=8), kv3[:, :, 0:64], kvg[:, c:c + 1],
                    r=[kvup32, kvg], w=[kvk])
            self.ts("dve", kvv[:, c, :].rearrange("p (h r) -> p h r", h=8), kv3[:, :, 64:128], kvg[:, c:c + 1],
                    r=[kvup32, kvg], w=[kvv])
        win = self.load_w_in(dr["o_w_in"][i], O_COLS, layer)
        self.ts("pool", wkrot[:, :, 0:16], win[:, :, O_KR + 16:O_KR + 32], -1.0, r=[win], w=[wkrot])
        self.cp("pool", wkrot[:, :, 16:32], win[:, :, O_KR:O_KR + 16], r=[win], w=[wkrot])
        S.fence()
        mW16 = a16.off
        import os as _os
        dbg = _os.environ.get("KDBG", "FMB")
        for tag, Sq in self.seqs:
            a16.off, a32.off = mW16, mF32
            if "F" in dbg:
                self.odd_fwd(tag, Sq, win, wkrot, qup, quprot, kvk, kvv, ifb, first)
            S.fence()
        for tag, Sq in self.seqs:
            a16.off, a32.off = 0, 0
            if "M" in dbg:
                self.mla_pass(tag, Sq)
            S.fence()
        if "B" not in dbg:
            return
        a16.off, a32.off = 0, 0
        gml = a32.alloc("gml_bc", (P,))
        self.DMA(gml.ap, dr["o_mlg"][i].partition_broadcast(P), w=[gml])
        mB32 = a32.off
        wout = self.load_w_out(dr["o_w_out"][i], 8)
        S.fence()
        mB16 = a16.off
        for tag, Sq in self.seqs:
            a16.off, a32.off = mB16, mB32
            self.odd_bwd(tag, Sq, wout, gml, first, last)
            S.fence()

    def mlstm_alloc(self):
        a16, a32 = self.a16, self.a32
        self.mq = [[a16.alloc(f"mq{b}{k}", (P,)) for k in range(3)] for b in range(2)]
        self.mC = a32.alloc("mC", (4, 130), nsub=4)
        self.mCb = a16.alloc("mCb", (4, 130), nsub=4)
        self.malpha = a32.alloc("malpha", (4,), nsub=4)
        self.mden = [a32.alloc(f"mden{b}", (2,)) for b in range(2)]
        self.mps = []
        for b in range(2):
            self.mps.append(dict(sT=self.ps(3 + 2 * b, 0, P, f"sT{b}"), nd=self.ps(4 + 2 * b, 0, 129, f"nd{b}"),
                                 Pm=self.ps(4 + 2 * b, 256, 129, f"mPm{b}")))
        self.shl = a16.alloc("shl", (2, 8))
        self.ones_bf = a16.alloc("ones_bf", (P,))
        self.ms("pool", self.ones_bf.ap, 1.0, w=[self.ones_bf])
        self.ms("dve", self.mC.ap, 0.0, w=[self.mC])
        self.ms("pool", self.mCb.ap, 0.0, w=[self.mCb])
        self.ms("dve", self.malpha.ap, 1.0, w=[self.malpha])

    def mlstm_step(self, kk, h, w, db, anext, qT, kT, ktok, vx, bwd, post):
        sTm, kw, qa = self.mq[kk]
        g = self.mps[kk]
        den = self.mden[kk]
        cbf = self.c_bf
        MSK = cbf[:, 3, :] if bwd else cbf[:, 1, :]
        al = self.malpha[:, h:h + 1]
        alb = self.malpha.bs[h]
        Cf, Cb = self.mC.bs[h], self.mCb.bs[h]
        self.mm(g["sT"].ap, kT[0], qT[0], r=[kT[1], qT[1]], w=[g["sT"]])
        self.ts("dve", kw.ap, ktok[0], w[0], r=[ktok[1], w[1]], w=[kw])
        self.ts("dve", qa.ap, qT[0], al, r=[qT[1], alb], w=[qa])
        yield
        self.stt(sTm.ap, g["sT"].ap, w[0], MSK, ALU.mult, ALU.mult, r=[g["sT"], w[1], cbf], w=[sTm])
        yield
        self.mm(g["nd"].ap, sTm.ap, vx[0], start=True, stop=False, r=[sTm, vx[1]], w=[g["nd"]])
        self.mm(g["nd"].ap, qa.ap, self.mCb[:, h, 0:129], start=False, stop=True, r=[qa, Cb], w=[g["nd"]])
        self.mm(g["Pm"].ap, kw.ap, vx[0], r=[kw, vx[1]], w=[g["Pm"]])
        yield
        self.cp("dve", den[:, 1:2], g["nd"][:, 128:129], r=[g["nd"]], w=[den])
        self.stt(self.mC[:, h, 0:129], self.mC[:, h, 0:129], al, g["Pm"].ap, ALU.mult, ALU.add,
                 r=[Cf, alb, g["Pm"]], w=[Cf])
        yield
        self.cp("act", self.mCb[:, h, 0:129], self.mC[:, h, 0:129], r=[Cf], w=[Cb])
        self.stt(den[:, 0:1], den[:, 1:2], -1.0, den[:, 1:2], ALU.mult, ALU.max, r=[den], w=[den])
        self.tt("dve", den[:, 0:1], den[:, 0:1], db[0], ALU.max, r=[den, db[1]], w=[den])
        self.A("dve", lambda e, o=den[:, 1:2], i_=den[:, 0:1]: e.reciprocal(out=o, in_=i_), r=[den], w=[den])
        self.cp("pool", al, anext[0], r=[anext[1]], w=[alb])
        yield
        yield from post(g["nd"], den)

    def odd_fwd(self, tag, Sq, win, wkrot, qup, quprot, kvk, kvv, ifb, first):
        a16, a32, dr, TB, NT = self.a16, self.a32, self.dr, self.TB, self.NT
        xsrc = dr[f"x_{tag}"] if first else dr[f"y_{tag}"]
        cbf = self.c_bf
        self.alloc_xh()
        self.mlstm_alloc()
        cq = {n: a16.alloc(f"cq{n}", (3, TB)) for n in ("T", "sq", "n")}
        ckv = {n: a16.alloc(f"ckv{n}", (2, TB)) for n in ("T", "sq", "n")}
        fmg = a16.alloc("fm_mgT", (4, TB), nsub=4)
        fKn = a16.alloc("fm_Kn", (4, TB), nsub=4)
        fmq = [a16.alloc(f"fm_mqT{b}", (4, TB), nsub=4) for b in range(2)]
        fmk = [a16.alloc(f"fm_mkT{b}", (4, TB), nsub=4) for b in range(2)]
        Qst = a16.alloc("Qst", (8, TB), nsub=8)
        KRst = a16.alloc("KRst", (TB,))
        rhl = a16.alloc("rhl", (2, TB))
        NB4 = 2 * NT
        tmk = [a16.alloc(f"tmk{b}", (512,)) for b in range(NB4)]
        tmv = [a16.alloc(f"tmv{b}", (4, 130)) for b in range(NB4)]
        wt = [a32.alloc(f"wt{b}", (24,)) for b in range(NB4)]
        tmog = [a16.alloc(f"tmog{b}", (512,)) for b in range(2)]
        tV = [a16.alloc(f"tV{b}", (512,)) for b in range(2)]
        thf = [a32.alloc(f"thf{b}", (512,), nsub=4) for b in range(2)]
        rrow = {n: a32.alloc(f"rrow{n}", (TB,)) for n in ("q", "kv")}
        rbc = {n: a32.alloc(f"rbc{n}", (TB,)) for n in ("q", "kv")}
        cosT = a32.alloc("cosT", (TB,))
        sinT = a32.alloc("sinT", (TB,))
        rt1 = a32.alloc("rt1", (TB,))
        rt2 = a32.alloc("rt2", (TB,))
        sig = a32.alloc("sig", (512,))
        sil = a32.alloc("sil", (512,))
        G = [a32.alloc(f"G{b}", (16,)) for b in range(2)]
        spm = [a32.alloc(f"spm{b}", (8,)) for b in range(2)]
        gbs = [a32.alloc(f"gbs{b}", (16,)) for b in range(2)]
        ps3 = [self.ps(k, 0, 512, f"ps3_{k}") for k in range(3)]
        for b in range(NB4):
            self.ms("pool", tmv[b][:, :, 128:130], 1.0, w=[tmv[b]])
        for b in range(2):
            self.ms("pool", gbs[b].ap, 0.0, w=[gbs[b]])
        ones_col = cbf[:, 1, P - 1:P]
        nb = Sq // TB
        st = dict(pc=0)

        def nps():
            p = ps3[st["pc"] % 3]
            st["pc"] += 1
            return p

        def sub(pf, p1, width, name="pfs"):
            t = Tile(pf[0:p1, 0:width], name)
            t.bs = pf.bs
            return t

        def Pgen(bi):
            t0 = bi * TB
            bp = bi % 2
            hT = self.hT[bi % 2]
            yield from self.x_to_hT(xsrc, t0, bi)
            for rows in ((0, 32), (64, 96)):
                self.DMA(cosT[rows[0]:rows[1], :], dr["ropeT"][0, :, t0:t0 + TB], w=[cosT])
                self.DMA(sinT[rows[0]:rows[1], :], dr["ropeT"][1, :, t0:t0 + TB], w=[sinT])
            for (nm, col, nch, dim, T3) in (("q", O_CQ, 3, 384, cq), ("kv", O_CKV, 2, 256, ckv)):
                for c in range(nch):
                    pst = sub(nps(), P, TB)
                    self.proj_fm(win, hT, col + c * P, P, pst)
                    self.cp("dve", T3["T"][:, c, :], pst.ap, r=[pst], w=[T3["T"]])
                    self.tt("pool", T3["sq"][:, c, :], T3["T"][:, c, :], T3["T"][:, c, :], ALU.mult, r=[T3["T"]],
                            w=[T3["sq"]])
                    yield
                pst = nps()
                for c in range(nch):
                    self.mm(pst[0:1, 0:TB], ones_col, T3["sq"][:, c, :], start=(c == 0), stop=(c == nch - 1),
                            r=[T3["sq"], cbf], w=[pst])
                rr = rrow[nm]
                self.cp("dve", rr[0:1, :], pst[0:1, 0:TB], r=[pst], w=[rr])
                self.rsqrt_(rr, dim, 0, 1)
                self.cp("dve", rhl[0:1, 0, :], rr[0:1, :], r=[rr], w=[rhl])
                self.tt("dve", rhl[0:1, 1, :], rr[0:1, :], rhl[0:1, 0, :], ALU.subtract, r=[rr, rhl], w=[rhl])
                yield
                pst = nps()
                for k2 in range(2):
                    self.mm(pst[:, 0:TB], cbf[0:1, 1, :], rhl[0:1, k2, :], start=(k2 == 0), stop=(k2 == 1),
                            r=[cbf, rhl], w=[pst])
                self.cp("act", rbc[nm].ap, pst[:, 0:TB], r=[pst], w=[rbc[nm]])
                yield
                for c in range(nch):
                    self.tt("pool", T3["n"][:, c, :], T3["T"][:, c, :], rbc[nm].ap, ALU.mult, r=[T3["T"], rbc[nm]],
                            w=[T3["n"]])
                yield
            pa = nps()
            pb = nps()
            self.proj_fm(win, hT, O_KR, 32, sub(pa, 32, TB))
            for kc in range(8):
                self.mm(pb[0:32, 0:TB], wkrot[:, kc, :], hT[:, kc, :], start=(kc == 0), stop=(kc == 7),
                        r=[wkrot, hT], w=[pb])
            self.tt("dve", rt1[0:32, :], pa[0:32, 0:TB], cosT[0:32, :], ALU.mult, r=[pa, cosT], w=[rt1])
            self.tt("dve", rt2[0:32, :], pb[0:32, 0:TB], sinT[0:32, :], ALU.mult, r=[pb, sinT], w=[rt2])
            self.tt("pool", KRst[0:32, :], rt1[0:32, :], rt2[0:32, :], ALU.add, r=[rt1, rt2], w=[KRst])
            self.DMA(dr[f"KR_{tag}"][:, t0:t0 + TB], KRst[0:32, :], r=[KRst])
            yield
            for (dst, col, mode) in ((fmg, O_MG, "silu"), (fmq[bp], O_MQ, "copy"), (fmk[bp], O_MK, "scale")):
                for h in range(4):
                    pst = sub(nps(), P, TB)
                    self.proj_fm(win, hT, col + h * P, P, pst)
                    if mode == "silu":
                        self.silu_ps(dst[:, h, :], [dst.bs[h]], pst, sig if h % 2 == 0 else sil, TB)
                    elif mode == "copy":
                        self.cp("dve", dst[:, h, :], pst.ap, r=[pst], w=[dst.bs[h]])
                    else:
                        self.A("act", lambda e, o=dst[:, h, :], p=pst.ap: e.mul(out=o, in_=p, mul=float(P) ** -0.5),
                               r=[pst], w=[dst.bs[h]])
                    yield
            for h in range(8):
                pa = nps()
                pb = nps()
                for c in range(3):
                    self.mm(pa[0:96, 0:TB], qup[:, c, h * 96:(h + 1) * 96], cq["n"][:, c, :], start=(c == 0),
                            stop=(c == 2), r=[qup, cq["n"]], w=[pa])
                for c in range(3):
                    self.mm(pb[0:96, 0:TB], quprot[:, c, h * 96:(h + 1) * 96], cq["n"][:, c, :], start=(c == 0),
                            stop=(c == 2), r=[quprot, cq["n"]], w=[pb])
                self.cp("dve", Qst[0:64, h, :], pa[0:64, 0:TB], r=[pa], w=[Qst.bs[h]])
                self.tt("dve", rt1[64:96, :], pa[64:96, 0:TB], cosT[64:96, :], ALU.mult, r=[pa, cosT], w=[rt1])
                self.tt("dve", rt2[64:96, :], pb[64:96, 0:TB], sinT[64:96, :], ALU.mult, r=[pb, sinT], w=[rt2])
                self.tt("pool", Qst[64:96, h, :], rt1[64:96, :], rt2[64:96, :], ALU.add, r=[rt1, rt2], w=[Qst.bs[h]])
                yield
            for pr in range(4):
                pst = nps()
                for c in range(2):
                    self.mm(pst[:, 0:TB], kvk[:, c, pr * P:(pr + 1) * P], ckv["n"][:, c, :], start=(c == 0), stop=(c == 1),
                            r=[kvk, ckv["n"]], w=[pst])
                self.cp("dve", fKn[:, pr, :], pst[:, 0:TB], r=[pst], w=[fKn.bs[pr]])
                yield
            self.DMA(dr[f"Q_{tag}"][:, :, t0:t0 + TB].rearrange("h r t -> r h t"), Qst[0:96], r=[Qst])
            for nm, src in (("Kn", fKn), ("mgT", fmg), ("mqT", fmq[bp]), ("mkT", fmk[bp])):
                self.DMA(dr[f"{nm}_{tag}"][:, :, t0:t0 + TB].rearrange("h p t -> p h t"), src.ap, r=[src])
            for j in range(NT):
                t = t0 + j * P
                b4 = (bi * NT + j) % NB4
                b2 = (bi * NT + j) % 2
                mk_, mv_, wt_ = tmk[b4], tmv[b4], wt[b4]
                mog_, V_, G_, spm_, gbs_ = tmog[b2], tV[b2], G[b2], spm[b2], gbs[b2]
                pst = nps()
                self.proj_tm(win, hT, j, O_MK, 512, pst)
                self.A("act", lambda e, o=mk_.ap, p=pst.ap: e.mul(out=o, in_=p, mul=float(P) ** -0.5), r=[pst], w=[mk_])
                yield
                pst = nps()
                self.proj_tm(win, hT, j, O_MV, 512, pst)
                self.cp("dve", mv_[:, :, 0:128], pst.ap.rearrange("p (h d) -> p h d", h=4), r=[pst], w=[mv_])
                yield
                pst = nps()
                self.proj_tm(win, hT, j, O_MO, 512, pst)
                self.act(sig.ap, pst.ap, AF.Exp, r=[pst], w=[sig], scale=-1.0)
                self.act(sig.ap, sig.ap, AF.Ln, r=[sig], w=[sig], bias=1.0)
                yield
                pst = nps()
                self.proj_tm(win, hT, j, O_MLG, 512, pst)
                self.act(sil.ap, pst.ap, AF.Exp, r=[pst], w=[sil], scale=-1.0)
                self.act(sil.ap, sil.ap, AF.Ln, r=[sil], w=[sil], bias=1.0)
                self.tt("pool", sig.ap, sig.ap, sil.ap, ALU.add, r=[sig, sil], w=[sig])
                self.act(sig.ap, sig.ap, AF.Exp, r=[sig], w=[sig], scale=-1.0)
                self.tt("dve", mog_.ap, pst.ap, sig.ap, ALU.mult, r=[pst, sig], w=[mog_])
                yield
                pst = nps()
                for c in range(2):
                    self.mm(pst.ap, ckv["n"][:, c, j * P:(j + 1) * P], kvv[:, c, :], start=(c == 0), stop=(c == 1),
                            r=[ckv["n"], kvv], w=[pst])
                self.cp("act", V_.ap, pst.ap, r=[pst], w=[V_])
                yield
                pst = nps()
                p16 = Tile(pst[:, 0:16], "p16")
                p16.bs = pst.bs
                self.proj_tm(win, hT, j, O_IF, 16, p16)
                self.tt("dve", G_.ap, pst[:, 0:16], ifb.ap, ALU.add, r=[pst, ifb], w=[G_])
                yield
                self.gates(G_, spm_, wt_, nps())
                self.cp("pool", gbs_[:, 0:4], wt_[:, 4:8], r=[wt_], w=[gbs_])
                self.cp("pool", gbs_[:, 4:8], wt_[:, 12:16], r=[wt_], w=[gbs_])
                self.cp("pool", gbs_[:, 8:12], wt_[:, 20:24], r=[wt_], w=[gbs_])
                yield
                self.DMA(dr[f"mk_{tag}"][t:t + P, :], mk_.ap, r=[mk_])
                self.DMA(dr[f"mv_{tag}"][t:t + P, :].rearrange("p (h d) -> p h d", h=4), mv_[:, :, 0:128], r=[mv_])
                self.DMA(dr[f"mog_{tag}"][t:t + P, :], mog_.ap, r=[mog_])
                self.DMA(dr[f"V_{tag}"][t:t + P, :], V_.ap, r=[V_])
                self.DMA(dr[f"gb_{tag}"][t:t + P, :], gbs_.ap, r=[gbs_])

        def Sgen(bi, par):
            t0 = bi * TB
            bp = bi % 2
            for j in range(NT):
                t = t0 + j * P
                b4 = (bi * NT + j) % NB4
                b2 = (bi * NT + j) % 2
                mk_, mv_, wt_, hf_ = tmk[b4], tmv[b4], wt[b4], thf[b2]
                for h in (par, par + 2):
                    def post(nd, den, h=h, hf_=hf_, t=t):
                        self.ts("dve", hf_[:, h * P:(h + 1) * P], nd[:, 0:P], den[:, 1:2], r=[nd, den], w=[hf_.bs[h]])
                        self.DMA(dr[f"hf_{tag}"][t:t + P, h * P:(h + 1) * P], hf_[:, h * P:(h + 1) * P], r=[hf_.bs[h]])
                        yield
                    yield from self.mlstm_step(par, h, (wt_[:, h:h + 1], wt_), (wt_[:, 8 + h:9 + h], wt_),
                                               (wt_[:, 16 + h:17 + h], wt_),
                                               (fmq[bp][:, h, j * P:(j + 1) * P], fmq[bp].bs[h]),
                                               (fmk[bp][:, h, j * P:(j + 1) * P], fmk[bp].bs[h]),
                                               (mk_[:, h * P:(h + 1) * P], mk_),
                                               (mv_[:, h, 0:129], mv_), False, post)

        for bi in range(nb + 1):
            streams = []
            if bi >= 1:
                streams += [Sgen(bi - 1, 0), Sgen(bi - 1, 1)]
            if bi < nb:
                streams.append(Pgen(bi))
            interleave(streams)

    def gates(self, G_, spm_, wt_, gp):
        cf32 = self.c_f32
        self.act(spm_.ap, G_[:, 8:16], AF.Exp, r=[G_], w=[spm_], scale=-1.0)
        self.act(spm_.ap, spm_.ap, AF.Ln, r=[spm_], w=[spm_], bias=1.0)
        cbf = self.c_bf
        shl = self.shl
        self.cp("dve", shl[:, 0, :], spm_.ap, r=[spm_], w=[shl])
        self.tt("dve", shl[:, 1, :], spm_.ap, shl[:, 0, :], ALU.subtract, r=[spm_, shl], w=[shl])
        for k2 in range(2):
            self.mm(gp[:, 0:4], cbf[:, 1, :], shl[:, k2, 0:4], start=(k2 == 0), stop=(k2 == 1), r=[cbf, shl], w=[gp])
        for k2 in range(2):
            self.mm(gp[:, 4:8], cbf[:, 2, :], shl[:, k2, 4:8], start=(k2 == 0), stop=(k2 == 1), r=[cbf, shl], w=[gp])
        for k2 in range(2):
            self.mm(gp[:, 8:16], self.ones_bf.ap, shl[:, k2, 0:8], start=(k2 == 0), stop=(k2 == 1),
                    r=[self.ones_bf, shl], w=[gp])
        self.cp("dve", wt_[:, 8:24], gp[:, 0:16], r=[gp], w=[wt_])
        self.tt("dve", wt_[:, 0:8], wt_[:, 8:16], G_[:, 0:8], ALU.add, r=[wt_, G_], w=[wt_])
        self.act(wt_[:, 0:16], wt_[:, 0:16], AF.Exp, r=[wt_], w=[wt_])
        self.act(wt_[:, 16:24], wt_[:, 16:24], AF.Exp, r=[wt_], w=[wt_], scale=-1.0)

    def mla_pass(self, tag, Sq):
        a16, a32, dr = self.a16, self.a32, self.dr
        cf32 = self.c_f32
        NK = Sq // P
        QB = 512
        KT = [a16.alloc(f"KT{b}", (Sq,)) for b in range(2)]
        QT = [a16.alloc(f"QT{b}", (Sq,)) for b in range(2)]
        Vh = [a16.alloc(f"Vh{b}", (NK, 66)) for b in range(2)]
        PT = [a16.alloc(f"PT{b}", (1024,)) for b in range(3)]
        ATs = [a16.alloc(f"ATs{b}", (QB,)) for b in range(2)]
        rhl = a16.alloc("mrhl", (2, QB))
        rrow = a32.alloc("mrrow", (QB,))
        bcs = a32.alloc("mbcs", (QB,))
        sT = []
        for g in range(2):
            t = Tile(self.psall[:, g * 1024:(g + 1) * 1024], f"msT{g}")
            t.bs = [self.bankbuf[2 * g], self.bankbuf[2 * g + 1]]
            sT.append(t)
        oT = [self.ps(4, 0, QB, "oT0"), self.ps(5, 0, QB, "oT1")]
        bcp = self.ps(6, 0, QB, "bcp")
        for b in range(2):
            self.ms("pool", Vh[b][:, :, 64:66], 1.0, w=[Vh[b]])
        scale = 96.0 ** -0.5
        qcnt = 0
        gcnt = 0
        pcnt = 0
        ngr = NK // 2
        for h in range(8):
            kb = h % 2
            K_, Q_, V_ = KT[kb], QT[kb], Vh[kb]
            r0 = (h % 2) * 64
            self.DMA(K_[0:64, :], dr[f"Kn_{tag}"][h // 2, r0:r0 + 64, :], w=[K_])
            self.DMA(K_[64:96, :], dr[f"KR_{tag}"][:, :], w=[K_])
            self.DMA(Q_[0:96, :], dr[f"Q_{tag}"][h], w=[Q_])
            self.DMA(V_[:, :, 0:64], dr[f"V_{tag}"][:, h * 64:(h + 1) * 64].rearrange("(n p) c -> p n c", p=P), w=[V_])
            for qb in range(Sq // QB):
                o_ = oT[qcnt % 2]
                at_ = ATs[qcnt % 2]
                qcnt += 1
                qs = Q_[0:96, qb * QB:(qb + 1) * QB]

                def qk(kg, sg):
                    for n in range(2):
                        kt = 2 * kg + n
                        self.mm(sg[:, n * 512:(n + 1) * 512], K_[0:96, kt * P:(kt + 1) * P], qs, r=[K_, Q_], w=[sg])
                def pv(kg, pt):
                    for n in range(2):
                        kt = 2 * kg + n
                        self.mm(o_[0:65, :], V_[:, kt, 0:65], pt[:, n * 512:(n + 1) * 512], start=(kt == 0),
                                stop=(kt == NK - 1), r=[V_, pt], w=[o_])
                qk(0, sT[gcnt % 2])
                prev = None
                for kg in range(ngr):
                    sg = sT[gcnt % 2]
                    gcnt += 1
                    pt = PT[pcnt % 3]
                    pcnt += 1
                    if kg + 1 < ngr:
                        qk(kg + 1, sT[gcnt % 2])
                    self.act(pt.ap, sg.ap, AF.Exp, r=[sg], w=[pt], scale=scale)
                    if prev is not None:
                        pv(*prev)
                    prev = (kg, pt)
                pv(*prev)
                self.A("dve", lambda e, o=rrow[64:65, :], i_=o_[64:65, :]: e.reciprocal(out=o, in_=i_), r=[o_], w=[rrow])
                self.cp("dve", rhl[64:65, 0, :], rrow[64:65, :], r=[rrow], w=[rhl])
                self.tt("dve", rhl[64:65, 1, :], rrow[64:65, :], rhl[64:65, 0, :], ALU.subtract, r=[rrow, rhl], w=[rhl])
                for k2 in range(2):
                    self.mm(bcp[0:64, :], self.c_bf[64:65, 2, 0:64], rhl[64:65, k2, :], start=(k2 == 0), stop=(k2 == 1),
                            r=[self.c_bf, rhl], w=[bcp])
                self.cp("act", bcs[0:64, :], bcp[0:64, :], r=[bcp], w=[bcs])
                self.tt("dve", at_[0:64, :], o_[0:64, :], bcs[0:64, :], ALU.mult, r=[o_, bcs], w=[at_])
                self.DMA(dr[f"AT_{tag}"][h // 2, r0:r0 + 64, qb * QB:(qb + 1) * QB], at_[0:64, :], r=[at_])

    def odd_bwd(self, tag, Sq, wout, gml, first, last):
        a16, a32, dr, TB, NT = self.a16, self.a32, self.dr, self.TB, self.NT
        xsrc = dr[f"x_{tag}"] if first else dr[f"y_{tag}"]
        ydst = dr[f"y_{tag}"]
        self.mlstm_alloc()
        self.alloc_resid()
        fmq = [a16.alloc(f"bmq{b}", (4, TB), nsub=4) for b in range(2)]
        fmk = [a16.alloc(f"bmk{b}", (4, TB), nsub=4) for b in range(2)]
        fmg = [a16.alloc(f"bmg{b}", (4, TB)) for b in range(2)]
        fat = [a16.alloc(f"bat{b}", (4, TB)) for b in range(2)]
        tmk = [a16.alloc(f"tmk{b}", (512,)) for b in range(2)]
        tmv = [a16.alloc(f"tmv{b}", (4, 130)) for b in range(2)]
        tmog = [a16.alloc(f"tmog{b}", (512,)) for b in range(2)]
        tgb = [a32.alloc(f"tgb{b}", (16,)) for b in range(2)]
        thf = [a32.alloc(f"thf{b}", (512,)) for b in range(2)]
        hsum = [a32.alloc(f"hsum{b}", (P,)) for b in range(2)]
        hsq = [a16.alloc(f"hsq{b}", (P,)) for b in range(2)]
        hss = [a32.alloc(f"hss{b}", (1,)) for b in range(2)]
        otmp = [a32.alloc(f"otmp{b}", (P,)) for b in range(2)]
        mlo = [a16.alloc(f"mlo{b}", (512,), nsub=4) for b in range(2)]
        featT = [a16.alloc(f"featT{b}", (8, TB), nsub=8) for b in range(2)]
        psA, psB = self.ps(1, 0, 512, "psA"), self.ps(2, 0, 512, "psB")
        for b in range(2):
            self.ms("pool", tmv[b][:, :, 128:130], 1.0, w=[tmv[b]])
        nb = Sq // TB
        tiles = [(bi, j) for bi in range(nb - 1, -1, -1) for j in range(NT - 1, -1, -1)]

        def Lgen(bi):
            t0 = bi * TB
            bb = bi % 2
            q_, k_f, mg_, at_, ft = fmq[bb], fmk[bb], fmg[bb], fat[bb], featT[bb]
            for (dst, nm) in ((q_, "mqT"), (k_f, "mkT"), (mg_, "mgT"), (at_, "AT")):
                self.DMA(dst.ap, dr[f"{nm}_{tag}"][:, :, t0:t0 + TB].rearrange("h p t -> p h t"), w=[dst])
            yield
            self.tt("pool", ft[:, 0:4, :], at_.ap, mg_.ap, ALU.mult, r=[at_, mg_], w=ft.bs[0:4])
            yield

        def TLgen(n):
            bi, j = tiles[n]
            t = bi * TB + j * P
            tb = n % 2
            self.DMA(tmk[tb].ap, dr[f"mk_{tag}"][t:t + P, :], w=[tmk[tb]])
            self.DMA(tmv[tb][:, :, 0:128], dr[f"mv_{tag}"][t:t + P, :].rearrange("p (h d) -> p h d", h=4), w=[tmv[tb]])
            self.DMA(tmog[tb].ap, dr[f"mog_{tag}"][t:t + P, :], w=[tmog[tb]])
            self.DMA(tgb[tb].ap, dr[f"gb_{tag}"][t:t + P, :], w=[tgb[tb]])
            self.DMA(thf[tb].ap, dr[f"hf_{tag}"][t:t + P, :], w=[thf[tb]])
            yield

        def Sgen(n, par):
            bi, j = tiles[n]
            bb = bi % 2
            tb = n % 2
            q_, k_f = fmq[bb], fmk[bb]
            mk_, mv_, mog_, gb_, hf_, ml = tmk[tb], tmv[tb], tmog[tb], tgb[tb], thf[tb], mlo[tb]
            for h in (par, par + 2):
                def post(nd, den, h=h):
                    hs_, ss_, ot_, hq_ = hsum[par], hss[par], otmp[par], hsq[par]
                    self.stt(hs_.ap, nd[:, 0:P], den[:, 1:2], hf_[:, h * P:(h + 1) * P], ALU.mult, ALU.add,
                             r=[nd, den, hf_], w=[hs_])
                    yield
                    self.act(hq_.ap, hs_.ap, AF.Square, r=[hs_], w=[hq_, ss_], accum=ss_.ap)
                    self.rsqrt_(ss_, P)
                    yield
                    self.stt(ot_.ap, hs_.ap, ss_[:, 0:1], gml.ap, ALU.mult, ALU.mult, r=[hs_, ss_, gml], w=[ot_])
                    yield
                    self.tt("pool", ml[:, h * P:(h + 1) * P], ot_.ap, mog_[:, h * P:(h + 1) * P], ALU.mult,
                            r=[ot_, mog_], w=[ml.bs[h]])
                    yield
                yield from self.mlstm_step(par, h, (gb_[:, h:h + 1], gb_), (gb_[:, 4 + h:5 + h], gb_),
                                           (gb_[:, 8 + h:9 + h], gb_),
                                           (q_[:, h, j * P:(j + 1) * P], q_.bs[h]),
                                           (k_f[:, h, j * P:(j + 1) * P], k_f.bs[h]),
                                           (mk_[:, h * P:(h + 1) * P], mk_),
                                           (mv_[:, h, 0:129], mv_), True, post)

        def Rgen(n):
            bi, j = tiles[n]
            bb = bi % 2
            tb = n % 2
            ft, ml = featT[bb], mlo[tb]
            t = bi * TB + j * P
            for c in range(4):
                self.tr(self.psT[:, c * P:(c + 1) * P], ml[:, c * P:(c + 1) * P], r=[ml], w=[self.psT])
            self.cp("act", ft[:, 4:8, j * P:(j + 1) * P],
                    self.psT[:, 0:512].rearrange("p (c t) -> p c t", c=4), r=[self.psT], w=ft.bs[4:8])
            yield
            yield from self.resid_out(ft, 8, wout, j, xsrc, ydst, t, last, psA, psB)

        interleave([Lgen(tiles[0][0]), TLgen(0)])
        for n in range(len(tiles) + 1):
            streams = []
            if n < len(tiles):
                streams += [Sgen(n, 0), Sgen(n, 1)]
                if n + 1 < len(tiles):
                    streams.append(TLgen(n + 1))
                    if tiles[n + 1][0] != tiles[n][0]:
                        streams.append(Lgen(tiles[n + 1][0]))
            if n >= 1:
                streams.append(Rgen(n - 1))
            interleave(streams)


def make_consts(smax):
    j = np.arange(P)[:, None]
    i = np.arange(P)[None, :]
    ident = (j == i)
    TL = (j <= i)
    TG = (j >= i)
    SG = (j > i)
    SL = (j < i)
    c_bf = np.stack([ident, TL, TG, SG, SL], axis=1).astype(np.float32).astype(ml_dtypes.bfloat16)
    c_f32 = np.stack([TL, TG, np.ones((P, P), bool)], axis=1).astype(np.float32)
    inv = (np.float32(10000.0) ** (-np.arange(0, 32, 2, dtype=np.float32) / np.float32(32))).astype(np.float32)
    ang = (np.arange(smax, dtype=np.float32)[:, None] * inv[None, :]).astype(np.float32)
    cos = np.cos(ang).astype(np.float32).T
    sin = np.sin(ang).astype(np.float32).T
    ropeT = np.stack([np.concatenate([cos, cos], 0), np.concatenate([sin, sin], 0)], 0)
    return c_bf, c_f32, np.ascontiguousarray(ropeT)


def make_invcnt(S):
    pos = np.arange(S)
    out = np.zeros((4, S), np.float32)
    for gi, w in enumerate((2, 4, 8, 16)):
        lo = np.clip(pos - w // 2, 0, S)
        hi = np.clip(pos + w // 2, 0, S)
        out[gi] = 1.0 / (hi - lo).astype(np.float32)
    return out


def prep_weights(w):
    f = lambda a: np.ascontiguousarray(np.asarray(a, dtype=np.float32))
    out = {}
    out["norm_gT"] = f(np.asarray(w["norm_g"]).reshape(-1, 8, P).transpose(0, 2, 1))
    out["final_g"] = f(w["final_norm_g"])
    out["e_w_in"] = f(w["e_w_in"])
    out["e_aup"] = f(np.concatenate([np.asarray(w["e_gla_a_up"]), np.asarray(w["e_gla_a_bias"])[:, :, None, :]], axis=2))
    out["e_gn"] = f(w["e_gla_norm_g"])
    out["e_pool_w"] = f(w["e_pool_w"])
    out["e_pscT"] = f(np.asarray(w["e_pool_scale"]).reshape(-1, 4, P).transpose(0, 2, 1))
    out["e_w_out"] = f(w["e_w_out"])
    out["o_w_in"] = f(w["o_w_in"])
    out["o_qgT"] = f(np.asarray(w["o_q_norm_g"]).reshape(-1, 3, P).transpose(0, 2, 1))
    out["o_q_up"] = f(w["o_q_up"])
    out["o_kvgT"] = f(np.asarray(w["o_kv_norm_g"]).reshape(-1, 2, P).transpose(0, 2, 1))
    out["o_kv_up"] = f(w["o_kv_up"])
    out["o_if_bias"] = f(w["o_if_bias"])
    out["o_mlg"] = f(w["o_mlstm_norm_g"])
    out["o_w_out"] = f(w["o_w_out"])
    return out


_CACHE = {}


def run_trunk(seq_inputs, weights, layers=(0, 1, 2, 3), final_norm=True, TB=256, n_cores=8):
    seqs = [(tag, a.shape[0]) for tag, a in seq_inputs[0].items()]
    key = (tuple(seqs), tuple(layers), final_norm, TB)
    if key not in _CACHE:
        b = Builder(seqs, layers, TB, final_norm)
        b.build()
        _CACHE[key] = b
    b = _CACHE[key]
    c_bf, c_f32, ropeT = make_consts(b.smax)
    wp = prep_weights(weights)
    in_maps = []
    for c in range(n_cores):
        m = dict(wp)
        m["c_bf"], m["c_f32"], m["ropeT"] = c_bf, c_f32, ropeT
        for tag, a in seq_inputs[c].items():
            m[f"x_{tag}"] = np.ascontiguousarray(a, dtype=np.float32)
            m[f"invcnt_{tag}"] = make_invcnt(a.shape[0])
        in_maps.append(m)
    res = run_bass_kernel_spmd(b.nc, in_maps, core_ids=list(range(n_cores)))
    return [{tag: r[f"y_{tag}"] for tag, _ in seqs} for r in res.results]


def kernel(x_prompt, x_sample, **weights):
    x_prompt = np.asarray(x_prompt, dtype=np.float32)
    x_sample = np.asarray(x_sample, dtype=np.float32)
    nb_p = x_prompt.shape[0]
    seq_inputs = []
    for c in range(8):
        seq_inputs.append({"s": x_sample[c], "p": x_prompt[c % nb_p]})
    outs = run_trunk(seq_inputs, weights)
    y_sample = np.stack([outs[c]["s"] for c in range(8)], axis=0)
    y_prompt = np.stack([outs[c]["p"] for c in range(nb_p)], axis=0)
    return (y_prompt, y_sample)
```

```python
import numpy as np
import ml_dtypes
from contextlib import ExitStack
import concourse.bass as bass
import concourse.mybir as mybir
from concourse.bass_utils import run_bass_kernel_spmd

F32 = mybir.dt.float32
BF16 = mybir.dt.bfloat16
ALU = mybir.AluOpType
AF = mybir.ActivationFunctionType

ENGS = ("pe", "act", "dve", "pool", "sp")
SEM_WRAP = 30000
P = 128
DM = 1024
EPS = 1e-6


class Buf:
    __slots__ = ("name", "writers", "readers")

    def __init__(self, name):
        self.name = name
        self.writers = []
        self.readers = []


class Instr:
    __slots__ = ("eng", "fn", "deps", "signal", "count", "semidx", "is_dma", "dma_slot", "dma_target", "dma_prev")

    def __init__(self, eng, fn, is_dma):
        self.eng = eng
        self.fn = fn
        self.deps = []
        self.signal = False
        self.count = 0
        self.semidx = 0
        self.is_dma = is_dma
        self.dma_slot = None
        self.dma_target = 0
        self.dma_prev = None


class Sched:
    def __init__(self, same_engine_sync=True, dma_slots=8):
        self.lists = {e: [] for e in ENGS}
        self.same_engine_sync = same_engine_sync
        self.dma_slots = dma_slots
        self.dma_count = {e: 0 for e in ENGS}
        self.dma_hist = {e: [] for e in ENGS}
        self.last_compute = {e: None for e in ENGS}
        self.cur_fence = None

    def add(self, eng, fn, reads=(), writes=(), dma=False):
        I = Instr(eng, fn, dma)
        deps = {}
        for b in reads:
            for w in b.writers:
                deps[id(w)] = w
        for b in writes:
            for w in b.writers:
                deps[id(w)] = w
            for r in b.readers:
                deps[id(r)] = r
        if self.cur_fence is not None:
            deps[id(self.cur_fence)] = self.cur_fence
        for d in deps.values():
            if (not d.is_dma) and d.eng == eng and not dma:
                if eng == "pe" or eng == "sp" or not self.same_engine_sync:
                    continue
            if not d.is_dma:
                d.signal = True
            I.deps.append(d)
        if dma:
            n = self.dma_count[eng]
            self.dma_count[eng] = n + 1
            I.dma_slot = n % self.dma_slots
            I.dma_target = 16 * (n // self.dma_slots + 1)
            if n >= self.dma_slots:
                I.dma_prev = self.dma_hist[eng][n - self.dma_slots]
            self.dma_hist[eng].append(I)
        else:
            self.last_compute[eng] = I
        for b in reads:
            if dma:
                b.readers.append(I)
            else:
                b.readers = [r for r in b.readers if r.is_dma or r.eng != eng]
                b.readers.append(I)
        for b in writes:
            b.writers = [I]
            b.readers = []
        self.lists[eng].append(I)
        return I

    def fence(self):
        I = Instr("sp", lambda e: e.nop(), False)
        for e in ENGS:
            lc = self.last_compute[e]
            if lc is not None and e != "sp":
                lc.signal = True
                I.deps.append(lc)
            h = self.dma_hist[e]
            for d in h[-self.dma_slots:]:
                I.deps.append(d)
        I.signal = True
        self.lists["sp"].append(I)
        self.last_compute["sp"] = I
        self.cur_fence = I

    def emit(self, nc, E):
        nsem = {}
        for e in ENGS:
            c = 0
            si = 0
            for I in self.lists[e]:
                if I.is_dma:
                    continue
                if I.signal:
                    c += 1
                    if c > SEM_WRAP:
                        si += 1
                        c = 1
                    I.count = c
                    I.semidx = si
            nsem[e] = si + 1
        prog = {e: [E(nc.semaphore(f"pg_{e}{i}")) for i in range(nsem[e])] for e in ENGS}
        dsem = {e: [E(nc.semaphore(f"dm_{e}{i}")) for i in range(self.dma_slots)]
                for e in ENGS if self.dma_count[e] > 0}
        block = E(nc.Block())
        engobj = {"pe": block.tensor, "act": block.scalar, "dve": block.vector, "pool": block.gpsimd,
                  "sp": block.sync}
        stats = {e: [0, 0] for e in ENGS}

        def run(e, eng):
            waited = {}
            for I in self.lists[e]:
                need = {}
                for d in I.deps:
                    if d.is_dma:
                        key = ("d", d.eng, d.dma_slot)
                        val = d.dma_target
                    else:
                        key = ("p", d.eng, d.semidx)
                        val = d.count
                    if need.get(key, 0) < val:
                        need[key] = val
                if I.is_dma and I.dma_prev is not None:
                    key = ("d", e, I.dma_slot)
                    val = I.dma_prev.dma_target
                    if need.get(key, 0) < val:
                        need[key] = val
                for key, val in need.items():
                    if waited.get(key, 0) >= val:
                        continue
                    waited[key] = val
                    sem = dsem[key[1]][key[2]] if key[0] == "d" else prog[key[1]][key[2]]
                    eng.wait_ge(sem, val)
                    stats[e][1] += 1
                ins = I.fn(eng)
                stats[e][0] += 1
                if I.is_dma:
                    ins.then_inc(dsem[e][I.dma_slot], 16)
                elif I.signal:
                    ins.then_inc(prog[e][I.semidx], 1)
            if e in dsem:
                n = self.dma_count[e]
                for s in range(min(n, self.dma_slots)):
                    last_n = ((n - 1 - s) // self.dma_slots) * self.dma_slots + s
                    eng.wait_ge(dsem[e][s], 16 * (last_n // self.dma_slots + 1))

        for e in ENGS:
            if not self.lists[e]:
                continue

            def mk(e):
                def f(eng):
                    run(e, eng)
                return f
            engobj[e](mk(e))
        return stats


class Tile:
    def __init__(self, ap, name, nsub=1):
        self.ap = ap
        self.bs = [Buf(f"{name}.{i}") for i in range(nsub)]

    def __getitem__(self, idx):
        return self.ap[idx]

    @property
    def b(self):
        return self.bs[0]


class Arena:
    def __init__(self, tensor, size):
        self.t = tensor
        self.size = size
        self.off = 0
        self.peak = 0

    def alloc(self, name, free, nsub=1):
        n = 1
        for f in free:
            n *= f
        n2 = (n + 3) // 4 * 4
        assert self.off + n2 <= self.size, f"arena overflow {name}: {self.off}+{n2}>{self.size}"
        a = self.t[:, self.off:self.off + n]
        self.off += n2
        self.peak = max(self.peak, self.off)
        if len(free) == 2:
            a = a.rearrange("p (a b) -> p a b", a=free[0])
        elif len(free) == 3:
            a = a.rearrange("p (a b c) -> p a b c", a=free[0], b=free[1])
        return Tile(a, name, nsub)


def interleave(gens):
    gens = [g for g in gens if g is not None]
    while gens:
        for g in list(gens):
            try:
                next(g)
            except StopIteration:
                gens.remove(g)


def zip2(a, b):
    gens = [g for g in (a, b) if g is not None]
    while gens:
        for g in list(gens):
            try:
                next(g)
            except StopIteration:
                gens.remove(g)
        yield


def run_pipeline(ntiles, chain_fn, tl_fn, l_fn, r_fn, r_delay=5):
    chains = {p: chain_fn(p) for p in (0, 1)}
    started = {0: -1, 1: -1}
    rounds_in_tile = 0
    cur = -1
    aux = []
    r_launched = -1
    while chains or aux or r_launched < ntiles - 1:
        for p in list(chains):
            try:
                v = next(chains[p])
            except StopIteration:
                del chains[p]
                started[p] = ntiles
                continue
            if isinstance(v, tuple):
                started[p] = v[1]
        m = min(started.values())
        if m > cur:
            cur = m
            rounds_in_tile = 0
            if cur + 1 < ntiles:
                aux.append(tl_fn(cur + 1))
                g = l_fn(cur + 1)
                if g is not None:
                    aux.append(g)
        else:
            rounds_in_tile += 1
        if r_launched < cur - 1 and (rounds_in_tile >= r_delay or cur >= ntiles):
            r_launched += 1
            aux.append(r_fn(r_launched))
        for g in list(aux):
            try:
                next(g)
            except StopIteration:
                aux.remove(g)


def _bl(xs):
    out = []
    for x in xs:
        if isinstance(x, Buf):
            out.append(x)
        elif isinstance(x, Tile):
            out.extend(x.bs)
        else:
            out.extend(_bl(x))
    return out


E_Q, E_K, E_V, E_G, E_LR, E_PU, E_PG, E_COLS = 0, 512, 1024, 2048, 3072, 3104, 3616, 4128
O_CQ, O_CKV, O_KR, O_MG, O_MQ, O_MK, O_MV, O_MO, O_IF, O_MLG, O_COLS = (0, 384, 640, 672, 1184, 1696, 2208, 2720,
                                                                      3232, 3248, 3760)
WMAX = 4128
A16_SIZE = 68000
A32_SIZE = 14848


class Builder:
    def __init__(self, seqs, layers=(0, 1, 2, 3), TB=256, final_norm=True, smax=None):
        self.seqs = list(seqs)
        self.layers = list(layers)
        self.TB = TB
        self.NT = TB // P
        self.final_norm = final_norm
        self.smax = smax or max(s for _, s in seqs)
        self.nc = bass.Bass("TRN2", target_bir_lowering=False)
        self.S = Sched()
        self.dr = {}
        self._decl_dram()

    def _in(self, name, shape, dt=F32):
        self.dr[name] = self.nc.dram_tensor(name, list(shape), dt, kind="ExternalInput").ap()

    def _scr(self, name, shape, dt):
        self.dr[name] = self.nc.dram_tensor(name, list(shape), dt, kind="Internal").ap()

    def _decl_dram(self):
        nc = self.nc
        for tag, S in self.seqs:
            self._in(f"x_{tag}", [S, DM])
            self.dr[f"y_{tag}"] = nc.dram_tensor(f"y_{tag}", [S, DM], F32, kind="ExternalOutput").ap()
            self._in(f"invcnt_{tag}", [4, S])
            self._scr(f"qT_{tag}", [4, P, S], BF16)
            self._scr(f"kT_{tag}", [4, P, S], BF16)
            self._scr(f"puT_{tag}", [4, P, S], BF16)
            self._scr(f"pgT_{tag}", [4, P, S], BF16)
            self._scr(f"k_{tag}", [S, 512], BF16)
            self._scr(f"spb_{tag}", [S, 512], BF16)
            self._scr(f"v_{tag}", [S, 1024], BF16)
            self._scr(f"gg_{tag}", [S, 1024], BF16)
            self._scr(f"of_{tag}", [S, 1024], F32)
            self._scr(f"Q_{tag}", [8, 96, S], BF16)
            self._scr(f"Kn_{tag}", [4, P, S], BF16)
            self._scr(f"KR_{tag}", [32, S], BF16)
            self._scr(f"V_{tag}", [S, 512], BF16)
            self._scr(f"AT_{tag}", [4, P, S], BF16)
            self._scr(f"mgT_{tag}", [4, P, S], BF16)
            self._scr(f"mqT_{tag}", [4, P, S], BF16)
            self._scr(f"mkT_{tag}", [4, P, S], BF16)
            self._scr(f"mk_{tag}", [S, 512], BF16)
            self._scr(f"mv_{tag}", [S, 512], BF16)
            self._scr(f"mog_{tag}", [S, 512], BF16)
            self._scr(f"gb_{tag}", [S, 16], F32)
            self._scr(f"hf_{tag}", [S, 512], F32)
        self._in("norm_gT", [4, P, 8])
        self._in("final_g", [DM])
        self._in("e_w_in", [2, DM, E_COLS])
        self._in("e_aup", [2, 2, 17, 512])
        self._in("e_gn", [2, 256])
        self._in("e_pool_w", [2, 4, P, P])
        self._in("e_pscT", [2, P, 4])
        self._in("e_w_out", [2, 1536, DM])
        self._in("o_w_in", [2, DM, O_COLS])
        self._in("o_qgT", [2, P, 3])
        self._in("o_q_up", [2, 384, 768])
        self._in("o_kvgT", [2, P, 2])
        self._in("o_kv_up", [2, 256, 1024])
        self._in("o_if_bias", [2, 16])
        self._in("o_mlg", [2, P])
        self._in("o_w_out", [2, 1024, DM])
        self._in("c_bf", [P, 5, P], BF16)
        self._in("c_f32", [P, 3, P], F32)
        self._in("ropeT", [2, 32, self.smax], F32)

    def A(self, eng, fn, r=(), w=()):
        return self.S.add(eng, fn, reads=_bl(r), writes=_bl(w))

    def DMA(self, out, in_, r=(), w=(), q="sp"):
        return self.S.add(q, lambda e: e.dma_start(out=out, in_=in_), reads=_bl(r), writes=_bl(w), dma=True)

    def mm(self, out, lhsT, rhs, start=True, stop=True, r=(), w=()):
        return self.A("pe", lambda e: e.matmul(out, lhsT=lhsT, rhs=rhs, start=start, stop=stop), r, w)

    def tr(self, out, in_, r=(), w=()):
        idt = self.c_bf[:, 0, :]
        return self.A("pe", lambda e: e.transpose(out=out, in_=in_, identity=idt), list(r) + [self.c_bf], w)

    def act(self, out, in_, func, r=(), w=(), scale=1.0, bias=0.0, accum=None):
        if accum is None:
            return self.A("act", lambda e: e.activation(out=out, in_=in_, func=func, bias=bias, scale=scale), r, w)
        return self.A("act", lambda e: e.activation(out=out, in_=in_, func=func, bias=bias, scale=scale,
                                                    accum_out=accum), r, w)

    def ts(self, eng, out, in0, s1, s2=None, op0=ALU.mult, op1=None, r=(), w=()):
        if op1 is None:
            return self.A(eng, lambda e: e.tensor_scalar(out=out, in0=in0, scalar1=s1, scalar2=None, op0=op0), r, w)
        return self.A(eng, lambda e: e.tensor_scalar(out=out, in0=in0, scalar1=s1, scalar2=s2, op0=op0, op1=op1), r, w)

    def tt(self, eng, out, in0, in1, op, r=(), w=()):
        return self.A(eng, lambda e: e.tensor_tensor(out=out, in0=in0, in1=in1, op=op), r, w)

    def stt(self, out, in0, scalar, in1, op0, op1, r=(), w=()):
        return self.A("dve", lambda e: e.scalar_tensor_tensor(out=out, in0=in0, scalar=scalar, in1=in1, op0=op0,
                                                               op1=op1), r, w)

    def cp(self, eng, out, in_, r=(), w=()):
        if eng == "act":
            return self.A("act", lambda e: e.copy(out=out, in_=in_), r, w)
        return self.A(eng, lambda e: e.tensor_copy(out=out, in_=in_), r, w)

    def ms(self, eng, ap, val, w=()):
        return self.A(eng, lambda e: e.memset(ap, val), (), w)

    def silu_ps(self, dst_ap, dst_bufs, pst, et, W):
        e = et[:, 0:W]
        self.act(e, pst.ap, AF.Exp, r=[pst], w=[et], scale=-1.0)
        self.act(e, e, AF.Ln, r=[et], w=[et], bias=1.0)
        self.act(e, e, AF.Exp, r=[et], w=[et], scale=-1.0)
        self.tt("dve", dst_ap, pst.ap, e, ALU.mult, r=[pst, et], w=dst_bufs)

    def rsqrt_(self, t, n, p0=0, p1=P):
        self.act(t[p0:p1], t[p0:p1], AF.Ln, r=[t], w=[t], scale=1.0 / n, bias=self.eps_t[p0:p1, 0:1])
        self.act(t[p0:p1], t[p0:p1], AF.Exp, r=[t], w=[t], scale=-0.5)

    def build(self):
        nc = self.nc
        with ExitStack() as es:
            E = es.enter_context
            self.a16 = Arena(E(nc.sbuf_tensor("arena16", [P, A16_SIZE], BF16)).ap(), A16_SIZE)
            self.a32 = Arena(E(nc.sbuf_tensor("arena32", [P, A32_SIZE], F32)).ap(), A32_SIZE)
            self.c_bf = Tile(E(nc.sbuf_tensor("sb_c_bf", [P, 5, P], BF16)).ap(), "c_bf")
            self.c_f32 = Tile(E(nc.sbuf_tensor("sb_c_f32", [P, 3, P], F32)).ap(), "c_f32")
            self.eps_t = Tile(E(nc.sbuf_tensor("eps_t", [P, 1], F32)).ap(), "eps")
            self.gcol = Tile(E(nc.sbuf_tensor("gcol", [P, 8], F32)).ap(), "gcol")
            self.gfin = Tile(E(nc.sbuf_tensor("gfin", [P, DM], F32)).ap(), "gfin")
            self.psall = E(nc.psum_tensor("psall", [P, 7 * 512], F32)).ap()
            self.psb = [self.psall[:, i * 512:(i + 1) * 512] for i in range(7)]
            self.psT = Tile(E(nc.psum_tensor("psT", [P, 1024], BF16)).ap(), "psT")
            self.bankbuf = [Buf(f"bank{i}") for i in range(7)]
            self.DMA(self.c_bf.ap, self.dr["c_bf"], w=[self.c_bf])
            self.DMA(self.c_f32.ap, self.dr["c_f32"], w=[self.c_f32])
            self.DMA(self.gfin.ap, self.dr["final_g"].partition_broadcast(P), w=[self.gfin])
            self.ms("dve", self.eps_t.ap, EPS, w=[self.eps_t])
            self.S.fence()
            nl = len(self.layers)
            for li, layer in enumerate(self.layers):
                last = (li == nl - 1)
                first = (li == 0)
                if layer % 2 == 0:
                    self.even_layer(layer, first, last)
                else:
                    self.odd_layer(layer, first, last)
            stats = self.S.emit(nc, E)
            self.stats = stats
        return nc

    def ps(self, bank, off, width, name, parts=P):
        t = Tile(self.psb[bank][0:parts, off:off + width], name)
        t.bs = [self.bankbuf[bank]]
        return t

    def load_w_in(self, dram_w, ncols, layer):
        a16, a32 = self.a16, self.a32
        win = a16.alloc("win", (8, WMAX))
        self.DMA(self.gcol.ap, self.dr["norm_gT"][layer], w=[self.gcol])
        m32 = a32.off
        CH = 1032
        nch = (ncols + CH - 1) // CH
        stg = [a32.alloc(f"wstg{i}", (CH,)) for i in range(3)]
        k = 0
        for kc in range(8):
            for c in range(nch):
                c0 = c * CH
                cw = min(CH, ncols - c0)
                s = stg[k % 3]
                self.DMA(s[:, 0:cw], dram_w[kc * P:(kc + 1) * P, c0:c0 + cw], w=[s])
                eng = ("dve", "pool", "act")[k % 3]
                if eng == "act":
                    self.A("act", lambda e, s=s, kc=kc, c0=c0, cw=cw: e.mul(out=win[:, kc, c0:c0 + cw], in_=s[:, 0:cw],
                                                                        mul=self.gcol[:, kc:kc + 1]),
                           r=[s, self.gcol], w=[win])
                else:
                    self.ts(eng, win[:, kc, c0:c0 + cw], s[:, 0:cw], self.gcol[:, kc:kc + 1], r=[s, self.gcol], w=[win])
                k += 1
        a32.off = m32
        return win

    def load_w_out(self, dram_w, nchunks):
        a16, a32 = self.a16, self.a32
        wout = a16.alloc("wout", (nchunks, DM))
        m32 = a32.off
        stg = [a32.alloc(f"wostg{i}", (DM,)) for i in range(3)]
        for c in range(nchunks):
            s = stg[c % 3]
            self.DMA(s.ap, dram_w[c * P:(c + 1) * P, :], w=[s])
            self.cp(("dve", "pool", "act")[c % 3], wout[:, c, :], s.ap, r=[s], w=[wout])
        a32.off = m32
        return wout

    def alloc_xh(self):
        a16, a32 = self.a16, self.a32
        self.xt = [a32.alloc(f"xt{i}", (DM,)) for i in range(2)]
        self.hn = [a16.alloc(f"hn{i}", (DM,)) for i in range(2)]
        self.sqj = a16.alloc("sqj", (DM,))
        self.ssq = [a32.alloc(f"ssq{i}", (1,)) for i in range(2)]
        self.hT = [a16.alloc(f"hT{i}", (8, self.TB)) for i in range(2)]
        self.xcnt = 0

    def x_to_hT(self, xsrc, t0, bi):
        hT = self.hT[bi % 2]
        for j in range(self.NT):
            k = self.xcnt
            self.xcnt += 1
            xt, hn, ssq = self.xt[k % 2], self.hn[k % 2], self.ssq[k % 2]
            self.DMA(xt.ap, xsrc[t0 + j * P:t0 + (j + 1) * P, :], w=[xt])
            self.act(self.sqj.ap, xt.ap, AF.Square, r=[xt], w=[self.sqj, ssq], accum=ssq.ap)
            self.rsqrt_(ssq, DM)
            self.ts("dve", hn.ap, xt.ap, ssq[:, 0:1], r=[xt, ssq], w=[hn])
            yield
            for c in range(8):
                self.tr(self.psT[:, c * P:(c + 1) * P], hn[:, c * P:(c + 1) * P], r=[hn], w=[self.psT])
            self.cp("act" if j % 2 == 0 else "dve", hT[:, :, j * P:(j + 1) * P],
                    self.psT.ap.rearrange("p (c t) -> p c t", c=8), r=[self.psT], w=[hT])
            yield

    def proj_fm(self, win, hT, col0, m, pst):
        for kc in range(8):
            self.mm(pst.ap, win[:, kc, col0:col0 + m], hT[:, kc, :], start=(kc == 0), stop=(kc == 7),
                    r=[win, hT], w=[pst])

    def proj_tm(self, win, hT, j, col0, n, pst):
        for kc in range(8):
            self.mm(pst.ap, hT[:, kc, j * P:(j + 1) * P], win[:, kc, col0:col0 + n], start=(kc == 0), stop=(kc == 7),
                    r=[win, hT], w=[pst])

    def alloc_resid(self):
        a32 = self.a32
        self.xr = [a32.alloc(f"xr{i}", (DM,)) for i in range(2)]
        self.xo = [a32.alloc(f"xo{i}", (DM,)) for i in range(2)]
        self.fssq = [a32.alloc(f"fssq{i}", (1,)) for i in range(2)]
        self.fsq = self.a16.alloc("fsq", (DM,))
        self.rcnt = 0

    def resid_out(self, featT, nchunks, wout, j, xsrc, ydst, t, last, psA, psB):
        k = self.rcnt
        self.rcnt += 1
        xr, xo, fssq = self.xr[k % 2], self.xo[k % 2], self.fssq[k % 2]
        self.DMA(xr.ap, xsrc[t:t + P, :], w=[xr])
        for n, pst in enumerate((psA, psB)):
            for c in range(nchunks):
                self.mm(pst.ap, featT[:, c, j * P:(j + 1) * P], wout[:, c, n * 512:(n + 1) * 512],
                        start=(c == 0), stop=(c == nchunks - 1), r=[featT, wout], w=[pst])
            self.tt("dve", xo[:, n * 512:(n + 1) * 512], pst.ap, xr[:, n * 512:(n + 1) * 512], ALU.add,
                    r=[pst, xr], w=[xo])
            yield
        if last and self.final_norm:
            self.act(self.fsq.ap, xo.ap, AF.Square, r=[xo], w=[self.fsq, fssq], accum=fssq.ap)
            self.rsqrt_(fssq, DM)
            self.stt(xo.ap, xo.ap, fssq[:, 0:1], self.gfin.ap, ALU.mult, ALU.mult, r=[xo, fssq, self.gfin], w=[xo])
        self.DMA(ydst[t:t + P, :], xo.ap, r=[xo])
        yield

    def even_layer(self, layer, first, last):
        i = layer // 2
        a16, a32 = self.a16, self.a32
        dr = self.dr
        S = self.S
        TB, NT = self.TB, self.NT
        a16.off = 0
        a32.off = 0
        aup32 = a32.alloc("aup32", (2, 512))
        aup = a16.alloc("aup", (2, 512))
        self.DMA(aup32[0:17], dr["e_aup"][i].rearrange("d r c -> r d c"), w=[aup32])
        self.cp("dve", aup[0:17], aup32[0:17], r=[aup32], w=[aup])
        win = self.load_w_in(dr["e_w_in"][i], E_COLS, layer)
        S.fence()
        mW16, mW32 = a16.off, 0
        for tag, Sq in self.seqs:
            a16.off, a32.off = mW16, mW32
            self.even_fwd(tag, Sq, win, aup, first)
            S.fence()
        a16.off, a32.off = 0, 0
        gn = a32.alloc("gn_bc", (256,))
        self.DMA(gn.ap, dr["e_gn"][i].partition_broadcast(P), w=[gn])
        psc = a32.alloc("psc", (4,))
        self.DMA(psc.ap, dr["e_pscT"][i], w=[psc])
        mB32 = a32.off
        pw32 = a32.alloc("pw32", (4, P))
        pw = a16.alloc("pw", (4, P))
        self.DMA(pw32.ap, dr["e_pool_w"][i].rearrange("g c d -> c g d"), w=[pw32])
        self.cp("dve", pw.ap, pw32.ap, r=[pw32], w=[pw])
        wout = self.load_w_out(dr["e_w_out"][i], 12)
        S.fence()
        mB16 = a16.off
        for tag, Sq in self.seqs:
            a16.off, a32.off = mB16, mB32
            self.even_bwd(tag, Sq, wout, gn, pw, psc, first, last)
            S.fence()

    def gla_alloc(self):
        a16, a32 = self.a16, self.a32
        self.gE = [[a32.alloc(f"gE{b}{k}", (P,)) for k in range(3)] for b in range(2)]
        self.gq = [[a16.alloc(f"gq{b}{k}", (P,)) for k in range(4)] for b in range(2)]
        self.gS = a32.alloc("gS", (4, 256), nsub=4)
        self.gSb = a16.alloc("gSb", (4, 256), nsub=4)
        self.gps = []
        for b in range(2):
            small = 3 + 2 * b
            big = 4 + 2 * b
            self.gps.append(dict(bT=self.ps(small, 0, P, f"bT{b}"), cT=self.ps(small, P, P, f"cT{b}"),
                                 aT=self.ps(small, 2 * P, P, f"aT{b}"), o=self.ps(big, 0, 256, f"o{b}"),
                                 Pm=self.ps(big, 256, 256, f"Pm{b}")))
        self.gcnt = 0

    def gla_init_state(self):
        self.ms("dve", self.gS.ap, 0.0, w=[self.gS])
        self.ms("pool", self.gSb.ap, 0.0, w=[self.gSb])

    def gla_step(self, kk, h, sp, qT, kT, ktok, v, bwd, post):
        E1, E2, E3 = self.gE[kk]
        qeT, kdT, kl, aTm = self.gq[kk]
        g = self.gps[kk]
        cbf = self.c_bf
        TRI = cbf[:, 2, :] if bwd else cbf[:, 1, :]
        CL = cbf[:, 4, :] if bwd else cbf[:, 3, :]
        MSK = cbf[:, 3, :] if bwd else cbf[:, 1, :]
        Sb = self.gSb.bs[h]
        Sf = self.gS.bs[h]
        self.mm(g["bT"].ap, sp[0], TRI, r=[sp[1], cbf], w=[g["bT"]])
        self.mm(g["cT"].ap, CL, sp[0], r=[sp[1], cbf], w=[g["cT"]])
        yield
        self.act(E1.ap, g["bT"].ap, AF.Exp, r=[g["bT"]], w=[E1], scale=-1.0 / 16)
        self.act(E2.ap, g["bT"].ap, AF.Exp, r=[g["bT"]], w=[E2], scale=1.0 / 16)
        self.act(E3.ap, g["cT"].ap, AF.Exp, r=[g["cT"]], w=[E3], scale=-1.0 / 16)
        yield
        self.tt("pool", qeT.ap, qT[0], E1.ap, ALU.mult, r=[qT[1], E1], w=[qeT])
        self.tt("pool", kdT.ap, kT[0], E2.ap, ALU.mult, r=[kT[1], E2], w=[kdT])
        self.tt("pool", kl.ap, ktok[0], E3.ap, ALU.mult, r=[ktok[1], E3], w=[kl])
        yield
        self.mm(g["aT"].ap, kdT.ap, qeT.ap, r=[kdT, qeT], w=[g["aT"]])
        yield
        self.tt("dve", aTm.ap, g["aT"].ap, MSK, ALU.mult, r=[g["aT"], cbf], w=[aTm])
        yield
        self.mm(g["o"].ap, aTm.ap, v[0], start=True, stop=False, r=[aTm, v[1]], w=[g["o"]])
        self.mm(g["o"].ap, qeT.ap, self.gSb[:, h, :], start=False, stop=True, r=[qeT, Sb], w=[g["o"]])
        self.mm(g["Pm"].ap, kl.ap, v[0], r=[kl, v[1]], w=[g["Pm"]])
        yield
        lam = E1[:, 0:1] if bwd else E1[:, P - 1:P]
        self.stt(self.gS[:, h, :], self.gS[:, h, :], lam, g["Pm"].ap, ALU.mult, ALU.add, r=[Sf, E1, g["Pm"]], w=[Sf])
        self.cp("act", self.gSb[:, h, :], self.gS[:, h, :], r=[Sf], w=[Sb])
        if post is not None:
            yield from post(g["o"])
        else:
            yield

    def even_fwd(self, tag, Sq, win, aup, first):
        a16, a32, dr, TB, NT = self.a16, self.a32, self.dr, self.TB, self.NT
        xsrc = dr[f"x_{tag}"] if first else dr[f"y_{tag}"]
        self.alloc_xh()
        self.gla_alloc()
        fmq = [a16.alloc(f"fm_qT{b}", (4, TB), nsub=4) for b in range(2)]
        fmk = [a16.alloc(f"fm_kT{b}", (4, TB), nsub=4) for b in range(2)]
        fpu = a16.alloc("fm_puT", (4, TB), nsub=4)
        fpg = a16.alloc("fm_pgT", (4, TB), nsub=4)
        lrx = [a16.alloc(f"lrx{d}", (TB,)) for d in range(2)]
        NB4 = 2 * NT
        tk = [a16.alloc(f"tk{b}", (512,)) for b in range(NB4)]
        tspf = [a16.alloc(f"tspf{b}", (512,)) for b in range(NB4)]
        tv = [a16.alloc(f"tv{b}", (1024,), nsub=2) for b in range(NB4)]
        tspb = [a16.alloc(f"tspb{b}", (512,)) for b in range(2)]
        tgg = [a16.alloc(f"tgg{b}", (1024,), nsub=2) for b in range(2)]
        tof = [a32.alloc(f"tof{b}", (1024,), nsub=4) for b in range(2)]
        etmp = [a32.alloc(f"etmp{b}", (512,)) for b in range(2)]
        esil = [a32.alloc(f"esil{b}", (512,)) for b in range(2)]
        ps3 = [self.ps(k, 0, 512, f"ps3_{k}") for k in range(3)]
        for d in range(2):
            self.ms("pool", lrx[d][0:32], 1.0, w=[lrx[d]])
        self.gla_init_state()
        nb = Sq // TB
        st = dict(pc=0, tc=0)

        def nps():
            p = ps3[st["pc"] % 3]
            st["pc"] += 1
            return p

        def Pgen(bi):
            t0 = bi * TB
            bp = bi % 2
            hT = self.hT[bi % 2]
            yield from self.x_to_hT(xsrc, t0, bi)
            specs = [("qT", h, E_Q + h * P, fmq[bp]) for h in range(4)] + \
                    [("kT", h, E_K + h * P, fmk[bp]) for h in range(4)] + \
                    [("puT", h, E_PU + h * P, fpu) for h in range(4)] + \
                    [("pgT", h, E_PG + h * P, fpg) for h in range(4)]
            for (nm, h, col, dst) in specs:
                pf = nps()
                pst = Tile(pf[:, 0:TB], "pf")
                pst.bs = pf.bs
                self.proj_fm(win, hT, col, P, pst)
                if nm == "qT":
                    self.A("act", lambda e, o=dst[:, h, :], p=pst.ap: e.mul(out=o, in_=p, mul=float(P) ** -0.5),
                           r=[pst], w=[dst.bs[h]])
                elif nm == "pgT":
                    self.silu_ps(dst[:, h, :], [dst.bs[h]], pst, esil[h % 2], TB)
                else:
                    self.cp("dve", dst[:, h, :], pst.ap, r=[pst], w=[dst.bs[h]])
                yield
            for d in range(2):
                pf = nps()
                p16 = Tile(pf[0:16, 0:TB], "p16")
                p16.bs = pf.bs
                self.proj_fm(win, hT, E_LR + 16 * d, 16, p16)
                self.cp("dve", lrx[d][0:16, :], pf[0:16, 0:TB], r=[pf], w=[lrx[d]])
            yield
            for nm, src in (("qT", fmq[bp]), ("kT", fmk[bp]), ("puT", fpu), ("pgT", fpg)):
                self.DMA(dr[f"{nm}_{tag}"][:, :, t0:t0 + TB].rearrange("h p t -> p h t"), src.ap, r=[src])
            for j in range(NT):
                t = t0 + j * P
                b4 = (bi * NT + j) % NB4
                b2 = (bi * NT + j) % 2
                k_, spf_, v_, spb_, gg_ = tk[b4], tspf[b4], tv[b4], tspb[b2], tgg[b2]
                for d, dst in ((0, spf_), (1, spb_)):
                    pst = nps()
                    self.mm(pst.ap, lrx[d][0:17, j * P:(j + 1) * P], aup[0:17, d, :], r=[lrx[d], aup], w=[pst])
                    et = etmp[d]
                    self.act(et.ap, pst.ap, AF.Exp, r=[pst], w=[et], scale=-1.0)
                    self.act(dst.ap, et.ap, AF.Ln, r=[et], w=[dst], bias=1.0)
                    yield
                pst = nps()
                self.proj_tm(win, hT, j, E_K, 512, pst)
                self.cp("dve", k_.ap, pst.ap, r=[pst], w=[k_])
                yield
                for n in range(2):
                    pst = nps()
                    self.proj_tm(win, hT, j, E_V + n * 512, 512, pst)
                    self.cp("act", v_[:, n * 512:(n + 1) * 512], pst.ap, r=[pst], w=[v_.bs[n]])
                    yield
                for n in range(2):
                    pst = nps()
                    self.proj_tm(win, hT, j, E_G + n * 512, 512, pst)
                    self.silu_ps(gg_[:, n * 512:(n + 1) * 512], [gg_.bs[n]], pst, esil[n], 512)
                    yield
                self.DMA(dr[f"k_{tag}"][t:t + P, :], k_.ap, r=[k_])
                self.DMA(dr[f"spb_{tag}"][t:t + P, :], spb_.ap, r=[spb_])
                self.DMA(dr[f"v_{tag}"][t:t + P, :], v_.ap, r=[v_])
                self.DMA(dr[f"gg_{tag}"][t:t + P, :], gg_.ap, r=[gg_])

        def Sgen(bi, par):
            t0 = bi * TB
            bp = bi % 2
            for j in range(NT):
                t = t0 + j * P
                b4 = (bi * NT + j) % NB4
                b2 = (bi * NT + j) % 2
                k_, spf_, v_, of_ = tk[b4], tspf[b4], tv[b4], tof[b2]
                for h in (par, par + 2):
                    def post(po, h=h, of_=of_, t=t):
                        self.cp("dve", of_[:, h * 256:(h + 1) * 256], po.ap, r=[po], w=[of_.bs[h]])
                        self.DMA(dr[f"of_{tag}"][t:t + P, h * 256:(h + 1) * 256], of_[:, h * 256:(h + 1) * 256],
                                 r=[of_.bs[h]])
                        yield
                    yield from self.gla_step(par, h, (spf_[:, h * P:(h + 1) * P], spf_),
                                             (fmq[bp][:, h, j * P:(j + 1) * P], fmq[bp].bs[h]),
                                             (fmk[bp][:, h, j * P:(j + 1) * P], fmk[bp].bs[h]),
                                             (k_[:, h * P:(h + 1) * P], k_),
                                             (v_[:, h * 256:(h + 1) * 256], v_.bs[h // 2]), False, post)

        for bi in range(nb + 1):
            streams = []
            if bi >= 1:
                streams += [Sgen(bi - 1, 0), Sgen(bi - 1, 1)]
            if bi < nb:
                streams.append(Pgen(bi))
            interleave(streams)

    def even_bwd(self, tag, Sq, wout, gn, pw, psc, first, last):
        a16, a32, dr, TB, NT = self.a16, self.a32, self.dr, self.TB, self.NT
        xsrc = dr[f"x_{tag}"] if first else dr[f"y_{tag}"]
        ydst = dr[f"y_{tag}"]
        self.gla_alloc()
        self.alloc_resid()
        H = 8
        fmq = [a16.alloc(f"bq{b}", (4, TB), nsub=4) for b in range(2)]
        fmk = [a16.alloc(f"bk{b}", (4, TB), nsub=4) for b in range(2)]
        fpg = [a16.alloc(f"bpg{b}", (4, TB)) for b in range(2)]
        fpu = [a16.alloc(f"bpu{b}", (4, TB + 2 * H)) for b in range(2)]
        icn = [a32.alloc(f"icn{b}", (4, TB)) for b in range(2)]
        tk = [a16.alloc(f"tk{b}", (512,)) for b in range(3)]
        tspb = [a16.alloc(f"tspb{b}", (512,)) for b in range(3)]
        tv = [a16.alloc(f"tv{b}", (1024,)) for b in range(3)]
        tgg = [a16.alloc(f"tgg{b}", (1024,)) for b in range(3)]
        tof = [a32.alloc(f"tof{b}", (1024,)) for b in range(3)]
        osum = [a32.alloc(f"osum{b}", (256,)) for b in range(2)]
        osq = [a16.alloc(f"osq{b}", (256,)) for b in range(2)]
        hss = [a32.alloc(f"hss{b}", (1,)) for b in range(2)]
        otmp = [a32.alloc(f"otmp{b}", (256,)) for b in range(2)]
        glo = [a16.alloc(f"glo{b}", (1024,), nsub=4) for b in range(2)]
        featT = [a16.alloc(f"featT{b}", (12, TB), nsub=12) for b in range(2)]
        pt32 = [a32.alloc(f"pt32_{k}", (TB + 2 * H,)) for k in range(3)]
        pwin = a32.alloc("pwin", (TB,))
        pld = a16.alloc("pld", (TB,))
        psA, psB = self.ps(1, 0, 512, "psA"), self.ps(2, 0, 512, "psB")
        psX = self.ps(0, 0, TB, "psX")
        self.gla_init_state()
        nb = Sq // TB
        tiles = [(bi, j) for bi in range(nb - 1, -1, -1) for j in range(NT - 1, -1, -1)]

        def Lgen(bi):
            t0 = bi * TB
            bb = bi % 2
            q_, k_f, pg_, pu_, ic_, ft = fmq[bb], fmk[bb], fpg[bb], fpu[bb], icn[bb], featT[bb]
            self.DMA(q_.ap, dr[f"qT_{tag}"][:, :, t0:t0 + TB].rearrange("h p t -> p h t"), w=[q_])
            self.DMA(k_f.ap, dr[f"kT_{tag}"][:, :, t0:t0 + TB].rearrange("h p t -> p h t"), w=[k_f])
            self.DMA(pg_.ap, dr[f"pgT_{tag}"][:, :, t0:t0 + TB].rearrange("h p t -> p h t"), w=[pg_])
            lo = max(t0 - H, 0)
            hi = min(t0 + TB + H, Sq)
            if lo != t0 - H or hi != t0 + TB + H:
                self.ms("pool", pu_.ap, 0.0, w=[pu_])
            self.DMA(pu_[:, :, lo - (t0 - H):hi - (t0 - H)],
                     dr[f"puT_{tag}"][:, :, lo:hi].rearrange("h p t -> p h t"), w=[pu_])
            for g in range(4):
                self.DMA(ic_[:, g, :], dr[f"invcnt_{tag}"][g, t0:t0 + TB].partition_broadcast(P), w=[ic_])
            yield
            for g in range(4):
                W = TB + 2 * H
                hw = (1, 2, 4, 8)[g]
                if g == 0:
                    self.tt("pool", pwin.ap, pu_[:, g, H - 1:H - 1 + TB], pu_[:, g, H:H + TB], ALU.add, r=[pu_], w=[pwin])
                else:
                    step = 1
                    cur_ap = pu_[:, g, :]
                    cur_t = pu_
                    width = W
                    for lv in range(g):
                        dst = pt32[lv]
                        width = width - step
                        self.tt("pool", dst[:, 0:width], cur_ap[:, 0:width], cur_ap[:, step:step + width], ALU.add,
                                r=[cur_t], w=[dst])
                        cur_ap = dst.ap
                        cur_t = dst
                        step *= 2
                    self.tt("pool", pwin.ap, cur_ap[:, H - hw:H - hw + TB], cur_ap[:, H:H + TB], ALU.add, r=[cur_t], w=[pwin])
                yield
                self.tt("pool", pwin.ap, pwin.ap, ic_[:, g, :], ALU.mult, r=[pwin, ic_], w=[pwin])
                self.tt("pool", pld.ap, pwin.ap, pu_[:, g, H:H + TB], ALU.subtract, r=[pwin, pu_], w=[pld])
                yield
                self.mm(psX.ap, pw[:, g, :], pld.ap, r=[pw, pld], w=[psX])
                self.stt(ft[:, 8 + g, :], psX.ap, psc[:, g:g + 1], pg_[:, g, :], ALU.mult, ALU.mult,
                         r=[psX, psc, pg_], w=[ft.bs[8 + g]])
                yield

        def TLgen(n):
            bi, j = tiles[n]
            t = bi * TB + j * P
            tb = n % 3
            self.DMA(tk[tb].ap, dr[f"k_{tag}"][t:t + P, :], w=[tk[tb]])
            self.DMA(tspb[tb].ap, dr[f"spb_{tag}"][t:t + P, :], w=[tspb[tb]])
            self.DMA(tv[tb].ap, dr[f"v_{tag}"][t:t + P, :], w=[tv[tb]])
            self.DMA(tgg[tb].ap, dr[f"gg_{tag}"][t:t + P, :], w=[tgg[tb]])
            self.DMA(tof[tb].ap, dr[f"of_{tag}"][t:t + P, :], w=[tof[tb]])
            yield

        def mkpost(par, h, po, of_, gg_, gl):
            os_, hs_, ot_, oq_ = osum[par], hss[par], otmp[par], osq[par]
            self.tt("dve", os_.ap, po.ap, of_[:, h * 256:(h + 1) * 256], ALU.add, r=[po, of_], w=[os_])
            yield
            self.act(oq_.ap, os_.ap, AF.Square, r=[os_], w=[oq_, hs_], accum=hs_.ap)
            self.rsqrt_(hs_, 256)
            yield
            self.stt(ot_.ap, os_.ap, hs_[:, 0:1], gn.ap, ALU.mult, ALU.mult, r=[os_, hs_, gn], w=[ot_])
            yield
            self.tt("pool", gl[:, h * 256:(h + 1) * 256], ot_.ap, gg_[:, h * 256:(h + 1) * 256], ALU.mult,
                    r=[ot_, gg_], w=[gl.bs[h]])
            yield

        def Schain(par):
            pending = None
            po = self.gps[par]["o"]
            for n in range(len(tiles)):
                bi, j = tiles[n]
                bb = bi % 2
                tb = n % 3
                q_, k_f = fmq[bb], fmk[bb]
                k_, spb_, v_, gg_, of_, gl = tk[tb], tspb[tb], tv[tb], tgg[tb], tof[tb], glo[n % 2]
                yield ("tile", n)
                for h in (par, par + 2):
                    step = self.gla_step(par, h, (spb_[:, h * P:(h + 1) * P], spb_),
                                         (q_[:, h, j * P:(j + 1) * P], q_.bs[h]),
                                         (k_f[:, h, j * P:(j + 1) * P], k_f.bs[h]),
                                         (k_[:, h * P:(h + 1) * P], k_),
                                         (v_[:, h * 256:(h + 1) * 256], v_), True, None)
                    yield from zip2(step, pending)
                    pending = mkpost(par, h, po, of_, gg_, gl)
            if pending is not None:
                yield from pending

        def Rgen(n):
            bi, j = tiles[n]
            bb = bi % 2
            ft, gl = featT[bb], glo[n % 2]
            t = bi * TB + j * P
            for c in range(8):
                self.tr(self.psT[:, c * P:(c + 1) * P], gl[:, c * P:(c + 1) * P], r=[gl], w=[self.psT])
            self.cp("act", ft[:, 0:8, j * P:(j + 1) * P], self.psT.ap.rearrange("p (c t) -> p c t", c=8),
                    r=[self.psT], w=ft.bs[0:8])
            yield
            yield from self.resid_out(ft, 12, wout, j, xsrc, ydst, t, last, psA, psB)

        def Lfor(n):
            if n == 0 or tiles[n][0] != tiles[n - 1][0]:
                return Lgen(tiles[n][0])
            return None

        interleave([Lfor(0), TLgen(0)])
        run_pipeline(len(tiles), Schain, TLgen, Lfor, Rgen)

    def odd_layer(self, layer, first, last):
        i = layer // 2
        a16, a32, dr, S = self.a16, self.a32, self.dr, self.S
        a16.off = 0
        a32.off = 0
        ifb = a32.alloc("ifb", (16,))
        self.DMA(ifb.ap, dr["o_if_bias"][i].partition_broadcast(P), w=[ifb])
        mF32 = a32.off
        qg = a32.alloc("qg", (4,))
        kvg = a32.alloc("kvg", (4,))
        self.DMA(qg[:, 0:3], dr["o_qgT"][i], w=[qg])
        self.DMA(kvg[:, 0:2], dr["o_kvgT"][i], w=[kvg])
        qup32 = a32.alloc("qup32", (3, 768))
        kvup32 = a32.alloc("kvup32", (2, 1024))
        self.DMA(qup32.ap, dr["o_q_up"][i].rearrange("(c p) n -> p c n", p=P), w=[qup32])
        self.DMA(kvup32.ap, dr["o_kv_up"][i].rearrange("(c p) n -> p c n", p=P), w=[kvup32])
        qup = a16.alloc("qup", (3, 768))
        quprot = a16.alloc("quprot", (3, 768))
        kvk = a16.alloc("kvk", (2, 512))
        kvv = a16.alloc("kvv", (2, 512))
        wkrot = a16.alloc("wkrot", (8, 32))
        self.ms("pool", quprot.ap, 0.0, w=[quprot])
        for c in range(3):
            self.ts("dve", qup[:, c, :], qup32[:, c, :], qg[:, c:c + 1], r=[qup32, qg], w=[qup])
            qv = qup[:, c, :].rearrange("p (h r) -> p h r", h=8)
            rv = quprot[:, c, :].rearrange("p (h r) -> p h r", h=8)
            self.ts("pool", rv[:, :, 64:80], qv[:, :, 80:96], -1.0, r=[qup], w=[quprot])
            self.cp("pool", rv[:, :, 80:96], qv[:, :, 64:80], r=[qup], w=[quprot])
        for c in range(2):
            kv3 = kvup32[:, c, :].rearrange("p (h r) -> p h r", h=8)
            self.ts("dve", kvk[:, c, :].rearrange("p (h r) -> p h r", h=8), kv3[:, :, 0:64], kvg[:, c:c + 1],
                    r=[kvup32, kvg], w=[kvk])
            self.ts("dve", kvv[:, c, :].rearrange("p (h r) -> p h r", h=8), kv3[:, :, 64:128], kvg[:, c:c + 1],
                    r=[kvup32, kvg], w=[kvv])
        win = self.load_w_in(dr["o_w_in"][i], O_COLS, layer)
        self.ts("pool", wkrot[:, :, 0:16], win[:, :, O_KR + 16:O_KR + 32], -1.0, r=[win], w=[wkrot])
        self.cp("pool", wkrot[:, :, 16:32], win[:, :, O_KR:O_KR + 16], r=[win], w=[wkrot])
        S.fence()
        mW16 = a16.off
        import os as _os
        dbg = _os.environ.get("KDBG", "FMB")
        for tag, Sq in self.seqs:
            a16.off, a32.off = mW16, mF32
            if "F" in dbg:
                self.odd_fwd(tag, Sq, win, wkrot, qup, quprot, kvk, kvv, ifb, first)
            S.fence()
        for tag, Sq in self.seqs:
            a16.off, a32.off = 0, 0
            if "M" in dbg:
                self.mla_pass(tag, Sq)
            S.fence()
        if "B" not in dbg:
            return
        a16.off, a32.off = 0, 0
        gml = a32.alloc("gml_bc", (P,))
        self.DMA(gml.ap, dr["o_mlg"][i].partition_broadcast(P), w=[gml])
        mB32 = a32.off
        wout = self.load_w_out(dr["o_w_out"][i], 8)
        S.fence()
        mB16 = a16.off
        for tag, Sq in self.seqs:
            a16.off, a32.off = mB16, mB32
            self.odd_bwd(tag, Sq, wout, gml, first, last)
            S.fence()

    def mlstm_alloc(self):
        a16, a32 = self.a16, self.a32
        self.mq = [[a16.alloc(f"mq{b}{k}", (P,)) for k in range(3)] for b in range(2)]
        self.mC = a32.alloc("mC", (4, 130), nsub=4)
        self.mCb = a16.alloc("mCb", (4, 130), nsub=4)
        self.malpha = a32.alloc("malpha", (4,), nsub=4)
        self.mden = [a32.alloc(f"mden{b}", (2,)) for b in range(2)]
        self.mps = []
        for b in range(2):
            self.mps.append(dict(sT=self.ps(3 + 2 * b, 0, P, f"sT{b}"), nd=self.ps(4 + 2 * b, 0, 129, f"nd{b}"),
                                 Pm=self.ps(4 + 2 * b, 256, 129, f"mPm{b}")))
        self.shl = a16.alloc("shl", (2, 8))
        self.ones_bf = a16.alloc("ones_bf", (P,))
        self.ms("pool", self.ones_bf.ap, 1.0, w=[self.ones_bf])
        self.ms("dve", self.mC.ap, 0.0, w=[self.mC])
        self.ms("pool", self.mCb.ap, 0.0, w=[self.mCb])
        self.ms("dve", self.malpha.ap, 1.0, w=[self.malpha])

    def mlstm_step(self, kk, h, w, db, anext, qT, kT, ktok, vx, bwd, post):
        sTm, kw, qa = self.mq[kk]
        g = self.mps[kk]
        den = self.mden[kk]
        cbf = self.c_bf
        MSK = cbf[:, 3, :] if bwd else cbf[:, 1, :]
        al = self.malpha[:, h:h + 1]
        alb = self.malpha.bs[h]
        Cf, Cb = self.mC.bs[h], self.mCb.bs[h]
        self.mm(g["sT"].ap, kT[0], qT[0], r=[kT[1], qT[1]], w=[g["sT"]])
        self.ts("dve", kw.ap, ktok[0], w[0], r=[ktok[1], w[1]], w=[kw])
        self.ts("dve", qa.ap, qT[0], al, r=[qT[1], alb], w=[qa])
        yield
        self.stt(sTm.ap, g["sT"].ap, w[0], MSK, ALU.mult, ALU.mult, r=[g["sT"], w[1], cbf], w=[sTm])
        yield
        self.mm(g["nd"].ap, sTm.ap, vx[0], start=True, stop=False, r=[sTm, vx[1]], w=[g["nd"]])
        self.mm(g["nd"].ap, qa.ap, self.mCb[:, h, 0:129], start=False, stop=True, r=[qa, Cb], w=[g["nd"]])
        self.mm(g["Pm"].ap, kw.ap, vx[0], r=[kw, vx[1]], w=[g["Pm"]])
        yield
        self.cp("dve", den[:, 1:2], g["nd"][:, 128:129], r=[g["nd"]], w=[den])
        self.stt(self.mC[:, h, 0:129], self.mC[:, h, 0:129], al, g["Pm"].ap, ALU.mult, ALU.add,
                 r=[Cf, alb, g["Pm"]], w=[Cf])
        yield
        self.cp("act", self.mCb[:, h, 0:129], self.mC[:, h, 0:129], r=[Cf], w=[Cb])
        self.stt(den[:, 0:1], den[:, 1:2], -1.0, den[:, 1:2], ALU.mult, ALU.max, r=[den], w=[den])
        self.tt("dve", den[:, 0:1], den[:, 0:1], db[0], ALU.max, r=[den, db[1]], w=[den])
        self.A("dve", lambda e, o=den[:, 1:2], i_=den[:, 0:1]: e.reciprocal(out=o, in_=i_), r=[den], w=[den])
        self.cp("pool", al, anext[0], r=[anext[1]], w=[alb])
        yield
        if post is not None:
            yield from post(g["nd"], den)

    def odd_fwd(self, tag, Sq, win, wkrot, qup, quprot, kvk, kvv, ifb, first):
        a16, a32, dr, TB, NT = self.a16, self.a32, self.dr, self.TB, self.NT
        xsrc = dr[f"x_{tag}"] if first else dr[f"y_{tag}"]
        cbf = self.c_bf
        self.alloc_xh()
        self.mlstm_alloc()
        cq = {n: a16.alloc(f"cq{n}", (3, TB)) for n in ("T", "sq", "n")}
        ckv = {n: a16.alloc(f"ckv{n}", (2, TB)) for n in ("T", "sq", "n")}
        fmg = a16.alloc("fm_mgT", (4, TB), nsub=4)
        fKn = a16.alloc("fm_Kn", (4, TB), nsub=4)
        fmq = [a16.alloc(f"fm_mqT{b}", (4, TB), nsub=4) for b in range(2)]
        fmk = [a16.alloc(f"fm_mkT{b}", (4, TB), nsub=4) for b in range(2)]
        Qst = a16.alloc("Qst", (8, TB), nsub=8)
        KRst = a16.alloc("KRst", (TB,))
        rhl = a16.alloc("rhl", (2, TB))
        NB4 = 2 * NT
        tmk = [a16.alloc(f"tmk{b}", (512,)) for b in range(NB4)]
        tmv = [a16.alloc(f"tmv{b}", (4, 130)) for b in range(NB4)]
        wt = [a32.alloc(f"wt{b}", (24,)) for b in range(NB4)]
        tmog = [a16.alloc(f"tmog{b}", (512,)) for b in range(2)]
        tV = [a16.alloc(f"tV{b}", (512,)) for b in range(2)]
        thf = [a32.alloc(f"thf{b}", (512,), nsub=4) for b in range(2)]
        rrow = {n: a32.alloc(f"rrow{n}", (TB,)) for n in ("q", "kv")}
        rbc = {n: a32.alloc(f"rbc{n}", (TB,)) for n in ("q", "kv")}
        cosT = a32.alloc("cosT", (TB,))
        sinT = a32.alloc("sinT", (TB,))
        rt1 = a32.alloc("rt1", (TB,))
        rt2 = a32.alloc("rt2", (TB,))
        sig = a32.alloc("sig", (512,))
        sil = a32.alloc("sil", (512,))
        G = [a32.alloc(f"G{b}", (16,)) for b in range(2)]
        spm = [a32.alloc(f"spm{b}", (8,)) for b in range(2)]
        gbs = [a32.alloc(f"gbs{b}", (16,)) for b in range(2)]
        ps3 = [self.ps(k, 0, 512, f"ps3_{k}") for k in range(3)]
        for b in range(NB4):
            self.ms("pool", tmv[b][:, :, 128:130], 1.0, w=[tmv[b]])
        for b in range(2):
            self.ms("pool", gbs[b].ap, 0.0, w=[gbs[b]])
        ones_col = cbf[:, 1, P - 1:P]
        nb = Sq // TB
        st = dict(pc=0)

        def nps():
            p = ps3[st["pc"] % 3]
            st["pc"] += 1
            return p

        def sub(pf, p1, width, name="pfs"):
            t = Tile(pf[0:p1, 0:width], name)
            t.bs = pf.bs
            return t

        def Pgen(bi):
            t0 = bi * TB
            bp = bi % 2
            hT = self.hT[bi % 2]
            yield from self.x_to_hT(xsrc, t0, bi)
            for rows in ((0, 32), (64, 96)):
                self.DMA(cosT[rows[0]:rows[1], :], dr["ropeT"][0, :, t0:t0 + TB], w=[cosT])
                self.DMA(sinT[rows[0]:rows[1], :], dr["ropeT"][1, :, t0:t0 + TB], w=[sinT])
            for (nm, col, nch, dim, T3) in (("q", O_CQ, 3, 384, cq), ("kv", O_CKV, 2, 256, ckv)):
                for c in range(nch):
                    pst = sub(nps(), P, TB)
                    self.proj_fm(win, hT, col + c * P, P, pst)
                    self.cp("dve", T3["T"][:, c, :], pst.ap, r=[pst], w=[T3["T"]])
                    self.tt("pool", T3["sq"][:, c, :], T3["T"][:, c, :], T3["T"][:, c, :], ALU.mult, r=[T3["T"]],
                            w=[T3["sq"]])
                    yield
                pst = nps()
                for c in range(nch):
                    self.mm(pst[0:1, 0:TB], ones_col, T3["sq"][:, c, :], start=(c == 0), stop=(c == nch - 1),
                            r=[T3["sq"], cbf], w=[pst])
                rr = rrow[nm]
                self.cp("dve", rr[0:1, :], pst[0:1, 0:TB], r=[pst], w=[rr])
                self.rsqrt_(rr, dim, 0, 1)
                self.cp("dve", rhl[0:1, 0, :], rr[0:1, :], r=[rr], w=[rhl])
                self.tt("dve", rhl[0:1, 1, :], rr[0:1, :], rhl[0:1, 0, :], ALU.subtract, r=[rr, rhl], w=[rhl])
                yield
                pst = nps()
                for k2 in range(2):
                    self.mm(pst[:, 0:TB], cbf[0:1, 1, :], rhl[0:1, k2, :], start=(k2 == 0), stop=(k2 == 1),
                            r=[cbf, rhl], w=[pst])
                self.cp("act", rbc[nm].ap, pst[:, 0:TB], r=[pst], w=[rbc[nm]])
                yield
                for c in range(nch):
                    self.tt("pool", T3["n"][:, c, :], T3["T"][:, c, :], rbc[nm].ap, ALU.mult, r=[T3["T"], rbc[nm]],
                            w=[T3["n"]])
                yield
            pa = nps()
            pb = nps()
            self.proj_fm(win, hT, O_KR, 32, sub(pa, 32, TB))
            for kc in range(8):
                self.mm(pb[0:32, 0:TB], wkrot[:, kc, :], hT[:, kc, :], start=(kc == 0), stop=(kc == 7),
                        r=[wkrot, hT], w=[pb])
            self.tt("dve", rt1[0:32, :], pa[0:32, 0:TB], cosT[0:32, :], ALU.mult, r=[pa, cosT], w=[rt1])
            self.tt("dve", rt2[0:32, :], pb[0:32, 0:TB], sinT[0:32, :], ALU.mult, r=[pb, sinT], w=[rt2])
            self.tt("pool", KRst[0:32, :], rt1[0:32, :], rt2[0:32, :], ALU.add, r=[rt1, rt2], w=[KRst])
            self.DMA(dr[f"KR_{tag}"][:, t0:t0 + TB], KRst[0:32, :], r=[KRst])
            yield
            for (dst, col, mode) in ((fmg, O_MG, "silu"), (fmq[bp], O_MQ, "copy"), (fmk[bp], O_MK, "scale")):
                for h in range(4):
                    pst = sub(nps(), P, TB)
                    self.proj_fm(win, hT, col + h * P, P, pst)
                    if mode == "silu":
                        self.silu_ps(dst[:, h, :], [dst.bs[h]], pst, sig if h % 2 == 0 else sil, TB)
                    elif mode == "copy":
                        self.cp("dve", dst[:, h, :], pst.ap, r=[pst], w=[dst.bs[h]])
                    else:
                        self.A("act", lambda e, o=dst[:, h, :], p=pst.ap: e.mul(out=o, in_=p, mul=float(P) ** -0.5),
                               r=[pst], w=[dst.bs[h]])
                    yield
            for h in range(8):
                pa = nps()
                pb = nps()
                for c in range(3):
                    self.mm(pa[0:96, 0:TB], qup[:, c, h * 96:(h + 1) * 96], cq["n"][:, c, :], start=(c == 0),
                            stop=(c == 2), r=[qup, cq["n"]], w=[pa])
                for c in range(3):
                    self.mm(pb[0:96, 0:TB], quprot[:, c, h * 96:(h + 1) * 96], cq["n"][:, c, :], start=(c == 0),
                            stop=(c == 2), r=[quprot, cq["n"]], w=[pb])
                self.cp("dve", Qst[0:64, h, :], pa[0:64, 0:TB], r=[pa], w=[Qst.bs[h]])
                self.tt("dve", rt1[64:96, :], pa[64:96, 0:TB], cosT[64:96, :], ALU.mult, r=[pa, cosT], w=[rt1])
                self.tt("dve", rt2[64:96, :], pb[64:96, 0:TB], sinT[64:96, :], ALU.mult, r=[pb, sinT], w=[rt2])
                self.tt("pool", Qst[64:96, h, :], rt1[64:96, :], rt2[64:96, :], ALU.add, r=[rt1, rt2], w=[Qst.bs[h]])
                yield
            for pr in range(4):
                pst = nps()
                for c in range(2):
                    self.mm(pst[:, 0:TB], kvk[:, c, pr * P:(pr + 1) * P], ckv["n"][:, c, :], start=(c == 0), stop=(c == 1),
                            r=[kvk, ckv["n"]], w=[pst])
                self.cp("dve", fKn[:, pr, :], pst[:, 0:TB], r=[pst], w=[fKn.bs[pr]])
                yield
            self.DMA(dr[f"Q_{tag}"][:, :, t0:t0 + TB].rearrange("h r t -> r h t"), Qst[0:96], r=[Qst])
            for nm, src in (("Kn", fKn), ("mgT", fmg), ("mqT", fmq[bp]), ("mkT", fmk[bp])):
                self.DMA(dr[f"{nm}_{tag}"][:, :, t0:t0 + TB].rearrange("h p t -> p h t"), src.ap, r=[src])
            for j in range(NT):
                t = t0 + j * P
                b4 = (bi * NT + j) % NB4
                b2 = (bi * NT + j) % 2
                mk_, mv_, wt_ = tmk[b4], tmv[b4], wt[b4]
                mog_, V_, G_, spm_, gbs_ = tmog[b2], tV[b2], G[b2], spm[b2], gbs[b2]
                pst = nps()
                self.proj_tm(win, hT, j, O_MK, 512, pst)
                self.A("act", lambda e, o=mk_.ap, p=pst.ap: e.mul(out=o, in_=p, mul=float(P) ** -0.5), r=[pst], w=[mk_])
                yield
                pst = nps()
                self.proj_tm(win, hT, j, O_MV, 512, pst)
                self.cp("dve", mv_[:, :, 0:128], pst.ap.rearrange("p (h d) -> p h d", h=4), r=[pst], w=[mv_])
                yield
                pst = nps()
                self.proj_tm(win, hT, j, O_MO, 512, pst)
                self.act(sig.ap, pst.ap, AF.Exp, r=[pst], w=[sig], scale=-1.0)
                self.act(sig.ap, sig.ap, AF.Ln, r=[sig], w=[sig], bias=1.0)
                yield
                pst = nps()
                self.proj_tm(win, hT, j, O_MLG, 512, pst)
                self.act(sil.ap, pst.ap, AF.Exp, r=[pst], w=[sil], scale=-1.0)
                self.act(sil.ap, sil.ap, AF.Ln, r=[sil], w=[sil], bias=1.0)
                self.tt("pool", sig.ap, sig.ap, sil.ap, ALU.add, r=[sig, sil], w=[sig])
                self.act(sig.ap, sig.ap, AF.Exp, r=[sig], w=[sig], scale=-1.0)
                self.tt("dve", mog_.ap, pst.ap, sig.ap, ALU.mult, r=[pst, sig], w=[mog_])
                yield
                pst = nps()
                for c in range(2):
                    self.mm(pst.ap, ckv["n"][:, c, j * P:(j + 1) * P], kvv[:, c, :], start=(c == 0), stop=(c == 1),
                            r=[ckv["n"], kvv], w=[pst])
                self.cp("act", V_.ap, pst.ap, r=[pst], w=[V_])
                yield
                pst = nps()
                p16 = Tile(pst[:, 0:16], "p16")
                p16.bs = pst.bs
                self.proj_tm(win, hT, j, O_IF, 16, p16)
                self.tt("dve", G_.ap, pst[:, 0:16], ifb.ap, ALU.add, r=[pst, ifb], w=[G_])
                yield
                self.gates(G_, spm_, wt_, nps())
                self.cp("pool", gbs_[:, 0:4], wt_[:, 4:8], r=[wt_], w=[gbs_])
                self.cp("pool", gbs_[:, 4:8], wt_[:, 12:16], r=[wt_], w=[gbs_])
                self.cp("pool", gbs_[:, 8:12], wt_[:, 20:24], r=[wt_], w=[gbs_])
                yield
                self.DMA(dr[f"mk_{tag}"][t:t + P, :], mk_.ap, r=[mk_])
                self.DMA(dr[f"mv_{tag}"][t:t + P, :].rearrange("p (h d) -> p h d", h=4), mv_[:, :, 0:128], r=[mv_])
                self.DMA(dr[f"mog_{tag}"][t:t + P, :], mog_.ap, r=[mog_])
                self.DMA(dr[f"V_{tag}"][t:t + P, :], V_.ap, r=[V_])
                self.DMA(dr[f"gb_{tag}"][t:t + P, :], gbs_.ap, r=[gbs_])

        def Sgen(bi, par):
            t0 = bi * TB
            bp = bi % 2
            for j in range(NT):
                t = t0 + j * P
                b4 = (bi * NT + j) % NB4
                b2 = (bi * NT + j) % 2
                mk_, mv_, wt_, hf_ = tmk[b4], tmv[b4], wt[b4], thf[b2]
                for h in (par, par + 2):
                    def post(nd, den, h=h, hf_=hf_, t=t):
                        self.ts("dve", hf_[:, h * P:(h + 1) * P], nd[:, 0:P], den[:, 1:2], r=[nd, den], w=[hf_.bs[h]])
                        self.DMA(dr[f"hf_{tag}"][t:t + P, h * P:(h + 1) * P], hf_[:, h * P:(h + 1) * P], r=[hf_.bs[h]])
                        yield
                    yield from self.mlstm_step(par, h, (wt_[:, h:h + 1], wt_), (wt_[:, 8 + h:9 + h], wt_),
                                               (wt_[:, 16 + h:17 + h], wt_),
                                               (fmq[bp][:, h, j * P:(j + 1) * P], fmq[bp].bs[h]),
                                               (fmk[bp][:, h, j * P:(j + 1) * P], fmk[bp].bs[h]),
                                               (mk_[:, h * P:(h + 1) * P], mk_),
                                               (mv_[:, h, 0:129], mv_), False, post)

        for bi in range(nb + 1):
            streams = []
            if bi >= 1:
                streams += [Sgen(bi - 1, 0), Sgen(bi - 1, 1)]
            if bi < nb:
                streams.append(Pgen(bi))
            interleave(streams)

    def gates(self, G_, spm_, wt_, gp):
        cf32 = self.c_f32
        self.act(spm_.ap, G_[:, 8:16], AF.Exp, r=[G_], w=[spm_], scale=-1.0)
        self.act(spm_.ap, spm_.ap, AF.Ln, r=[spm_], w=[spm_], bias=1.0)
        cbf = self.c_bf
        shl = self.shl
        self.cp("dve", shl[:, 0, :], spm_.ap, r=[spm_], w=[shl])
        self.tt("dve", shl[:, 1, :], spm_.ap, shl[:, 0, :], ALU.subtract, r=[spm_, shl], w=[shl])
        for k2 in range(2):
            self.mm(gp[:, 0:4], cbf[:, 1, :], shl[:, k2, 0:4], start=(k2 == 0), stop=(k2 == 1), r=[cbf, shl], w=[gp])
        for k2 in range(2):
            self.mm(gp[:, 4:8], cbf[:, 2, :], shl[:, k2, 4:8], start=(k2 == 0), stop=(k2 == 1), r=[cbf, shl], w=[gp])
        for k2 in range(2):
            self.mm(gp[:, 8:16], self.ones_bf.ap, shl[:, k2, 0:8], start=(k2 == 0), stop=(k2 == 1),
                    r=[self.ones_bf, shl], w=[gp])
        self.cp("dve", wt_[:, 8:24], gp[:, 0:16], r=[gp], w=[wt_])
        self.tt("dve", wt_[:, 0:8], wt_[:, 8:16], G_[:, 0:8], ALU.add, r=[wt_, G_], w=[wt_])
        self.act(wt_[:, 0:16], wt_[:, 0:16], AF.Exp, r=[wt_], w=[wt_])
        self.act(wt_[:, 16:24], wt_[:, 16:24], AF.Exp, r=[wt_], w=[wt_], scale=-1.0)

    def mla_pass(self, tag, Sq):
        a16, a32, dr = self.a16, self.a32, self.dr
        cf32 = self.c_f32
        NK = Sq // P
        QB = 512
        KT = [a16.alloc(f"KT{b}", (Sq,)) for b in range(2)]
        QT = [a16.alloc(f"QT{b}", (Sq,)) for b in range(2)]
        Vh = [a16.alloc(f"Vh{b}", (NK, 66)) for b in range(2)]
        PT = [a16.alloc(f"PT{b}", (1024,)) for b in range(3)]
        ATs = [a16.alloc(f"ATs{b}", (QB,)) for b in range(2)]
        rhl = a16.alloc("mrhl", (2, QB))
        rrow = a32.alloc("mrrow", (QB,))
        bcs = a32.alloc("mbcs", (QB,))
        sT = []
        for g in range(2):
            t = Tile(self.psall[:, g * 1024:(g + 1) * 1024], f"msT{g}")
            t.bs = [self.bankbuf[2 * g], self.bankbuf[2 * g + 1]]
            sT.append(t)
        oT = [self.ps(4, 0, QB, "oT0"), self.ps(5, 0, QB, "oT1")]
        bcp = self.ps(6, 0, QB, "bcp")
        for b in range(2):
            self.ms("pool", Vh[b][:, :, 64:66], 1.0, w=[Vh[b]])
        scale = 96.0 ** -0.5
        qcnt = 0
        gcnt = 0
        pcnt = 0
        ngr = NK // 2
        for h in range(8):
            kb = h % 2
            K_, Q_, V_ = KT[kb], QT[kb], Vh[kb]
            r0 = (h % 2) * 64
            self.DMA(K_[0:64, :], dr[f"Kn_{tag}"][h // 2, r0:r0 + 64, :], w=[K_])
            self.DMA(K_[64:96, :], dr[f"KR_{tag}"][:, :], w=[K_])
            self.DMA(Q_[0:96, :], dr[f"Q_{tag}"][h], w=[Q_])
            self.DMA(V_[:, :, 0:64], dr[f"V_{tag}"][:, h * 64:(h + 1) * 64].rearrange("(n p) c -> p n c", p=P), w=[V_])
            for qb in range(Sq // QB):
                o_ = oT[qcnt % 2]
                at_ = ATs[qcnt % 2]
                qcnt += 1
                qs = Q_[0:96, qb * QB:(qb + 1) * QB]

                def qk(kg, sg):
                    for n in range(2):
                        kt = 2 * kg + n
                        self.mm(sg[:, n * 512:(n + 1) * 512], K_[0:96, kt * P:(kt + 1) * P], qs, r=[K_, Q_], w=[sg])
                def pv(kg, pt):
                    for n in range(2):
                        kt = 2 * kg + n
                        self.mm(o_[0:65, :], V_[:, kt, 0:65], pt[:, n * 512:(n + 1) * 512], start=(kt == 0),
                                stop=(kt == NK - 1), r=[V_, pt], w=[o_])
                qk(0, sT[gcnt % 2])
                prev = None
                for kg in range(ngr):
                    sg = sT[gcnt % 2]
                    gcnt += 1
                    pt = PT[pcnt % 3]
                    pcnt += 1
                    if kg + 1 < ngr:
                        qk(kg + 1, sT[gcnt % 2])
                    self.act(pt.ap, sg.ap, AF.Exp, r=[sg], w=[pt], scale=scale)
                    if prev is not None:
                        pv(*prev)
                    prev = (kg, pt)
                pv(*prev)
                self.A("dve", lambda e, o=rrow[64:65, :], i_=o_[64:65, :]: e.reciprocal(out=o, in_=i_), r=[o_], w=[rrow])
                self.cp("dve", rhl[64:65, 0, :], rrow[64:65, :], r=[rrow], w=[rhl])
                self.tt("dve", rhl[64:65, 1, :], rrow[64:65, :], rhl[64:65, 0, :], ALU.subtract, r=[rrow, rhl], w=[rhl])
                for k2 in range(2):
                    self.mm(bcp[0:64, :], self.c_bf[64:65, 2, 0:64], rhl[64:65, k2, :], start=(k2 == 0), stop=(k2 == 1),
                            r=[self.c_bf, rhl], w=[bcp])
                self.cp("act", bcs[0:64, :], bcp[0:64, :], r=[bcp], w=[bcs])
                self.tt("dve", at_[0:64, :], o_[0:64, :], bcs[0:64, :], ALU.mult, r=[o_, bcs], w=[at_])
                self.DMA(dr[f"AT_{tag}"][h // 2, r0:r0 + 64, qb * QB:(qb + 1) * QB], at_[0:64, :], r=[at_])

    def odd_bwd(self, tag, Sq, wout, gml, first, last):
        a16, a32, dr, TB, NT = self.a16, self.a32, self.dr, self.TB, self.NT
        xsrc = dr[f"x_{tag}"] if first else dr[f"y_{tag}"]
        ydst = dr[f"y_{tag}"]
        self.mlstm_alloc()
        self.alloc_resid()
        fmq = [a16.alloc(f"bmq{b}", (4, TB), nsub=4) for b in range(2)]
        fmk = [a16.alloc(f"bmk{b}", (4, TB), nsub=4) for b in range(2)]
        fmg = [a16.alloc(f"bmg{b}", (4, TB)) for b in range(2)]
        fat = [a16.alloc(f"bat{b}", (4, TB)) for b in range(2)]
        tmk = [a16.alloc(f"tmk{b}", (512,)) for b in range(3)]
        tmv = [a16.alloc(f"tmv{b}", (4, 130)) for b in range(3)]
        tmog = [a16.alloc(f"tmog{b}", (512,)) for b in range(3)]
        tgb = [a32.alloc(f"tgb{b}", (16,)) for b in range(3)]
        thf = [a32.alloc(f"thf{b}", (512,)) for b in range(3)]
        hsum = [a32.alloc(f"hsum{b}", (P,)) for b in range(2)]
        hsq = [a16.alloc(f"hsq{b}", (P,)) for b in range(2)]
        hss = [a32.alloc(f"hss{b}", (1,)) for b in range(2)]
        otmp = [a32.alloc(f"otmp{b}", (P,)) for b in range(2)]
        mlo = [a16.alloc(f"mlo{b}", (512,), nsub=4) for b in range(2)]
        featT = [a16.alloc(f"featT{b}", (8, TB), nsub=8) for b in range(2)]
        psA, psB = self.ps(1, 0, 512, "psA"), self.ps(2, 0, 512, "psB")
        for b in range(3):
            self.ms("pool", tmv[b][:, :, 128:130], 1.0, w=[tmv[b]])
        nb = Sq // TB
        tiles = [(bi, j) for bi in range(nb - 1, -1, -1) for j in range(NT - 1, -1, -1)]

        def Lgen(bi):
            t0 = bi * TB
            bb = bi % 2
            q_, k_f, mg_, at_, ft = fmq[bb], fmk[bb], fmg[bb], fat[bb], featT[bb]
            for (dst, nm) in ((q_, "mqT"), (k_f, "mkT"), (mg_, "mgT"), (at_, "AT")):
                self.DMA(dst.ap, dr[f"{nm}_{tag}"][:, :, t0:t0 + TB].rearrange("h p t -> p h t"), w=[dst])
            yield
            self.tt("pool", ft[:, 0:4, :], at_.ap, mg_.ap, ALU.mult, r=[at_, mg_], w=ft.bs[0:4])
            yield

        def TLgen(n):
            bi, j = tiles[n]
            t = bi * TB + j * P
            tb = n % 3
            self.DMA(tmk[tb].ap, dr[f"mk_{tag}"][t:t + P, :], w=[tmk[tb]])
            self.DMA(tmv[tb][:, :, 0:128], dr[f"mv_{tag}"][t:t + P, :].rearrange("p (h d) -> p h d", h=4), w=[tmv[tb]])
            self.DMA(tmog[tb].ap, dr[f"mog_{tag}"][t:t + P, :], w=[tmog[tb]])
            self.DMA(tgb[tb].ap, dr[f"gb_{tag}"][t:t + P, :], w=[tgb[tb]])
            self.DMA(thf[tb].ap, dr[f"hf_{tag}"][t:t + P, :], w=[thf[tb]])
            yield

        def mkpost(par, h, nd, den, hf_, mog_, ml):
            hs_, ss_, ot_, hq_ = hsum[par], hss[par], otmp[par], hsq[par]
            self.stt(hs_.ap, nd[:, 0:P], den[:, 1:2], hf_[:, h * P:(h + 1) * P], ALU.mult, ALU.add,
                     r=[nd, den, hf_], w=[hs_])
            yield
            self.act(hq_.ap, hs_.ap, AF.Square, r=[hs_], w=[hq_, ss_], accum=ss_.ap)
            self.rsqrt_(ss_, P)
            yield
            self.stt(ot_.ap, hs_.ap, ss_[:, 0:1], gml.ap, ALU.mult, ALU.mult, r=[hs_, ss_, gml], w=[ot_])
            yield
            self.tt("pool", ml[:, h * P:(h + 1) * P], ot_.ap, mog_[:, h * P:(h + 1) * P], ALU.mult,
                    r=[ot_, mog_], w=[ml.bs[h]])
            yield

        def Schain(par):
            pending = None
            nd, den = self.mps[par]["nd"], self.mden[par]
            for n in range(len(tiles)):
                bi, j = tiles[n]
                bb = bi % 2
                tb = n % 3
                q_, k_f = fmq[bb], fmk[bb]
                mk_, mv_, mog_, gb_, hf_, ml = tmk[tb], tmv[tb], tmog[tb], tgb[tb], thf[tb], mlo[n % 2]
                yield ("tile", n)
                for h in (par, par + 2):
                    step = self.mlstm_step(par, h, (gb_[:, h:h + 1], gb_), (gb_[:, 4 + h:5 + h], gb_),
                                           (gb_[:, 8 + h:9 + h], gb_),
                                           (q_[:, h, j * P:(j + 1) * P], q_.bs[h]),
                                           (k_f[:, h, j * P:(j + 1) * P], k_f.bs[h]),
                                           (mk_[:, h * P:(h + 1) * P], mk_),
                                           (mv_[:, h, 0:129], mv_), True, None)
                    yield from zip2(step, pending)
                    pending = mkpost(par, h, nd, den, hf_, mog_, ml)
            if pending is not None:
                yield from pending

        def Rgen(n):
            bi, j = tiles[n]
            bb = bi % 2
            ft, ml = featT[bb], mlo[n % 2]
            t = bi * TB + j * P
            for c in range(4):
                self.tr(self.psT[:, c * P:(c + 1) * P], ml[:, c * P:(c + 1) * P], r=[ml], w=[self.psT])
            self.cp("act", ft[:, 4:8, j * P:(j + 1) * P],
                    self.psT[:, 0:512].rearrange("p (c t) -> p c t", c=4), r=[self.psT], w=ft.bs[4:8])
            yield
            yield from self.resid_out(ft, 8, wout, j, xsrc, ydst, t, last, psA, psB)

        def Lfor(n):
            if n == 0 or tiles[n][0] != tiles[n - 1][0]:
                return Lgen(tiles[n][0])
            return None

        interleave([Lfor(0), TLgen(0)])
        run_pipeline(len(tiles), Schain, TLgen, Lfor, Rgen)


def make_consts(smax):
    j = np.arange(P)[:, None]
    i = np.arange(P)[None, :]
    ident = (j == i)
    TL = (j <= i)
    TG = (j >= i)
    SG = (j > i)
    SL = (j < i)
    c_bf = np.stack([ident, TL, TG, SG, SL], axis=1).astype(np.float32).astype(ml_dtypes.bfloat16)
    c_f32 = np.stack([TL, TG, np.ones((P, P), bool)], axis=1).astype(np.float32)
    inv = (np.float32(10000.0) ** (-np.arange(0, 32, 2, dtype=np.float32) / np.float32(32))).astype(np.float32)
    ang = (np.arange(smax, dtype=np.float32)[:, None] * inv[None, :]).astype(np.float32)
    cos = np.cos(ang).astype(np.float32).T
    sin = np.sin(ang).astype(np.float32).T
    ropeT = np.stack([np.concatenate([cos, cos], 0), np.concatenate([sin, sin], 0)], 0)
    return c_bf, c_f32, np.ascontiguousarray(ropeT)


def make_invcnt(S):
    pos = np.arange(S)
    out = np.zeros((4, S), np.float32)
    for gi, w in enumerate((2, 4, 8, 16)):
        lo = np.clip(pos - w // 2, 0, S)
        hi = np.clip(pos + w // 2, 0, S)
        out[gi] = 1.0 / (hi - lo).astype(np.float32)
    return out


def prep_weights(w):
    f = lambda a: np.ascontiguousarray(np.asarray(a, dtype=np.float32))
    out = {}
    out["norm_gT"] = f(np.asarray(w["norm_g"]).reshape(-1, 8, P).transpose(0, 2, 1))
    out["final_g"] = f(w["final_norm_g"])
    out["e_w_in"] = f(w["e_w_in"])
    out["e_aup"] = f(np.concatenate([np.asarray(w["e_gla_a_up"]), np.asarray(w["e_gla_a_bias"])[:, :, None, :]], axis=2))
    out["e_gn"] = f(w["e_gla_norm_g"])
    out["e_pool_w"] = f(w["e_pool_w"])
    out["e_pscT"] = f(np.asarray(w["e_pool_scale"]).reshape(-1, 4, P).transpose(0, 2, 1))
    out["e_w_out"] = f(w["e_w_out"])
    out["o_w_in"] = f(w["o_w_in"])
    out["o_qgT"] = f(np.asarray(w["o_q_norm_g"]).reshape(-1, 3, P).transpose(0, 2, 1))
    out["o_q_up"] = f(w["o_q_up"])
    out["o_kvgT"] = f(np.asarray(w["o_kv_norm_g"]).reshape(-1, 2, P).transpose(0, 2, 1))
    out["o_kv_up"] = f(w["o_kv_up"])
    out["o_if_bias"] = f(w["o_if_bias"])
    out["o_mlg"] = f(w["o_mlstm_norm_g"])
    out["o_w_out"] = f(w["o_w_out"])
    return out


_CACHE = {}


def run_trunk(seq_inputs, weights, layers=(0, 1, 2, 3), final_norm=True, TB=256, n_cores=8):
    seqs = [(tag, a.shape[0]) for tag, a in seq_inputs[0].items()]
    key = (tuple(seqs), tuple(layers), final_norm, TB)
    if key not in _CACHE:
        b = Builder(seqs, layers, TB, final_norm)
        b.build()
        _CACHE[key] = b
    b = _CACHE[key]
    c_bf, c_f32, ropeT = make_consts(b.smax)
    wp = prep_weights(weights)
    in_maps = []
    for c in range(n_cores):
        m = dict(wp)
        m["c_bf"], m["c_f32"], m["ropeT"] = c_bf, c_f32, ropeT
        for tag, a in seq_inputs[c].items():
            m[f"x_{tag}"] = np.ascontiguousarray(a, dtype=np.float32)
            m[f"invcnt_{tag}"] = make_invcnt(a.shape[0])
        in_maps.append(m)
    res = run_bass_kernel_spmd(b.nc, in_maps, core_ids=list(range(n_cores)))
    return [{tag: r[f"y_{tag}"] for tag, _ in seqs} for r in res.results]


def kernel(x_prompt, x_sample, **weights):
    x_prompt = np.asarray(x_prompt, dtype=np.float32)
    x_sample = np.asarray(x_sample, dtype=np.float32)
    nb_p = x_prompt.shape[0]
    seq_inputs = []
    for c in range(8):
        seq_inputs.append({"s": x_sample[c], "p": x_prompt[c % nb_p]})
    outs = run_trunk(seq_inputs, weights)
    y_sample = np.stack([outs[c]["s"] for c in range(8)], axis=0)
    y_prompt = np.stack([outs[c]["p"] for c in range(nb_p)], axis=0)
    return (y_prompt, y_sample)
```

```python
import numpy as np
import ml_dtypes
from contextlib import ExitStack
import concourse.bass as bass
import concourse.mybir as mybir
from concourse.bass_utils import run_bass_kernel_spmd

F32 = mybir.dt.float32
BF16 = mybir.dt.bfloat16
ALU = mybir.AluOpType
AF = mybir.ActivationFunctionType

ENGS = ("pe", "act", "dve", "pool", "sp")
SEM_WRAP = 30000
P = 128
DM = 1024
EPS = 1e-6


class Buf:
    __slots__ = ("name", "writers", "readers")

    def __init__(self, name):
        self.name = name
        self.writers = []
        self.readers = []


class Instr:
    __slots__ = ("eng", "fn", "deps", "signal", "count", "semidx", "is_dma", "dma_slot", "dma_target", "dma_prev")

    def __init__(self, eng, fn, is_dma):
        self.eng = eng
        self.fn = fn
        self.deps = []
        self.signal = False
        self.count = 0
        self.semidx = 0
        self.is_dma = is_dma
        self.dma_slot = None
        self.dma_target = 0
        self.dma_prev = None


class Sched:
    def __init__(self, same_engine_sync=True, dma_slots=8):
        self.lists = {e: [] for e in ENGS}
        self.same_engine_sync = same_engine_sync
        self.dma_slots = dma_slots
        self.dma_count = {e: 0 for e in ENGS}
        self.dma_hist = {e: [] for e in ENGS}
        self.last_compute = {e: None for e in ENGS}
        self.cur_fence = None

    def add(self, eng, fn, reads=(), writes=(), dma=False):
        I = Instr(eng, fn, dma)
        deps = {}
        for b in reads:
            for w in b.writers:
                deps[id(w)] = w
        for b in writes:
            for w in b.writers:
                deps[id(w)] = w
            for r in b.readers:
                deps[id(r)] = r
        if self.cur_fence is not None:
            deps[id(self.cur_fence)] = self.cur_fence
        for d in deps.values():
            if (not d.is_dma) and d.eng == eng and not dma:
                if eng == "pe" or eng == "sp" or not self.same_engine_sync:
                    continue
            if not d.is_dma:
                d.signal = True
            I.deps.append(d)
        if dma:
            n = self.dma_count[eng]
            self.dma_count[eng] = n + 1
            I.dma_slot = n % self.dma_slots
            I.dma_target = 16 * (n // self.dma_slots + 1)
            if n >= self.dma_slots:
                I.dma_prev = self.dma_hist[eng][n - self.dma_slots]
            self.dma_hist[eng].append(I)
        else:
            self.last_compute[eng] = I
        for b in reads:
            if dma:
                b.readers.append(I)
            else:
                b.readers = [r for r in b.readers if r.is_dma or r.eng != eng]
                b.readers.append(I)
        for b in writes:
            b.writers = [I]
            b.readers = []
        self.lists[eng].append(I)
        return I

    def fence(self):
        I = Instr("sp", lambda e: e.nop(), False)
        for e in ENGS:
            lc = self.last_compute[e]
            if lc is not None and e != "sp":
                lc.signal = True
                I.deps.append(lc)
            h = self.dma_hist[e]
            for d in h[-self.dma_slots:]:
                I.deps.append(d)
        I.signal = True
        self.lists["sp"].append(I)
        self.last_compute["sp"] = I
        self.cur_fence = I

    def emit(self, nc, E):
        nsem = {}
        for e in ENGS:
            c = 0
            si = 0
            for I in self.lists[e]:
                if I.is_dma:
                    continue
                if I.signal:
                    c += 1
                    if c > SEM_WRAP:
                        si += 1
                        c = 1
                    I.count = c
                    I.semidx = si
            nsem[e] = si + 1
        prog = {e: [E(nc.semaphore(f"pg_{e}{i}")) for i in range(nsem[e])] for e in ENGS}
        dsem = {e: [E(nc.semaphore(f"dm_{e}{i}")) for i in range(self.dma_slots)]
                for e in ENGS if self.dma_count[e] > 0}
        block = E(nc.Block())
        engobj = {"pe": block.tensor, "act": block.scalar, "dve": block.vector, "pool": block.gpsimd,
                  "sp": block.sync}
        stats = {e: [0, 0] for e in ENGS}

        def run(e, eng):
            waited = {}
            for I in self.lists[e]:
                need = {}
                for d in I.deps:
                    if d.is_dma:
                        key = ("d", d.eng, d.dma_slot)
                        val = d.dma_target
                    else:
                        key = ("p", d.eng, d.semidx)
                        val = d.count
                    if need.get(key, 0) < val:
                        need[key] = val
                if I.is_dma and I.dma_prev is not None:
                    key = ("d", e, I.dma_slot)
                    val = I.dma_prev.dma_target
                    if need.get(key, 0) < val:
                        need[key] = val
                for key, val in need.items():
                    if waited.get(key, 0) >= val:
                        continue
                    waited[key] = val
                    sem = dsem[key[1]][key[2]] if key[0] == "d" else prog[key[1]][key[2]]
                    eng.wait_ge(sem, val)
                    stats[e][1] += 1
                ins = I.fn(eng)
                stats[e][0] += 1
                if I.is_dma:
                    ins.then_inc(dsem[e][I.dma_slot], 16)
                elif I.signal:
                    ins.then_inc(prog[e][I.semidx], 1)
            if e in dsem:
                n = self.dma_count[e]
                for s in range(min(n, self.dma_slots)):
                    last_n = ((n - 1 - s) // self.dma_slots) * self.dma_slots + s
                    eng.wait_ge(dsem[e][s], 16 * (last_n // self.dma_slots + 1))

        for e in ENGS:
            if not self.lists[e]:
                continue

            def mk(e):
                def f(eng):
                    run(e, eng)
                return f
            engobj[e](mk(e))
        return stats


class Tile:
    def __init__(self, ap, name, nsub=1):
        self.ap = ap
        self.bs = [Buf(f"{name}.{i}") for i in range(nsub)]

    def __getitem__(self, idx):
        return self.ap[idx]

    @property
    def b(self):
        return self.bs[0]


class Arena:
    def __init__(self, tensor, size):
        self.t = tensor
        self.size = size
        self.off = 0
        self.peak = 0

    def alloc(self, name, free, nsub=1):
        n = 1
        for f in free:
            n *= f
        n2 = (n + 3) // 4 * 4
        assert self.off + n2 <= self.size, f"arena overflow {name}: {self.off}+{n2}>{self.size}"
        a = self.t[:, self.off:self.off + n]
        self.off += n2
        self.peak = max(self.peak, self.off)
        if len(free) == 2:
            a = a.rearrange("p (a b) -> p a b", a=free[0])
        elif len(free) == 3:
            a = a.rearrange("p (a b c) -> p a b c", a=free[0], b=free[1])
        return Tile(a, name, nsub)


def interleave(gens):
    gens = [g for g in gens if g is not None]
    while gens:
        for g in list(gens):
            try:
                next(g)
            except StopIteration:
                gens.remove(g)


def zip2(a, b):
    gens = [g for g in (a, b) if g is not None]
    while gens:
        for g in list(gens):
            try:
                next(g)
            except StopIteration:
                gens.remove(g)
        yield


def run_pipeline(ntiles, chain_fn, tl_fn, l_fn, r_fn, r_delay=5):
    chains = {p: chain_fn(p) for p in (0, 1)}
    started = {0: -1, 1: -1}
    rounds_in_tile = 0
    cur = -1
    aux = []
    r_launched = -1
    while chains or aux or r_launched < ntiles - 1:
        for p in list(chains):
            try:
                v = next(chains[p])
            except StopIteration:
                del chains[p]
                started[p] = ntiles
                continue
            if isinstance(v, tuple):
                started[p] = v[1]
        m = min(started.values())
        if m > cur:
            cur = m
            rounds_in_tile = 0
            if cur + 1 < ntiles:
                aux.append(tl_fn(cur + 1))
                g = l_fn(cur + 1)
                if g is not None:
                    aux.append(g)
        else:
            rounds_in_tile += 1
        if r_launched < cur - 1 and (rounds_in_tile >= r_delay or cur >= ntiles):
            r_launched += 1
            aux.append(r_fn(r_launched))
        for g in list(aux):
            try:
                next(g)
            except StopIteration:
                aux.remove(g)


def _bl(xs):
    out = []
    for x in xs:
        if isinstance(x, Buf):
            out.append(x)
        elif isinstance(x, Tile):
            out.extend(x.bs)
        else:
            out.extend(_bl(x))
    return out


E_Q, E_K, E_V, E_G, E_LR, E_PU, E_PG, E_COLS = 0, 512, 1024, 2048, 3072, 3104, 3616, 4128
O_CQ, O_CKV, O_KR, O_MG, O_MQ, O_MK, O_MV, O_MO, O_IF, O_MLG, O_COLS = (0, 384, 640, 672, 1184, 1696, 2208, 2720,
                                                                      3232, 3248, 3760)
WMAX = 4128
A16_SIZE = 68000
A32_SIZE = 14848


class Builder:
    def __init__(self, seqs, layers=(0, 1, 2, 3), TB=256, final_norm=True, smax=None):
        self.seqs = list(seqs)
        self.layers = list(layers)
        self.TB = TB
        self.NT = TB // P
        self.final_norm = final_norm
        self.smax = smax or max(s for _, s in seqs)
        self.nc = bass.Bass("TRN2", target_bir_lowering=False)
        self.S = Sched()
        self.dr = {}
        self._decl_dram()

    def _in(self, name, shape, dt=F32):
        self.dr[name] = self.nc.dram_tensor(name, list(shape), dt, kind="ExternalInput").ap()

    def _scr(self, name, shape, dt):
        self.dr[name] = self.nc.dram_tensor(name, list(shape), dt, kind="Internal").ap()

    def _decl_dram(self):
        nc = self.nc
        for tag, S in self.seqs:
            self._in(f"x_{tag}", [S, DM])
            self.dr[f"y_{tag}"] = nc.dram_tensor(f"y_{tag}", [S, DM], F32, kind="ExternalOutput").ap()
            self._in(f"invcnt_{tag}", [4, S])
            self._scr(f"qT_{tag}", [4, P, S], BF16)
            self._scr(f"kT_{tag}", [4, P, S], BF16)
            self._scr(f"puT_{tag}", [4, P, S], BF16)
            self._scr(f"pgT_{tag}", [4, P, S], BF16)
            self._scr(f"k_{tag}", [S, 512], BF16)
            self._scr(f"spb_{tag}", [S, 512], BF16)
            self._scr(f"v_{tag}", [S, 1024], BF16)
            self._scr(f"gg_{tag}", [S, 1024], BF16)
            self._scr(f"of_{tag}", [S, 1024], F32)
            self._scr(f"Q_{tag}", [8, 96, S], BF16)
            self._scr(f"Kn_{tag}", [4, P, S], BF16)
            self._scr(f"KR_{tag}", [32, S], BF16)
            self._scr(f"V_{tag}", [S, 512], BF16)
            self._scr(f"AT_{tag}", [4, P, S], BF16)
            self._scr(f"mgT_{tag}", [4, P, S], BF16)
            self._scr(f"mqT_{tag}", [4, P, S], BF16)
            self._scr(f"mkT_{tag}", [4, P, S], BF16)
            self._scr(f"mk_{tag}", [S, 512], BF16)
            self._scr(f"mv_{tag}", [S, 512], BF16)
            self._scr(f"mog_{tag}", [S, 512], BF16)
            self._scr(f"gb_{tag}", [S, 16], F32)
            self._scr(f"hf_{tag}", [S, 512], F32)
        self._in("norm_gT", [4, P, 8])
        self._in("final_g", [DM])
        self._in("e_w_in", [2, DM, E_COLS])
        self._in("e_aup", [2, 2, 17, 512])
        self._in("e_gn", [2, 256])
        self._in("e_pool_w", [2, 4, P, P])
        self._in("e_pscT", [2, P, 4])
        self._in("e_w_out", [2, 1536, DM])
        self._in("o_w_in", [2, DM, O_COLS])
        self._in("o_qgT", [2, P, 3])
        self._in("o_q_up", [2, 384, 768])
        self._in("o_kvgT", [2, P, 2])
        self._in("o_kv_up", [2, 256, 1024])
        self._in("o_if_bias", [2, 16])
        self._in("o_mlg", [2, P])
        self._in("o_w_out", [2, 1024, DM])
        self._in("c_bf", [P, 5, P], BF16)
        self._in("c_f32", [P, 3, P], F32)
        self._in("ropeT", [2, 32, self.smax], F32)

    def A(self, eng, fn, r=(), w=()):
        return self.S.add(eng, fn, reads=_bl(r), writes=_bl(w))

    def DMA(self, out, in_, r=(), w=(), q="sp"):
        return self.S.add(q, lambda e: e.dma_start(out=out, in_=in_), reads=_bl(r), writes=_bl(w), dma=True)

    def mm(self, out, lhsT, rhs, start=True, stop=True, r=(), w=()):
        return self.A("pe", lambda e: e.matmul(out, lhsT=lhsT, rhs=rhs, start=start, stop=stop), r, w)

    def tr(self, out, in_, r=(), w=()):
        idt = self.c_bf[:, 0, :]
        return self.A("pe", lambda e: e.transpose(out=out, in_=in_, identity=idt), list(r) + [self.c_bf], w)

    def act(self, out, in_, func, r=(), w=(), scale=1.0, bias=0.0, accum=None):
        if accum is None:
            return self.A("act", lambda e: e.activation(out=out, in_=in_, func=func, bias=bias, scale=scale), r, w)
        return self.A("act", lambda e: e.activation(out=out, in_=in_, func=func, bias=bias, scale=scale,
                                                    accum_out=accum), r, w)

    def ts(self, eng, out, in0, s1, s2=None, op0=ALU.mult, op1=None, r=(), w=()):
        if op1 is None:
            return self.A(eng, lambda e: e.tensor_scalar(out=out, in0=in0, scalar1=s1, scalar2=None, op0=op0), r, w)
        return self.A(eng, lambda e: e.tensor_scalar(out=out, in0=in0, scalar1=s1, scalar2=s2, op0=op0, op1=op1), r, w)

    def tt(self, eng, out, in0, in1, op, r=(), w=()):
        return self.A(eng, lambda e: e.tensor_tensor(out=out, in0=in0, in1=in1, op=op), r, w)

    def stt(self, out, in0, scalar, in1, op0, op1, r=(), w=()):
        return self.A("dve", lambda e: e.scalar_tensor_tensor(out=out, in0=in0, scalar=scalar, in1=in1, op0=op0,
                                                               op1=op1), r, w)

    def cp(self, eng, out, in_, r=(), w=()):
        if eng == "act":
            return self.A("act", lambda e: e.copy(out=out, in_=in_), r, w)
        return self.A(eng, lambda e: e.tensor_copy(out=out, in_=in_), r, w)

    def ms(self, eng, ap, val, w=()):
        return self.A(eng, lambda e: e.memset(ap, val), (), w)

    def silu_ps(self, dst_ap, dst_bufs, pst, et, W):
        e = et[:, 0:W]
        self.act(e, pst.ap, AF.Exp, r=[pst], w=[et], scale=-1.0)
        self.act(e, e, AF.Ln, r=[et], w=[et], bias=1.0)
        self.act(e, e, AF.Exp, r=[et], w=[et], scale=-1.0)
        self.tt("dve", dst_ap, pst.ap, e, ALU.mult, r=[pst, et], w=dst_bufs)

    def rsqrt_(self, t, n, p0=0, p1=P):
        self.act(t[p0:p1], t[p0:p1], AF.Ln, r=[t], w=[t], scale=1.0 / n, bias=self.eps_t[p0:p1, 0:1])
        self.act(t[p0:p1], t[p0:p1], AF.Exp, r=[t], w=[t], scale=-0.5)

    def build(self):
        nc = self.nc
        with ExitStack() as es:
            E = es.enter_context
            self.a16 = Arena(E(nc.sbuf_tensor("arena16", [P, A16_SIZE], BF16)).ap(), A16_SIZE)
            self.a32 = Arena(E(nc.sbuf_tensor("arena32", [P, A32_SIZE], F32)).ap(), A32_SIZE)
            self.c_bf = Tile(E(nc.sbuf_tensor("sb_c_bf", [P, 5, P], BF16)).ap(), "c_bf")
            self.c_f32 = Tile(E(nc.sbuf_tensor("sb_c_f32", [P, 3, P], F32)).ap(), "c_f32")
            self.eps_t = Tile(E(nc.sbuf_tensor("eps_t", [P, 1], F32)).ap(), "eps")
            self.gcol = Tile(E(nc.sbuf_tensor("gcol", [P, 8], F32)).ap(), "gcol")
            self.gfin = Tile(E(nc.sbuf_tensor("gfin", [P, DM], F32)).ap(), "gfin")
            self.psall = E(nc.psum_tensor("psall", [P, 7 * 512], F32)).ap()
            self.psb = [self.psall[:, i * 512:(i + 1) * 512] for i in range(7)]
            self.psT = Tile(E(nc.psum_tensor("psT", [P, 1024], BF16)).ap(), "psT")
            self.bankbuf = [Buf(f"bank{i}") for i in range(7)]
            self.DMA(self.c_bf.ap, self.dr["c_bf"], w=[self.c_bf])
            self.DMA(self.c_f32.ap, self.dr["c_f32"], w=[self.c_f32])
            self.DMA(self.gfin.ap, self.dr["final_g"].partition_broadcast(P), w=[self.gfin])
            self.ms("dve", self.eps_t.ap, EPS, w=[self.eps_t])
            self.S.fence()
            nl = len(self.layers)
            for li, layer in enumerate(self.layers):
                last = (li == nl - 1)
                first = (li == 0)
                if layer % 2 == 0:
                    self.even_layer(layer, first, last)
                else:
                    self.odd_layer(layer, first, last)
            stats = self.S.emit(nc, E)
            self.stats = stats
        return nc

    def ps(self, bank, off, width, name, parts=P):
        t = Tile(self.psb[bank][0:parts, off:off + width], name)
        t.bs = [self.bankbuf[bank]]
        return t

    def load_w_in(self, dram_w, ncols, layer):
        a16, a32 = self.a16, self.a32
        win = a16.alloc("win", (8, WMAX))
        self.DMA(self.gcol.ap, self.dr["norm_gT"][layer], w=[self.gcol])
        m32 = a32.off
        CH = 1032
        nch = (ncols + CH - 1) // CH
        stg = [a32.alloc(f"wstg{i}", (CH,)) for i in range(3)]
        k = 0
        for kc in range(8):
            for c in range(nch):
                c0 = c * CH
                cw = min(CH, ncols - c0)
                s = stg[k % 3]
                self.DMA(s[:, 0:cw], dram_w[kc * P:(kc + 1) * P, c0:c0 + cw], w=[s])
                eng = ("dve", "pool", "act")[k % 3]
                if eng == "act":
                    self.A("act", lambda e, s=s, kc=kc, c0=c0, cw=cw: e.mul(out=win[:, kc, c0:c0 + cw], in_=s[:, 0:cw],
                                                                        mul=self.gcol[:, kc:kc + 1]),
                           r=[s, self.gcol], w=[win])
                else:
                    self.ts(eng, win[:, kc, c0:c0 + cw], s[:, 0:cw], self.gcol[:, kc:kc + 1], r=[s, self.gcol], w=[win])
                k += 1
        a32.off = m32
        return win

    def load_w_out(self, dram_w, nchunks):
        a16, a32 = self.a16, self.a32
        wout = a16.alloc("wout", (nchunks, DM))
        m32 = a32.off
        stg = [a32.alloc(f"wostg{i}", (DM,)) for i in range(3)]
        for c in range(nchunks):
            s = stg[c % 3]
            self.DMA(s.ap, dram_w[c * P:(c + 1) * P, :], w=[s])
            self.cp(("dve", "pool", "act")[c % 3], wout[:, c, :], s.ap, r=[s], w=[wout])
        a32.off = m32
        return wout

    def alloc_xh(self):
        a16, a32 = self.a16, self.a32
        self.xt = [a32.alloc(f"xt{i}", (DM,)) for i in range(2)]
        self.hn = [a16.alloc(f"hn{i}", (DM,)) for i in range(2)]
        self.sqj = a16.alloc("sqj", (DM,))
        self.ssq = [a32.alloc(f"ssq{i}", (1,)) for i in range(2)]
        self.hT = [a16.alloc(f"hT{i}", (8, self.TB)) for i in range(2)]
        self.xcnt = 0

    def x_to_hT(self, xsrc, t0, bi):
        hT = self.hT[bi % 2]
        for j in range(self.NT):
            k = self.xcnt
            self.xcnt += 1
            xt, hn, ssq = self.xt[k % 2], self.hn[k % 2], self.ssq[k % 2]
            self.DMA(xt.ap, xsrc[t0 + j * P:t0 + (j + 1) * P, :], w=[xt])
            self.act(self.sqj.ap, xt.ap, AF.Square, r=[xt], w=[self.sqj, ssq], accum=ssq.ap)
            self.rsqrt_(ssq, DM)
            self.ts("dve", hn.ap, xt.ap, ssq[:, 0:1], r=[xt, ssq], w=[hn])
            yield
            for c in range(8):
                self.tr(self.psT[:, c * P:(c + 1) * P], hn[:, c * P:(c + 1) * P], r=[hn], w=[self.psT])
            self.cp("act" if j % 2 == 0 else "dve", hT[:, :, j * P:(j + 1) * P],
                    self.psT.ap.rearrange("p (c t) -> p c t", c=8), r=[self.psT], w=[hT])
            yield

    def proj_fm(self, win, hT, col0, m, pst):
        for kc in range(8):
            self.mm(pst.ap, win[:, kc, col0:col0 + m], hT[:, kc, :], start=(kc == 0), stop=(kc == 7),
                    r=[win, hT], w=[pst])

    def proj_tm(self, win, hT, j, col0, n, pst):
        for kc in range(8):
            self.mm(pst.ap, hT[:, kc, j * P:(j + 1) * P], win[:, kc, col0:col0 + n], start=(kc == 0), stop=(kc == 7),
                    r=[win, hT], w=[pst])

    def alloc_resid(self):
        a32 = self.a32
        self.xr = [a32.alloc(f"xr{i}", (DM,)) for i in range(2)]
        self.xo = [a32.alloc(f"xo{i}", (DM,)) for i in range(2)]
        self.fssq = [a32.alloc(f"fssq{i}", (1,)) for i in range(2)]
        self.fsq = self.a16.alloc("fsq", (DM,))
        self.rcnt = 0

    def resid_out(self, featT, nchunks, wout, j, xsrc, ydst, t, last, psA, psB):
        k = self.rcnt
        self.rcnt += 1
        xr, xo, fssq = self.xr[k % 2], self.xo[k % 2], self.fssq[k % 2]
        self.DMA(xr.ap, xsrc[t:t + P, :], w=[xr])
        for n, pst in enumerate((psA, psB)):
            for c in range(nchunks):
                self.mm(pst.ap, featT[:, c, j * P:(j + 1) * P], wout[:, c, n * 512:(n + 1) * 512],
                        start=(c == 0), stop=(c == nchunks - 1), r=[featT, wout], w=[pst])
            self.tt("dve", xo[:, n * 512:(n + 1) * 512], pst.ap, xr[:, n * 512:(n + 1) * 512], ALU.add,
                    r=[pst, xr], w=[xo])
            yield
        if last and self.final_norm:
            self.act(self.fsq.ap, xo.ap, AF.Square, r=[xo], w=[self.fsq, fssq], accum=fssq.ap)
            self.rsqrt_(fssq, DM)
            self.stt(xo.ap, xo.ap, fssq[:, 0:1], self.gfin.ap, ALU.mult, ALU.mult, r=[xo, fssq, self.gfin], w=[xo])
        self.DMA(ydst[t:t + P, :], xo.ap, r=[xo])
        yield

    def even_layer(self, layer, first, last):
        i = layer // 2
        a16, a32 = self.a16, self.a32
        dr = self.dr
        S = self.S
        TB, NT = self.TB, self.NT
        a16.off = 0
        a32.off = 0
        aup32 = a32.alloc("aup32", (2, 512))
        aup = a16.alloc("aup", (2, 512))
        self.DMA(aup32[0:17], dr["e_aup"][i].rearrange("d r c -> r d c"), w=[aup32])
        self.cp("dve", aup[0:17], aup32[0:17], r=[aup32], w=[aup])
        win = self.load_w_in(dr["e_w_in"][i], E_COLS, layer)
        S.fence()
        mW16, mW32 = a16.off, 0
        for tag, Sq in self.seqs:
            a16.off, a32.off = mW16, mW32
            self.even_fwd(tag, Sq, win, aup, first)
            S.fence()
        a16.off, a32.off = 0, 0
        gn = a32.alloc("gn_bc", (256,))
        self.DMA(gn.ap, dr["e_gn"][i].partition_broadcast(P), w=[gn])
        psc = a32.alloc("psc", (4,))
        self.DMA(psc.ap, dr["e_pscT"][i], w=[psc])
        mB32 = a32.off
        pw32 = a32.alloc("pw32", (4, P))
        pw = a16.alloc("pw", (4, P))
        self.DMA(pw32.ap, dr["e_pool_w"][i].rearrange("g c d -> c g d"), w=[pw32])
        self.cp("dve", pw.ap, pw32.ap, r=[pw32], w=[pw])
        wout = self.load_w_out(dr["e_w_out"][i], 12)
        S.fence()
        mB16 = a16.off
        for tag, Sq in self.seqs:
            a16.off, a32.off = mB16, mB32
            self.even_bwd(tag, Sq, wout, gn, pw, psc, first, last)
            S.fence()

    def gla_alloc(self):
        a16, a32 = self.a16, self.a32
        self.gE = [[a32.alloc(f"gE{b}{k}", (P,)) for k in range(3)] for b in range(2)]
        self.gq = [[a16.alloc(f"gq{b}{k}", (P,)) for k in range(4)] for b in range(2)]
        self.gS = a32.alloc("gS", (4, 256), nsub=4)
        self.gSb = a16.alloc("gSb", (4, 256), nsub=4)
        self.gps = []
        for b in range(2):
            small = 3 + 2 * b
            big = 4 + 2 * b
            self.gps.append(dict(bT=self.ps(small, 0, P, f"bT{b}"), cT=self.ps(small, P, P, f"cT{b}"),
                                 aT=self.ps(small, 2 * P, P, f"aT{b}"), o=self.ps(big, 0, 256, f"o{b}"),
                                 Pm=self.ps(big, 256, 256, f"Pm{b}")))
        self.gcnt = 0

    def gla_init_state(self):
        self.ms("dve", self.gS.ap, 0.0, w=[self.gS])
        self.ms("pool", self.gSb.ap, 0.0, w=[self.gSb])

    def gla_step(self, kk, h, sp, qT, kT, ktok, v, bwd, post):
        E1, E2, E3 = self.gE[kk]
        qeT, kdT, kl, aTm = self.gq[kk]
        g = self.gps[kk]
        cbf = self.c_bf
        TRI = cbf[:, 2, :] if bwd else cbf[:, 1, :]
        CL = cbf[:, 4, :] if bwd else cbf[:, 3, :]
        MSK = cbf[:, 3, :] if bwd else cbf[:, 1, :]
        Sb = self.gSb.bs[h]
        Sf = self.gS.bs[h]
        self.mm(g["bT"].ap, sp[0], TRI, r=[sp[1], cbf], w=[g["bT"]])
        self.mm(g["cT"].ap, CL, sp[0], r=[sp[1], cbf], w=[g["cT"]])
        yield
        self.act(E1.ap, g["bT"].ap, AF.Exp, r=[g["bT"]], w=[E1], scale=-1.0 / 16)
        self.act(E2.ap, g["bT"].ap, AF.Exp, r=[g["bT"]], w=[E2], scale=1.0 / 16)
        self.act(E3.ap, g["cT"].ap, AF.Exp, r=[g["cT"]], w=[E3], scale=-1.0 / 16)
        yield
        self.tt("pool", qeT.ap, qT[0], E1.ap, ALU.mult, r=[qT[1], E1], w=[qeT])
        self.tt("dve", kdT.ap, kT[0], E2.ap, ALU.mult, r=[kT[1], E2], w=[kdT])
        self.tt("pool", kl.ap, ktok[0], E3.ap, ALU.mult, r=[ktok[1], E3], w=[kl])
        yield
        self.mm(g["aT"].ap, kdT.ap, qeT.ap, r=[kdT, qeT], w=[g["aT"]])
        yield
        self.tt("dve", aTm.ap, g["aT"].ap, MSK, ALU.mult, r=[g["aT"], cbf], w=[aTm])
        yield
        self.mm(g["o"].ap, aTm.ap, v[0], start=True, stop=False, r=[aTm, v[1]], w=[g["o"]])
        self.mm(g["o"].ap, qeT.ap, self.gSb[:, h, :], start=False, stop=True, r=[qeT, Sb], w=[g["o"]])
        self.mm(g["Pm"].ap, kl.ap, v[0], r=[kl, v[1]], w=[g["Pm"]])
        yield
        lam = E1[:, 0:1] if bwd else E1[:, P - 1:P]
        self.stt(self.gS[:, h, :], self.gS[:, h, :], lam, g["Pm"].ap, ALU.mult, ALU.add, r=[Sf, E1, g["Pm"]], w=[Sf])
        self.cp("act", self.gSb[:, h, :], self.gS[:, h, :], r=[Sf], w=[Sb])
        if post is not None:
            yield from post(g["o"])
        else:
            yield

    def even_fwd(self, tag, Sq, win, aup, first):
        a16, a32, dr, TB, NT = self.a16, self.a32, self.dr, self.TB, self.NT
        xsrc = dr[f"x_{tag}"] if first else dr[f"y_{tag}"]
        self.alloc_xh()
        self.gla_alloc()
        fmq = [a16.alloc(f"fm_qT{b}", (4, TB), nsub=4) for b in range(2)]
        fmk = [a16.alloc(f"fm_kT{b}", (4, TB), nsub=4) for b in range(2)]
        fpu = a16.alloc("fm_puT", (4, TB), nsub=4)
        fpg = a16.alloc("fm_pgT", (4, TB), nsub=4)
        lrx = [a16.alloc(f"lrx{d}", (TB,)) for d in range(2)]
        NB4 = 2 * NT
        tk = [a16.alloc(f"tk{b}", (512,)) for b in range(NB4)]
        tspf = [a16.alloc(f"tspf{b}", (512,)) for b in range(NB4)]
        tv = [a16.alloc(f"tv{b}", (1024,), nsub=2) for b in range(NB4)]
        tspb = [a16.alloc(f"tspb{b}", (512,)) for b in range(2)]
        tgg = [a16.alloc(f"tgg{b}", (1024,), nsub=2) for b in range(2)]
        tof = [a32.alloc(f"tof{b}", (1024,), nsub=4) for b in range(2)]
        etmp = [a32.alloc(f"etmp{b}", (512,)) for b in range(2)]
        esil = [a32.alloc(f"esil{b}", (512,)) for b in range(2)]
        ps3 = [self.ps(k, 0, 512, f"ps3_{k}") for k in range(3)]
        for d in range(2):
            self.ms("pool", lrx[d][0:32], 1.0, w=[lrx[d]])
        self.gla_init_state()
        nb = Sq // TB
        st = dict(pc=0, tc=0)

        def nps():
            p = ps3[st["pc"] % 3]
            st["pc"] += 1
            return p

        def Pgen(bi):
            t0 = bi * TB
            bp = bi % 2
            hT = self.hT[bi % 2]
            yield from self.x_to_hT(xsrc, t0, bi)
            specs = [("qT", h, E_Q + h * P, fmq[bp]) for h in range(4)] + \
                    [("kT", h, E_K + h * P, fmk[bp]) for h in range(4)] + \
                    [("puT", h, E_PU + h * P, fpu) for h in range(4)] + \
                    [("pgT", h, E_PG + h * P, fpg) for h in range(4)]
            for (nm, h, col, dst) in specs:
                pf = nps()
                pst = Tile(pf[:, 0:TB], "pf")
                pst.bs = pf.bs
                self.proj_fm(win, hT, col, P, pst)
                if nm == "qT":
                    self.A("act", lambda e, o=dst[:, h, :], p=pst.ap: e.mul(out=o, in_=p, mul=float(P) ** -0.5),
                           r=[pst], w=[dst.bs[h]])
                elif nm == "pgT":
                    self.silu_ps(dst[:, h, :], [dst.bs[h]], pst, esil[h % 2], TB)
                else:
                    self.cp("dve", dst[:, h, :], pst.ap, r=[pst], w=[dst.bs[h]])
                yield
            for d in range(2):
                pf = nps()
                p16 = Tile(pf[0:16, 0:TB], "p16")
                p16.bs = pf.bs
                self.proj_fm(win, hT, E_LR + 16 * d, 16, p16)
                self.cp("dve", lrx[d][0:16, :], pf[0:16, 0:TB], r=[pf], w=[lrx[d]])
            yield
            for nm, src in (("qT", fmq[bp]), ("kT", fmk[bp]), ("puT", fpu), ("pgT", fpg)):
                self.DMA(dr[f"{nm}_{tag}"][:, :, t0:t0 + TB].rearrange("h p t -> p h t"), src.ap, r=[src])
            for j in range(NT):
                t = t0 + j * P
                b4 = (bi * NT + j) % NB4
                b2 = (bi * NT + j) % 2
                k_, spf_, v_, spb_, gg_ = tk[b4], tspf[b4], tv[b4], tspb[b2], tgg[b2]
                for d, dst in ((0, spf_), (1, spb_)):
                    pst = nps()
                    self.mm(pst.ap, lrx[d][0:17, j * P:(j + 1) * P], aup[0:17, d, :], r=[lrx[d], aup], w=[pst])
                    et = etmp[d]
                    self.act(et.ap, pst.ap, AF.Exp, r=[pst], w=[et], scale=-1.0)
                    self.act(dst.ap, et.ap, AF.Ln, r=[et], w=[dst], bias=1.0)
                    yield
                pst = nps()
                self.proj_tm(win, hT, j, E_K, 512, pst)
                self.cp("dve", k_.ap, pst.ap, r=[pst], w=[k_])
                yield
                for n in range(2):
                    pst = nps()
                    self.proj_tm(win, hT, j, E_V + n * 512, 512, pst)
                    self.cp("act", v_[:, n * 512:(n + 1) * 512], pst.ap, r=[pst], w=[v_.bs[n]])
                    yield
                for n in range(2):
                    pst = nps()
                    self.proj_tm(win, hT, j, E_G + n * 512, 512, pst)
                    self.silu_ps(gg_[:, n * 512:(n + 1) * 512], [gg_.bs[n]], pst, esil[n], 512)
                    yield
                self.DMA(dr[f"k_{tag}"][t:t + P, :], k_.ap, r=[k_])
                self.DMA(dr[f"spb_{tag}"][t:t + P, :], spb_.ap, r=[spb_])
                self.DMA(dr[f"v_{tag}"][t:t + P, :], v_.ap, r=[v_])
                self.DMA(dr[f"gg_{tag}"][t:t + P, :], gg_.ap, r=[gg_])

        def Sgen(bi, par):
            t0 = bi * TB
            bp = bi % 2
            for j in range(NT):
                t = t0 + j * P
                b4 = (bi * NT + j) % NB4
                b2 = (bi * NT + j) % 2
                k_, spf_, v_, of_ = tk[b4], tspf[b4], tv[b4], tof[b2]
                for h in (par, par + 2):
                    def post(po, h=h, of_=of_, t=t):
                        self.cp("dve", of_[:, h * 256:(h + 1) * 256], po.ap, r=[po], w=[of_.bs[h]])
                        self.DMA(dr[f"of_{tag}"][t:t + P, h * 256:(h + 1) * 256], of_[:, h * 256:(h + 1) * 256],
                                 r=[of_.bs[h]])
                        yield
                    yield from self.gla_step(par, h, (spf_[:, h * P:(h + 1) * P], spf_),
                                             (fmq[bp][:, h, j * P:(j + 1) * P], fmq[bp].bs[h]),
                                             (fmk[bp][:, h, j * P:(j + 1) * P], fmk[bp].bs[h]),
                                             (k_[:, h * P:(h + 1) * P], k_),
                                             (v_[:, h * 256:(h + 1) * 256], v_.bs[h // 2]), False, post)

        for bi in range(nb + 1):
            streams = []
            if bi >= 1:
                streams += [Sgen(bi - 1, 0), Sgen(bi - 1, 1)]
            if bi < nb:
                streams.append(Pgen(bi))
            interleave(streams)

    def even_bwd(self, tag, Sq, wout, gn, pw, psc, first, last):
        a16, a32, dr, TB, NT = self.a16, self.a32, self.dr, self.TB, self.NT
        xsrc = dr[f"x_{tag}"] if first else dr[f"y_{tag}"]
        ydst = dr[f"y_{tag}"]
        self.gla_alloc()
        self.alloc_resid()
        H = 8
        fmq = [a16.alloc(f"bq{b}", (4, TB), nsub=4) for b in range(2)]
        fmk = [a16.alloc(f"bk{b}", (4, TB), nsub=4) for b in range(2)]
        fpg = [a16.alloc(f"bpg{b}", (4, TB)) for b in range(2)]
        fpu = [a16.alloc(f"bpu{b}", (4, TB + 2 * H)) for b in range(2)]
        icn = [a32.alloc(f"icn{b}", (4, TB)) for b in range(2)]
        tk = [a16.alloc(f"tk{b}", (512,)) for b in range(3)]
        tspb = [a16.alloc(f"tspb{b}", (512,)) for b in range(3)]
        tv = [a16.alloc(f"tv{b}", (1024,)) for b in range(3)]
        tgg = [a16.alloc(f"tgg{b}", (1024,)) for b in range(3)]
        tof = [a32.alloc(f"tof{b}", (1024,)) for b in range(3)]
        osum = [a32.alloc(f"osum{b}", (256,)) for b in range(2)]
        osq = [a16.alloc(f"osq{b}", (256,)) for b in range(2)]
        hss = [a32.alloc(f"hss{b}", (1,)) for b in range(2)]
        otmp = [a32.alloc(f"otmp{b}", (256,)) for b in range(2)]
        glo = [a16.alloc(f"glo{b}", (1024,), nsub=4) for b in range(2)]
        featT = [a16.alloc(f"featT{b}", (12, TB), nsub=12) for b in range(2)]
        pt32 = [a32.alloc(f"pt32_{k}", (TB + 2 * H,)) for k in range(3)]
        pwin = a32.alloc("pwin", (TB,))
        pld = a16.alloc("pld", (TB,))
        psA, psB = self.ps(1, 0, 512, "psA"), self.ps(2, 0, 512, "psB")
        psX = self.ps(0, 0, TB, "psX")
        self.gla_init_state()
        nb = Sq // TB
        tiles = [(bi, j) for bi in range(nb - 1, -1, -1) for j in range(NT - 1, -1, -1)]

        def Lgen(bi):
            t0 = bi * TB
            bb = bi % 2
            q_, k_f, pg_, pu_, ic_, ft = fmq[bb], fmk[bb], fpg[bb], fpu[bb], icn[bb], featT[bb]
            self.DMA(q_.ap, dr[f"qT_{tag}"][:, :, t0:t0 + TB].rearrange("h p t -> p h t"), w=[q_])
            self.DMA(k_f.ap, dr[f"kT_{tag}"][:, :, t0:t0 + TB].rearrange("h p t -> p h t"), w=[k_f])
            self.DMA(pg_.ap, dr[f"pgT_{tag}"][:, :, t0:t0 + TB].rearrange("h p t -> p h t"), w=[pg_])
            lo = max(t0 - H, 0)
            hi = min(t0 + TB + H, Sq)
            if lo != t0 - H or hi != t0 + TB + H:
                self.ms("pool", pu_.ap, 0.0, w=[pu_])
            self.DMA(pu_[:, :, lo - (t0 - H):hi - (t0 - H)],
                     dr[f"puT_{tag}"][:, :, lo:hi].rearrange("h p t -> p h t"), w=[pu_])
            for g in range(4):
                self.DMA(ic_[:, g, :], dr[f"invcnt_{tag}"][g, t0:t0 + TB].partition_broadcast(P), w=[ic_])
            yield
            for g in range(4):
                W = TB + 2 * H
                hw = (1, 2, 4, 8)[g]
                if g == 0:
                    self.tt("pool", pwin.ap, pu_[:, g, H - 1:H - 1 + TB], pu_[:, g, H:H + TB], ALU.add, r=[pu_], w=[pwin])
                else:
                    step = 1
                    cur_ap = pu_[:, g, :]
                    cur_t = pu_
                    width = W
                    for lv in range(g):
                        dst = pt32[lv]
                        width = width - step
                        self.tt("pool", dst[:, 0:width], cur_ap[:, 0:width], cur_ap[:, step:step + width], ALU.add,
                                r=[cur_t], w=[dst])
                        cur_ap = dst.ap
                        cur_t = dst
                        step *= 2
                    self.tt("pool", pwin.ap, cur_ap[:, H - hw:H - hw + TB], cur_ap[:, H:H + TB], ALU.add, r=[cur_t], w=[pwin])
                yield
                self.tt("pool", pwin.ap, pwin.ap, ic_[:, g, :], ALU.mult, r=[pwin, ic_], w=[pwin])
                self.tt("pool", pld.ap, pwin.ap, pu_[:, g, H:H + TB], ALU.subtract, r=[pwin, pu_], w=[pld])
                yield
                self.mm(psX.ap, pw[:, g, :], pld.ap, r=[pw, pld], w=[psX])
                self.stt(ft[:, 8 + g, :], psX.ap, psc[:, g:g + 1], pg_[:, g, :], ALU.mult, ALU.mult,
                         r=[psX, psc, pg_], w=[ft.bs[8 + g]])
                yield

        def TLgen(n):
            bi, j = tiles[n]
            t = bi * TB + j * P
            tb = n % 3
            self.DMA(tk[tb].ap, dr[f"k_{tag}"][t:t + P, :], w=[tk[tb]])
            self.DMA(tspb[tb].ap, dr[f"spb_{tag}"][t:t + P, :], w=[tspb[tb]])
            self.DMA(tv[tb].ap, dr[f"v_{tag}"][t:t + P, :], w=[tv[tb]])
            self.DMA(tgg[tb].ap, dr[f"gg_{tag}"][t:t + P, :], w=[tgg[tb]])
            self.DMA(tof[tb].ap, dr[f"of_{tag}"][t:t + P, :], w=[tof[tb]])
            yield

        def mkpost(par, h, po, of_, gg_, gl):
            os_, hs_, ot_, oq_ = osum[par], hss[par], otmp[par], osq[par]
            self.tt("dve", os_.ap, po.ap, of_[:, h * 256:(h + 1) * 256], ALU.add, r=[po, of_], w=[os_])
            yield
            self.act(oq_.ap, os_.ap, AF.Square, r=[os_], w=[oq_, hs_], accum=hs_.ap)
            self.rsqrt_(hs_, 256)
            yield
            self.stt(ot_.ap, os_.ap, hs_[:, 0:1], gn.ap, ALU.mult, ALU.mult, r=[os_, hs_, gn], w=[ot_])
            yield
            self.tt("pool", gl[:, h * 256:(h + 1) * 256], ot_.ap, gg_[:, h * 256:(h + 1) * 256], ALU.mult,
                    r=[ot_, gg_], w=[gl.bs[h]])
            yield

        def Schain(par):
            pending = None
            po = self.gps[par]["o"]
            for n in range(len(tiles)):
                bi, j = tiles[n]
                bb = bi % 2
                tb = n % 3
                q_, k_f = fmq[bb], fmk[bb]
                k_, spb_, v_, gg_, of_, gl = tk[tb], tspb[tb], tv[tb], tgg[tb], tof[tb], glo[n % 2]
                yield ("tile", n)
                for h in (par, par + 2):
                    step = self.gla_step(par, h, (spb_[:, h * P:(h + 1) * P], spb_),
                                         (q_[:, h, j * P:(j + 1) * P], q_.bs[h]),
                                         (k_f[:, h, j * P:(j + 1) * P], k_f.bs[h]),
                                         (k_[:, h * P:(h + 1) * P], k_),
                                         (v_[:, h * 256:(h + 1) * 256], v_), True, None)
                    yield from zip2(step, pending)
                    pending = mkpost(par, h, po, of_, gg_, gl)
            if pending is not None:
                yield from pending

        def Rgen(n):
            bi, j = tiles[n]
            bb = bi % 2
            ft, gl = featT[bb], glo[n % 2]
            t = bi * TB + j * P
            for c in range(8):
                self.tr(self.psT[:, c * P:(c + 1) * P], gl[:, c * P:(c + 1) * P], r=[gl], w=[self.psT])
            self.cp("act", ft[:, 0:8, j * P:(j + 1) * P], self.psT.ap.rearrange("p (c t) -> p c t", c=8),
                    r=[self.psT], w=ft.bs[0:8])
            yield
            yield from self.resid_out(ft, 12, wout, j, xsrc, ydst, t, last, psA, psB)

        def Lfor(n):
            if n == 0 or tiles[n][0] != tiles[n - 1][0]:
                return Lgen(tiles[n][0])
            return None

        interleave([Lfor(0), TLgen(0)])
        run_pipeline(len(tiles), Schain, TLgen, Lfor, Rgen)

    def odd_layer(self, layer, first, last):
        i = layer // 2
        a16, a32, dr, S = self.a16, self.a32, self.dr, self.S
        a16.off = 0
        a32.off = 0
        ifb = a32.alloc("ifb", (16,))
        self.DMA(ifb.ap, dr["o_if_bias"][i].partition_broadcast(P), w=[ifb])
        mF32 = a32.off
        qg = a32.alloc("qg", (4,))
        kvg = a32.alloc("kvg", (4,))
        self.DMA(qg[:, 0:3], dr["o_qgT"][i], w=[qg])
        self.DMA(kvg[:, 0:2], dr["o_kvgT"][i], w=[kvg])
        qup32 = a32.alloc("qup32", (3, 768))
        kvup32 = a32.alloc("kvup32", (2, 1024))
        self.DMA(qup32.ap, dr["o_q_up"][i].rearrange("(c p) n -> p c n", p=P), w=[qup32])
        self.DMA(kvup32.ap, dr["o_kv_up"][i].rearrange("(c p) n -> p c n", p=P), w=[kvup32])
        qup = a16.alloc("qup", (3, 768))
        quprot = a16.alloc("quprot", (3, 768))
        kvk = a16.alloc("kvk", (2, 512))
        kvv = a16.alloc("kvv", (2, 512))
        wkrot = a16.alloc("wkrot", (8, 32))
        self.ms("pool", quprot.ap, 0.0, w=[quprot])
        for c in range(3):
            self.ts("dve", qup[:, c, :], qup32[:, c, :], qg[:, c:c + 1], r=[qup32, qg], w=[qup])
            qv = qup[:, c, :].rearrange("p (h r) -> p h r", h=8)
            rv = quprot[:, c, :].rearrange("p (h r) -> p h r", h=8)
            self.ts("pool", rv[:, :, 64:80], qv[:, :, 80:96], -1.0, r=[qup], w=[quprot])
            self.cp("pool", rv[:, :, 80:96], qv[:, :, 64:80], r=[qup], w=[quprot])
        for c in range(2):
            kv3 = kvup32[:, c, :].rearrange("p (h r) -> p h r", h=8)
            self.ts("dve", kvk[:, c, :].rearrange("p (h r) -> p h r", h=8), kv3[:, :, 0:64], kvg[:, c:c + 1],
                    r=[kvup32, kvg], w=[kvk])
            self.ts("dve", kvv[:, c, :].rearrange("p (h r) -> p h r", h=8), kv3[:, :, 64:128], kvg[:, c:c + 1],
                    r=[kvup32, kvg], w=[kvv])
        win = self.load_w_in(dr["o_w_in"][i], O_COLS, layer)
        self.ts("pool", wkrot[:, :, 0:16], win[:, :, O_KR + 16:O_KR + 32], -1.0, r=[win], w=[wkrot])
        self.cp("pool", wkrot[:, :, 16:32], win[:, :, O_KR:O_KR + 16], r=[win], w=[wkrot])
        S.fence()
        mW16 = a16.off
        import os as _os
        dbg = _os.environ.get("KDBG", "FMB")
        for tag, Sq in self.seqs:
            a16.off, a32.off = mW16, mF32
            if "F" in dbg:
                self.odd_fwd(tag, Sq, win, wkrot, qup, quprot, kvk, kvv, ifb, first)
            S.fence()
        for tag, Sq in self.seqs:
            a16.off, a32.off = 0, 0
            if "M" in dbg:
                self.mla_pass(tag, Sq)
            S.fence()
        if "B" not in dbg:
            return
        a16.off, a32.off = 0, 0
        gml = a32.alloc("gml_bc", (P,))
        self.DMA(gml.ap, dr["o_mlg"][i].partition_broadcast(P), w=[gml])
        mB32 = a32.off
        wout = self.load_w_out(dr["o_w_out"][i], 8)
        S.fence()
        mB16 = a16.off
        for tag, Sq in self.seqs:
            a16.off, a32.off = mB16, mB32
            self.odd_bwd(tag, Sq, wout, gml, first, last)
            S.fence()

    def mlstm_alloc(self):
        a16, a32 = self.a16, self.a32
        self.mq = [[a16.alloc(f"mq{b}{k}", (P,)) for k in range(3)] for b in range(2)]
        self.mC = a32.alloc("mC", (4, 130), nsub=4)
        self.mCb = a16.alloc("mCb", (4, 130), nsub=4)
        self.malpha = a32.alloc("malpha", (4,), nsub=4)
        self.mden = [a32.alloc(f"mden{b}", (2,)) for b in range(2)]
        self.mps = []
        for b in range(2):
            self.mps.append(dict(sT=self.ps(3 + 2 * b, 0, P, f"sT{b}"), nd=self.ps(4 + 2 * b, 0, 129, f"nd{b}"),
                                 Pm=self.ps(4 + 2 * b, 256, 129, f"mPm{b}")))
        self.shl = a16.alloc("shl", (2, 8))
        self.ones_bf = a16.alloc("ones_bf", (P,))
        self.ms("pool", self.ones_bf.ap, 1.0, w=[self.ones_bf])
        self.ms("dve", self.mC.ap, 0.0, w=[self.mC])
        self.ms("pool", self.mCb.ap, 0.0, w=[self.mCb])
        self.ms("dve", self.malpha.ap, 1.0, w=[self.malpha])

    def mlstm_step(self, kk, h, w, db, anext, qT, kT, ktok, vx, bwd, post):
        sTm, kw, qa = self.mq[kk]
        g = self.mps[kk]
        den = self.mden[kk]
        cbf = self.c_bf
        MSK = cbf[:, 3, :] if bwd else cbf[:, 1, :]
        al = self.malpha[:, h:h + 1]
        alb = self.malpha.bs[h]
        Cf, Cb = self.mC.bs[h], self.mCb.bs[h]
        self.mm(g["sT"].ap, kT[0], qT[0], r=[kT[1], qT[1]], w=[g["sT"]])
        self.ts("dve", kw.ap, ktok[0], w[0], r=[ktok[1], w[1]], w=[kw])
        self.ts("dve", qa.ap, qT[0], al, r=[qT[1], alb], w=[qa])
        yield
        self.stt(sTm.ap, g["sT"].ap, w[0], MSK, ALU.mult, ALU.mult, r=[g["sT"], w[1], cbf], w=[sTm])
        yield
        self.mm(g["nd"].ap, sTm.ap, vx[0], start=True, stop=False, r=[sTm, vx[1]], w=[g["nd"]])
        self.mm(g["nd"].ap, qa.ap, self.mCb[:, h, 0:129], start=False, stop=True, r=[qa, Cb], w=[g["nd"]])
        self.mm(g["Pm"].ap, kw.ap, vx[0], r=[kw, vx[1]], w=[g["Pm"]])
        yield
        self.cp("dve", den[:, 1:2], g["nd"][:, 128:129], r=[g["nd"]], w=[den])
        self.stt(self.mC[:, h, 0:129], self.mC[:, h, 0:129], al, g["Pm"].ap, ALU.mult, ALU.add,
                 r=[Cf, alb, g["Pm"]], w=[Cf])
        yield
        self.cp("act", self.mCb[:, h, 0:129], self.mC[:, h, 0:129], r=[Cf], w=[Cb])
        self.stt(den[:, 0:1], den[:, 1:2], -1.0, den[:, 1:2], ALU.mult, ALU.max, r=[den], w=[den])
        self.tt("dve", den[:, 0:1], den[:, 0:1], db[0], ALU.max, r=[den, db[1]], w=[den])
        self.A("dve", lambda e, o=den[:, 1:2], i_=den[:, 0:1]: e.reciprocal(out=o, in_=i_), r=[den], w=[den])
        self.cp("pool", al, anext[0], r=[anext[1]], w=[alb])
        yield
        if post is not None:
            yield from post(g["nd"], den)

    def odd_fwd(self, tag, Sq, win, wkrot, qup, quprot, kvk, kvv, ifb, first):
        a16, a32, dr, TB, NT = self.a16, self.a32, self.dr, self.TB, self.NT
        xsrc = dr[f"x_{tag}"] if first else dr[f"y_{tag}"]
        cbf = self.c_bf
        self.alloc_xh()
        self.mlstm_alloc()
        cq = {n: a16.alloc(f"cq{n}", (3, TB)) for n in ("T", "sq", "n")}
        ckv = {n: a16.alloc(f"ckv{n}", (2, TB)) for n in ("T", "sq", "n")}
        fmg = a16.alloc("fm_mgT", (4, TB), nsub=4)
        fKn = a16.alloc("fm_Kn", (4, TB), nsub=4)
        fmq = [a16.alloc(f"fm_mqT{b}", (4, TB), nsub=4) for b in range(2)]
        fmk = [a16.alloc(f"fm_mkT{b}", (4, TB), nsub=4) for b in range(2)]
        Qst = a16.alloc("Qst", (8, TB), nsub=8)
        KRst = a16.alloc("KRst", (TB,))
        rhl = a16.alloc("rhl", (2, TB))
        NB4 = 2 * NT
        tmk = [a16.alloc(f"tmk{b}", (512,)) for b in range(NB4)]
        tmv = [a16.alloc(f"tmv{b}", (4, 130)) for b in range(NB4)]
        wt = [a32.alloc(f"wt{b}", (24,)) for b in range(NB4)]
        tmog = [a16.alloc(f"tmog{b}", (512,)) for b in range(2)]
        tV = [a16.alloc(f"tV{b}", (512,)) for b in range(2)]
        thf = [a32.alloc(f"thf{b}", (512,), nsub=4) for b in range(2)]
        rrow = {n: a32.alloc(f"rrow{n}", (TB,)) for n in ("q", "kv")}
        rbc = {n: a32.alloc(f"rbc{n}", (TB,)) for n in ("q", "kv")}
        cosT = a32.alloc("cosT", (TB,))
        sinT = a32.alloc("sinT", (TB,))
        rt1 = a32.alloc("rt1", (TB,))
        rt2 = a32.alloc("rt2", (TB,))
        sig = a32.alloc("sig", (512,))
        sil = a32.alloc("sil", (512,))
        G = [a32.alloc(f"G{b}", (16,)) for b in range(2)]
        spm = [a32.alloc(f"spm{b}", (8,)) for b in range(2)]
        gbs = [a32.alloc(f"gbs{b}", (16,)) for b in range(2)]
        ps3 = [self.ps(k, 0, 512, f"ps3_{k}") for k in range(3)]
        for b in range(NB4):
            self.ms("pool", tmv[b][:, :, 128:130], 1.0, w=[tmv[b]])
        for b in range(2):
            self.ms("pool", gbs[b].ap, 0.0, w=[gbs[b]])
        ones_col = cbf[:, 1, P - 1:P]
        nb = Sq // TB
        st = dict(pc=0)

        def nps():
            p = ps3[st["pc"] % 3]
            st["pc"] += 1
            return p

        def sub(pf, p1, width, name="pfs"):
            t = Tile(pf[0:p1, 0:width], name)
            t.bs = pf.bs
            return t

        def Pgen(bi):
            t0 = bi * TB
            bp = bi % 2
            hT = self.hT[bi % 2]
            yield from self.x_to_hT(xsrc, t0, bi)
            for rows in ((0, 32), (64, 96)):
                self.DMA(cosT[rows[0]:rows[1], :], dr["ropeT"][0, :, t0:t0 + TB], w=[cosT])
                self.DMA(sinT[rows[0]:rows[1], :], dr["ropeT"][1, :, t0:t0 + TB], w=[sinT])
            for (nm, col, nch, dim, T3) in (("q", O_CQ, 3, 384, cq), ("kv", O_CKV, 2, 256, ckv)):
                for c in range(nch):
                    pst = sub(nps(), P, TB)
                    self.proj_fm(win, hT, col + c * P, P, pst)
                    self.cp("dve", T3["T"][:, c, :], pst.ap, r=[pst], w=[T3["T"]])
                    self.tt("pool", T3["sq"][:, c, :], T3["T"][:, c, :], T3["T"][:, c, :], ALU.mult, r=[T3["T"]],
                            w=[T3["sq"]])
                    yield
                pst = nps()
                for c in range(nch):
                    self.mm(pst[0:1, 0:TB], ones_col, T3["sq"][:, c, :], start=(c == 0), stop=(c == nch - 1),
                            r=[T3["sq"], cbf], w=[pst])
                rr = rrow[nm]
                self.cp("dve", rr[0:1, :], pst[0:1, 0:TB], r=[pst], w=[rr])
                self.rsqrt_(rr, dim, 0, 1)
                self.cp("dve", rhl[0:1, 0, :], rr[0:1, :], r=[rr], w=[rhl])
                self.tt("dve", rhl[0:1, 1, :], rr[0:1, :], rhl[0:1, 0, :], ALU.subtract, r=[rr, rhl], w=[rhl])
                yield
                pst = nps()
                for k2 in range(2):
                    self.mm(pst[:, 0:TB], cbf[0:1, 1, :], rhl[0:1, k2, :], start=(k2 == 0), stop=(k2 == 1),
                            r=[cbf, rhl], w=[pst])
                self.cp("act", rbc[nm].ap, pst[:, 0:TB], r=[pst], w=[rbc[nm]])
                yield
                for c in range(nch):
                    self.tt("pool", T3["n"][:, c, :], T3["T"][:, c, :], rbc[nm].ap, ALU.mult, r=[T3["T"], rbc[nm]],
                            w=[T3["n"]])
                yield
            pa = nps()
            pb = nps()
            self.proj_fm(win, hT, O_KR, 32, sub(pa, 32, TB))
            for kc in range(8):
                self.mm(pb[0:32, 0:TB], wkrot[:, kc, :], hT[:, kc, :], start=(kc == 0), stop=(kc == 7),
                        r=[wkrot, hT], w=[pb])
            self.tt("dve", rt1[0:32, :], pa[0:32, 0:TB], cosT[0:32, :], ALU.mult, r=[pa, cosT], w=[rt1])
            self.tt("dve", rt2[0:32, :], pb[0:32, 0:TB], sinT[0:32, :], ALU.mult, r=[pb, sinT], w=[rt2])
            self.tt("pool", KRst[0:32, :], rt1[0:32, :], rt2[0:32, :], ALU.add, r=[rt1, rt2], w=[KRst])
            self.DMA(dr[f"KR_{tag}"][:, t0:t0 + TB], KRst[0:32, :], r=[KRst])
            yield
            for (dst, col, mode) in ((fmg, O_MG, "silu"), (fmq[bp], O_MQ, "copy"), (fmk[bp], O_MK, "scale")):
                for h in range(4):
                    pst = sub(nps(), P, TB)
                    self.proj_fm(win, hT, col + h * P, P, pst)
                    if mode == "silu":
                        self.silu_ps(dst[:, h, :], [dst.bs[h]], pst, sig if h % 2 == 0 else sil, TB)
                    elif mode == "copy":
                        self.cp("dve", dst[:, h, :], pst.ap, r=[pst], w=[dst.bs[h]])
                    else:
                        self.A("act", lambda e, o=dst[:, h, :], p=pst.ap: e.mul(out=o, in_=p, mul=float(P) ** -0.5),
                               r=[pst], w=[dst.bs[h]])
                    yield
            for h in range(8):
                pa = nps()
                pb = nps()
                for c in range(3):
                    self.mm(pa[0:96, 0:TB], qup[:, c, h * 96:(h + 1) * 96], cq["n"][:, c, :], start=(c == 0),
                            stop=(c == 2), r=[qup, cq["n"]], w=[pa])
                for c in range(3):
                    self.mm(pb[0:96, 0:TB], quprot[:, c, h * 96:(h + 1) * 96], cq["n"][:, c, :], start=(c == 0),
                            stop=(c == 2), r=[quprot, cq["n"]], w=[pb])
                self.cp("dve", Qst[0:64, h, :], pa[0:64, 0:TB], r=[pa], w=[Qst.bs[h]])
                self.tt("dve", rt1[64:96, :], pa[64:96, 0:TB], cosT[64:96, :], ALU.mult, r=[pa, cosT], w=[rt1])
                self.tt("dve", rt2[64:96, :], pb[64:96, 0:TB], sinT[64:96, :], ALU.mult, r=[pb, sinT], w=[rt2])
                self.tt("pool", Qst[64:96, h, :], rt1[64:96, :], rt2[64:96, :], ALU.add, r=[rt1, rt2], w=[Qst.bs[h]])
                yield
            for pr in range(4):
                pst = nps()
                for c in range(2):
                    self.mm(pst[:, 0:TB], kvk[:, c, pr * P:(pr + 1) * P], ckv["n"][:, c, :], start=(c == 0), stop=(c == 1),
                            r=[kvk, ckv["n"]], w=[pst])
                self.cp("dve", fKn[:, pr, :], pst[:, 0:TB], r=[pst], w=[fKn.bs[pr]])
                yield
            self.DMA(dr[f"Q_{tag}"][:, :, t0:t0 + TB].rearrange("h r t -> r h t"), Qst[0:96], r=[Qst])
            for nm, src in (("Kn", fKn), ("mgT", fmg), ("mqT", fmq[bp]), ("mkT", fmk[bp])):
                self.DMA(dr[f"{nm}_{tag}"][:, :, t0:t0 + TB].rearrange("h p t -> p h t"), src.ap, r=[src])
            for j in range(NT):
                t = t0 + j * P
                b4 = (bi * NT + j) % NB4
                b2 = (bi * NT + j) % 2
                mk_, mv_, wt_ = tmk[b4], tmv[b4], wt[b4]
                mog_, V_, G_, spm_, gbs_ = tmog[b2], tV[b2], G[b2], spm[b2], gbs[b2]
                pst = nps()
                self.proj_tm(win, hT, j, O_MK, 512, pst)
                self.A("act", lambda e, o=mk_.ap, p=pst.ap: e.mul(out=o, in_=p, mul=float(P) ** -0.5), r=[pst], w=[mk_])
                yield
                pst = nps()
                self.proj_tm(win, hT, j, O_MV, 512, pst)
                self.cp("dve", mv_[:, :, 0:128], pst.ap.rearrange("p (h d) -> p h d", h=4), r=[pst], w=[mv_])
                yield
                pst = nps()
                self.proj_tm(win, hT, j, O_MO, 512, pst)
                self.act(sig.ap, pst.ap, AF.Exp, r=[pst], w=[sig], scale=-1.0)
                self.act(sig.ap, sig.ap, AF.Ln, r=[sig], w=[sig], bias=1.0)
                yield
                pst = nps()
                self.proj_tm(win, hT, j, O_MLG, 512, pst)
                self.act(sil.ap, pst.ap, AF.Exp, r=[pst], w=[sil], scale=-1.0)
                self.act(sil.ap, sil.ap, AF.Ln, r=[sil], w=[sil], bias=1.0)
                self.tt("pool", sig.ap, sig.ap, sil.ap, ALU.add, r=[sig, sil], w=[sig])
                self.act(sig.ap, sig.ap, AF.Exp, r=[sig], w=[sig], scale=-1.0)
                self.tt("dve", mog_.ap, pst.ap, sig.ap, ALU.mult, r=[pst, sig], w=[mog_])
                yield
                pst = nps()
                for c in range(2):
                    self.mm(pst.ap, ckv["n"][:, c, j * P:(j + 1) * P], kvv[:, c, :], start=(c == 0), stop=(c == 1),
                            r=[ckv["n"], kvv], w=[pst])
                self.cp("act", V_.ap, pst.ap, r=[pst], w=[V_])
                yield
                pst = nps()
                p16 = Tile(pst[:, 0:16], "p16")
                p16.bs = pst.bs
                self.proj_tm(win, hT, j, O_IF, 16, p16)
                self.tt("dve", G_.ap, pst[:, 0:16], ifb.ap, ALU.add, r=[pst, ifb], w=[G_])
                yield
                self.gates(G_, spm_, wt_, nps())
                self.cp("pool", gbs_[:, 0:4], wt_[:, 4:8], r=[wt_], w=[gbs_])
                self.cp("pool", gbs_[:, 4:8], wt_[:, 12:16], r=[wt_], w=[gbs_])
                self.cp("pool", gbs_[:, 8:12], wt_[:, 20:24], r=[wt_], w=[gbs_])
                yield
                self.DMA(dr[f"mk_{tag}"][t:t + P, :], mk_.ap, r=[mk_])
                self.DMA(dr[f"mv_{tag}"][t:t + P, :].rearrange("p (h d) -> p h d", h=4), mv_[:, :, 0:128], r=[mv_])
                self.DMA(dr[f"mog_{tag}"][t:t + P, :], mog_.ap, r=[mog_])
                self.DMA(dr[f"V_{tag}"][t:t + P, :], V_.ap, r=[V_])
                self.DMA(dr[f"gb_{tag}"][t:t + P, :], gbs_.ap, r=[gbs_])

        def Sgen(bi, par):
            t0 = bi * TB
            bp = bi % 2
            for j in range(NT):
                t = t0 + j * P
                b4 = (bi * NT + j) % NB4
                b2 = (bi * NT + j) % 2
                mk_, mv_, wt_, hf_ = tmk[b4], tmv[b4], wt[b4], thf[b2]
                for h in (par, par + 2):
                    def post(nd, den, h=h, hf_=hf_, t=t):
                        self.ts("dve", hf_[:, h * P:(h + 1) * P], nd[:, 0:P], den[:, 1:2], r=[nd, den], w=[hf_.bs[h]])
                        self.DMA(dr[f"hf_{tag}"][t:t + P, h * P:(h + 1) * P], hf_[:, h * P:(h + 1) * P], r=[hf_.bs[h]])
                        yield
                    yield from self.mlstm_step(par, h, (wt_[:, h:h + 1], wt_), (wt_[:, 8 + h:9 + h], wt_),
                                               (wt_[:, 16 + h:17 + h], wt_),
                                               (fmq[bp][:, h, j * P:(j + 1) * P], fmq[bp].bs[h]),
                                               (fmk[bp][:, h, j * P:(j + 1) * P], fmk[bp].bs[h]),
                                               (mk_[:, h * P:(h + 1) * P], mk_),
                                               (mv_[:, h, 0:129], mv_), False, post)

        for bi in range(nb + 1):
            streams = []
            if bi >= 1:
                streams += [Sgen(bi - 1, 0), Sgen(bi - 1, 1)]
            if bi < nb:
                streams.append(Pgen(bi))
            interleave(streams)

    def gates(self, G_, spm_, wt_, gp):
        cf32 = self.c_f32
        self.act(spm_.ap, G_[:, 8:16], AF.Exp, r=[G_], w=[spm_], scale=-1.0)
        self.act(spm_.ap, spm_.ap, AF.Ln, r=[spm_], w=[spm_], bias=1.0)
        cbf = self.c_bf
        shl = self.shl
        self.cp("dve", shl[:, 0, :], spm_.ap, r=[spm_], w=[shl])
        self.tt("dve", shl[:, 1, :], spm_.ap, shl[:, 0, :], ALU.subtract, r=[spm_, shl], w=[shl])
        for k2 in range(2):
            self.mm(gp[:, 0:4], cbf[:, 1, :], shl[:, k2, 0:4], start=(k2 == 0), stop=(k2 == 1), r=[cbf, shl], w=[gp])
        for k2 in range(2):
            self.mm(gp[:, 4:8], cbf[:, 2, :], shl[:, k2, 4:8], start=(k2 == 0), stop=(k2 == 1), r=[cbf, shl], w=[gp])
        for k2 in range(2):
            self.mm(gp[:, 8:16], self.ones_bf.ap, shl[:, k2, 0:8], start=(k2 == 0), stop=(k2 == 1),
                    r=[self.ones_bf, shl], w=[gp])
        self.cp("dve", wt_[:, 8:24], gp[:, 0:16], r=[gp], w=[wt_])
        self.tt("dve", wt_[:, 0:8], wt_[:, 8:16], G_[:, 0:8], ALU.add, r=[wt_, G_], w=[wt_])
        self.act(wt_[:, 0:16], wt_[:, 0:16], AF.Exp, r=[wt_], w=[wt_])
        self.act(wt_[:, 16:24], wt_[:, 16:24], AF.Exp, r=[wt_], w=[wt_], scale=-1.0)

    def mla_pass(self, tag, Sq):
        a16, a32, dr = self.a16, self.a32, self.dr
        cf32 = self.c_f32
        NK = Sq // P
        QB = 512
        KT = [a16.alloc(f"KT{b}", (Sq,)) for b in range(2)]
        QT = [a16.alloc(f"QT{b}", (Sq,)) for b in range(2)]
        Vh = [a16.alloc(f"Vh{b}", (NK, 66)) for b in range(2)]
        PT = [a16.alloc(f"PT{b}", (1024,)) for b in range(3)]
        ATs = [a16.alloc(f"ATs{b}", (QB,)) for b in range(2)]
        rhl = a16.alloc("mrhl", (2, QB))
        rrow = a32.alloc("mrrow", (QB,))
        bcs = a32.alloc("mbcs", (QB,))
        sT = []
        for g in range(2):
            t = Tile(self.psall[:, g * 1024:(g + 1) * 1024], f"msT{g}")
            t.bs = [self.bankbuf[2 * g], self.bankbuf[2 * g + 1]]
            sT.append(t)
        oT = [self.ps(4, 0, QB, "oT0"), self.ps(5, 0, QB, "oT1")]
        bcp = self.ps(6, 0, QB, "bcp")
        for b in range(2):
            self.ms("pool", Vh[b][:, :, 64:66], 1.0, w=[Vh[b]])
        scale = 96.0 ** -0.5
        qcnt = 0
        gcnt = 0
        pcnt = 0
        ngr = NK // 2
        for h in range(8):
            kb = h % 2
            K_, Q_, V_ = KT[kb], QT[kb], Vh[kb]
            r0 = (h % 2) * 64
            self.DMA(K_[0:64, :], dr[f"Kn_{tag}"][h // 2, r0:r0 + 64, :], w=[K_])
            self.DMA(K_[64:96, :], dr[f"KR_{tag}"][:, :], w=[K_])
            self.DMA(Q_[0:96, :], dr[f"Q_{tag}"][h], w=[Q_])
            self.DMA(V_[:, :, 0:64], dr[f"V_{tag}"][:, h * 64:(h + 1) * 64].rearrange("(n p) c -> p n c", p=P), w=[V_])
            for qb in range(Sq // QB):
                o_ = oT[qcnt % 2]
                at_ = ATs[qcnt % 2]
                qcnt += 1
                qs = Q_[0:96, qb * QB:(qb + 1) * QB]

                def qk(kg, sg):
                    for n in range(2):
                        kt = 2 * kg + n
                        self.mm(sg[:, n * 512:(n + 1) * 512], K_[0:96, kt * P:(kt + 1) * P], qs, r=[K_, Q_], w=[sg])
                def pv(kg, pt):
                    for n in range(2):
                        kt = 2 * kg + n
                        self.mm(o_[0:65, :], V_[:, kt, 0:65], pt[:, n * 512:(n + 1) * 512], start=(kt == 0),
                                stop=(kt == NK - 1), r=[V_, pt], w=[o_])
                qk(0, sT[gcnt % 2])
                prev = None
                for kg in range(ngr):
                    sg = sT[gcnt % 2]
                    gcnt += 1
                    pt = PT[pcnt % 3]
                    pcnt += 1
                    if kg + 1 < ngr:
                        qk(kg + 1, sT[gcnt % 2])
                    self.act(pt.ap, sg.ap, AF.Exp, r=[sg], w=[pt], scale=scale)
                    if prev is not None:
                        pv(*prev)
                    prev = (kg, pt)
                pv(*prev)
                self.A("dve", lambda e, o=rrow[64:65, :], i_=o_[64:65, :]: e.reciprocal(out=o, in_=i_), r=[o_], w=[rrow])
                self.cp("dve", rhl[64:65, 0, :], rrow[64:65, :], r=[rrow], w=[rhl])
                self.tt("dve", rhl[64:65, 1, :], rrow[64:65, :], rhl[64:65, 0, :], ALU.subtract, r=[rrow, rhl], w=[rhl])
                for k2 in range(2):
                    self.mm(bcp[0:64, :], self.c_bf[64:65, 2, 0:64], rhl[64:65, k2, :], start=(k2 == 0), stop=(k2 == 1),
                            r=[self.c_bf, rhl], w=[bcp])
                self.cp("act", bcs[0:64, :], bcp[0:64, :], r=[bcp], w=[bcs])
                self.tt("dve", at_[0:64, :], o_[0:64, :], bcs[0:64, :], ALU.mult, r=[o_, bcs], w=[at_])
                self.DMA(dr[f"AT_{tag}"][h // 2, r0:r0 + 64, qb * QB:(qb + 1) * QB], at_[0:64, :], r=[at_])

    def odd_bwd(self, tag, Sq, wout, gml, first, last):
        a16, a32, dr, TB, NT = self.a16, self.a32, self.dr, self.TB, self.NT
        xsrc = dr[f"x_{tag}"] if first else dr[f"y_{tag}"]
        ydst = dr[f"y_{tag}"]
        self.mlstm_alloc()
        self.alloc_resid()
        fmq = [a16.alloc(f"bmq{b}", (4, TB), nsub=4) for b in range(2)]
        fmk = [a16.alloc(f"bmk{b}", (4, TB), nsub=4) for b in range(2)]
        fmg = [a16.alloc(f"bmg{b}", (4, TB)) for b in range(2)]
        fat = [a16.alloc(f"bat{b}", (4, TB)) for b in range(2)]
        tmk = [a16.alloc(f"tmk{b}", (512,)) for b in range(3)]
        tmv = [a16.alloc(f"tmv{b}", (4, 130)) for b in range(3)]
        tmog = [a16.alloc(f"tmog{b}", (512,)) for b in range(3)]
        tgb = [a32.alloc(f"tgb{b}", (16,)) for b in range(3)]
        thf = [a32.alloc(f"thf{b}", (512,)) for b in range(3)]
        hsum = [a32.alloc(f"hsum{b}", (P,)) for b in range(2)]
        hsq = [a16.alloc(f"hsq{b}", (P,)) for b in range(2)]
        hss = [a32.alloc(f"hss{b}", (1,)) for b in range(2)]
        otmp = [a32.alloc(f"otmp{b}", (P,)) for b in range(2)]
        mlo = [a16.alloc(f"mlo{b}", (512,), nsub=4) for b in range(2)]
        featT = [a16.alloc(f"featT{b}", (8, TB), nsub=8) for b in range(2)]
        psA, psB = self.ps(1, 0, 512, "psA"), self.ps(2, 0, 512, "psB")
        for b in range(3):
            self.ms("pool", tmv[b][:, :, 128:130], 1.0, w=[tmv[b]])
        nb = Sq // TB
        tiles = [(bi, j) for bi in range(nb - 1, -1, -1) for j in range(NT - 1, -1, -1)]

        def Lgen(bi):
            t0 = bi * TB
            bb = bi % 2
            q_, k_f, mg_, at_, ft = fmq[bb], fmk[bb], fmg[bb], fat[bb], featT[bb]
            for (dst, nm) in ((q_, "mqT"), (k_f, "mkT"), (mg_, "mgT"), (at_, "AT")):
                self.DMA(dst.ap, dr[f"{nm}_{tag}"][:, :, t0:t0 + TB].rearrange("h p t -> p h t"), w=[dst])
            yield
            self.tt("pool", ft[:, 0:4, :], at_.ap, mg_.ap, ALU.mult, r=[at_, mg_], w=ft.bs[0:4])
            yield

        def TLgen(n):
            bi, j = tiles[n]
            t = bi * TB + j * P
            tb = n % 3
            self.DMA(tmk[tb].ap, dr[f"mk_{tag}"][t:t + P, :], w=[tmk[tb]])
            self.DMA(tmv[tb][:, :, 0:128], dr[f"mv_{tag}"][t:t + P, :].rearrange("p (h d) -> p h d", h=4), w=[tmv[tb]])
            self.DMA(tmog[tb].ap, dr[f"mog_{tag}"][t:t + P, :], w=[tmog[tb]])
            self.DMA(tgb[tb].ap, dr[f"gb_{tag}"][t:t + P, :], w=[tgb[tb]])
            self.DMA(thf[tb].ap, dr[f"hf_{tag}"][t:t + P, :], w=[thf[tb]])
            yield

        def mkpost(par, h, nd, den, hf_, mog_, ml):
            hs_, ss_, ot_, hq_ = hsum[par], hss[par], otmp[par], hsq[par]
            self.stt(hs_.ap, nd[:, 0:P], den[:, 1:2], hf_[:, h * P:(h + 1) * P], ALU.mult, ALU.add,
                     r=[nd, den, hf_], w=[hs_])
            yield
            self.act(hq_.ap, hs_.ap, AF.Square, r=[hs_], w=[hq_, ss_], accum=ss_.ap)
            self.rsqrt_(ss_, P)
            yield
            self.stt(ot_.ap, hs_.ap, ss_[:, 0:1], gml.ap, ALU.mult, ALU.mult, r=[hs_, ss_, gml], w=[ot_])
            yield
            self.tt("pool", ml[:, h * P:(h + 1) * P], ot_.ap, mog_[:, h * P:(h + 1) * P], ALU.mult,
                    r=[ot_, mog_], w=[ml.bs[h]])
            yield

        def Schain(par):
            pending = None
            nd, den = self.mps[par]["nd"], self.mden[par]
            for n in range(len(tiles)):
                bi, j = tiles[n]
                bb = bi % 2
                tb = n % 3
                q_, k_f = fmq[bb], fmk[bb]
                mk_, mv_, mog_, gb_, hf_, ml = tmk[tb], tmv[tb], tmog[tb], tgb[tb], thf[tb], mlo[n % 2]
                yield ("tile", n)
                for h in (par, par + 2):
                    step = self.mlstm_step(par, h, (gb_[:, h:h + 1], gb_), (gb_[:, 4 + h:5 + h], gb_),
                                           (gb_[:, 8 + h:9 + h], gb_),
                                           (q_[:, h, j * P:(j + 1) * P], q_.bs[h]),
                                           (k_f[:, h, j * P:(j + 1) * P], k_f.bs[h]),
                                           (mk_[:, h * P:(h + 1) * P], mk_),
                                           (mv_[:, h, 0:129], mv_), True, None)
                    yield from zip2(step, pending)
                    pending = mkpost(par, h, nd, den, hf_, mog_, ml)
            if pending is not None:
                yield from pending

        def Rgen(n):
            bi, j = tiles[n]
            bb = bi % 2
            ft, ml = featT[bb], mlo[n % 2]
            t = bi * TB + j * P
            for c in range(4):
                self.tr(self.psT[:, c * P:(c + 1) * P], ml[:, c * P:(c + 1) * P], r=[ml], w=[self.psT])
            self.cp("act", ft[:, 4:8, j * P:(j + 1) * P],
                    self.psT[:, 0:512].rearrange("p (c t) -> p c t", c=4), r=[self.psT], w=ft.bs[4:8])
            yield
            yield from self.resid_out(ft, 8, wout, j, xsrc, ydst, t, last, psA, psB)

        def Lfor(n):
            if n == 0 or tiles[n][0] != tiles[n - 1][0]:
                return Lgen(tiles[n][0])
            return None

        interleave([Lfor(0), TLgen(0)])
        run_pipeline(len(tiles), Schain, TLgen, Lfor, Rgen)


def make_consts(smax):
    j = np.arange(P)[:, None]
    i = np.arange(P)[None, :]
    ident = (j == i)
    TL = (j <= i)
    TG = (j >= i)
    SG = (j > i)
    SL = (j < i)
    c_bf = np.stack([ident, TL, TG, SG, SL], axis=1).astype(np.float32).astype(ml_dtypes.bfloat16)
    c_f32 = np.stack([TL, TG, np.ones((P, P), bool)], axis=1).astype(np.float32)
    inv = (np.float32(10000.0) ** (-np.arange(0, 32, 2, dtype=np.float32) / np.float32(32))).astype(np.float32)
    ang = (np.arange(smax, dtype=np.float32)[:, None] * inv[None, :]).astype(np.float32)
    cos = np.cos(ang).astype(np.float32).T
    sin = np.sin(ang).astype(np.float32).T
    ropeT = np.stack([np.concatenate([cos, cos], 0), np.concatenate([sin, sin], 0)], 0)
    return c_bf, c_f32, np.ascontiguousarray(ropeT)


def make_invcnt(S):
    pos = np.arange(S)
    out = np.zeros((4, S), np.float32)
    for gi, w in enumerate((2, 4, 8, 16)):
        lo = np.clip(pos - w // 2, 0, S)
        hi = np.clip(pos + w // 2, 0, S)
        out[gi] = 1.0 / (hi - lo).astype(np.float32)
    return out


def prep_weights(w):
    f = lambda a: np.ascontiguousarray(np.asarray(a, dtype=np.float32))
    out = {}
    out["norm_gT"] = f(np.asarray(w["norm_g"]).reshape(-1, 8, P).transpose(0, 2, 1))
    out["final_g"] = f(w["final_norm_g"])
    out["e_w_in"] = f(w["e_w_in"])
    out["e_aup"] = f(np.concatenate([np.asarray(w["e_gla_a_up"]), np.asarray(w["e_gla_a_bias"])[:, :, None, :]], axis=2))
    out["e_gn"] = f(w["e_gla_norm_g"])
    out["e_pool_w"] = f(w["e_pool_w"])
    out["e_pscT"] = f(np.asarray(w["e_pool_scale"]).reshape(-1, 4, P).transpose(0, 2, 1))
    out["e_w_out"] = f(w["e_w_out"])
    out["o_w_in"] = f(w["o_w_in"])
    out["o_qgT"] = f(np.asarray(w["o_q_norm_g"]).reshape(-1, 3, P).transpose(0, 2, 1))
    out["o_q_up"] = f(w["o_q_up"])
    out["o_kvgT"] = f(np.asarray(w["o_kv_norm_g"]).reshape(-1, 2, P).transpose(0, 2, 1))
    out["o_kv_up"] = f(w["o_kv_up"])
    out["o_if_bias"] = f(w["o_if_bias"])
    out["o_mlg"] = f(w["o_mlstm_norm_g"])
    out["o_w_out"] = f(w["o_w_out"])
    return out


_CACHE = {}


def run_trunk(seq_inputs, weights, layers=(0, 1, 2, 3), final_norm=True, TB=256, n_cores=8):
    seqs = [(tag, a.shape[0]) for tag, a in seq_inputs[0].items()]
    key = (tuple(seqs), tuple(layers), final_norm, TB)
    if key not in _CACHE:
        b = Builder(seqs, layers, TB, final_norm)
        b.build()
        _CACHE[key] = b
    b = _CACHE[key]
    c_bf, c_f32, ropeT = make_consts(b.smax)
    wp = prep_weights(weights)
    in_maps = []
    for c in range(n_cores):
        m = dict(wp)
        m["c_bf"], m["c_f32"], m["ropeT"] = c_bf, c_f32, ropeT
        for tag, a in seq_inputs[c].items():
            m[f"x_{tag}"] = np.ascontiguousarray(a, dtype=np.float32)
            m[f"invcnt_{tag}"] = make_invcnt(a.shape[0])
        in_maps.append(m)
    res = run_bass_kernel_spmd(b.nc, in_maps, core_ids=list(range(n_cores)))
    return [{tag: r[f"y_{tag}"] for tag, _ in seqs} for r in res.results]


def kernel(x_prompt, x_sample, **weights):
    x_prompt = np.asarray(x_prompt, dtype=np.float32)
    x_sample = np.asarray(x_sample, dtype=np.float32)
    nb_p = x_prompt.shape[0]
    seq_inputs = []
    for c in range(8):
        seq_inputs.append({"s": x_sample[c], "p": x_prompt[c % nb_p]})
    outs = run_trunk(seq_inputs, weights)
    y_sample = np.stack([outs[c]["s"] for c in range(8)], axis=0)
    y_prompt = np.stack([outs[c]["p"] for c in range(nb_p)], axis=0)
    return (y_prompt, y_sample)
```
